# Optimizing a Trainium2 kernel written in Bass

```python
import jax, jax.numpy as jnp
from jax import lax
import numpy as np


D_MODEL = 1024
BATCH = 8
SEQ = 4096
DEPTH = 1
DEC_BATCH = 32
DEC_SEQ = 8
PAST_LEN = 16384
PAGE_SIZE = 128

FOX_HEADS = 8
FOX_HEAD_DIM = 64
FOX_WIDTH = FOX_HEADS * FOX_HEAD_DIM
MLSTM_HEADS = 4
MLSTM_HEAD_DIM = 128
MLSTM_WIDTH = MLSTM_HEADS * MLSTM_HEAD_DIM
MIX_WIDTH = FOX_WIDTH + MLSTM_WIDTH
CONV_WIDTH = 4
D_FF = 4 * D_MODEL
Q_BLOCK = 128
MLSTM_CHUNK = 64
LN_EPS = 1e-5
DN_ALPHA = (2 * DEPTH) ** 0.25
DN_BETA = (8 * DEPTH) ** -0.25
FOX_FORGET_BIAS = 2.0

OFF_FOX_Q = 0
OFF_FOX_K = FOX_WIDTH
OFF_FOX_V = 2 * FOX_WIDTH
OFF_FOX_F = 3 * FOX_WIDTH
OFF_M_QK = OFF_FOX_F + FOX_HEADS
OFF_M_V = OFF_M_QK + 2 * MLSTM_WIDTH
OFF_M_I = OFF_M_V + MLSTM_WIDTH
OFF_M_F = OFF_M_I + MLSTM_HEADS
OFF_M_O = OFF_M_F + MLSTM_HEADS
IN_COLS = OFF_M_O + MLSTM_WIDTH

kernel_name = 'fox_mlstm_parallel_heads_deepnorm_step'


def _layer_norm(x, g, b):
    xf = x.astype(jnp.float32)
    mu = jnp.mean(xf, -1, keepdims=True)
    var = jnp.mean(jnp.square(xf - mu), -1, keepdims=True)
    return ((xf - mu) * lax.rsqrt(var + LN_EPS) * g + b).astype(x.dtype)


def _mixer_inputs(x, conv_prev, w_in, b_fox_f, b_ig, b_fg, b_og, conv_w, conv_b):
    B, S, _ = x.shape
    z = jnp.einsum('bsd,dc->bsc', x, w_in)
    fq = z[..., OFF_FOX_Q:OFF_FOX_K].reshape(B, S, FOX_HEADS, FOX_HEAD_DIM)
    fk = z[..., OFF_FOX_K:OFF_FOX_V].reshape(B, S, FOX_HEADS, FOX_HEAD_DIM)
    fv = z[..., OFF_FOX_V:OFF_FOX_F].reshape(B, S, FOX_HEADS, FOX_HEAD_DIM)
    flogf = jax.nn.log_sigmoid((z[..., OFF_FOX_F:OFF_M_QK] + b_fox_f).astype(jnp.float32))
    qk_in = z[..., OFF_M_QK:OFF_M_V]
    xpad = jnp.concatenate([conv_prev.astype(qk_in.dtype), qk_in], axis=1)
    conv = conv_b + xpad[:, 0:S] * conv_w[0]
    for j in range(1, CONV_WIDTH):
        conv = conv + xpad[:, j:j + S] * conv_w[j]
    qk = jax.nn.silu(conv)
    mq = qk[..., :MLSTM_WIDTH].reshape(B, S, MLSTM_HEADS, MLSTM_HEAD_DIM)
    mk = qk[..., MLSTM_WIDTH:].reshape(B, S, MLSTM_HEADS, MLSTM_HEAD_DIM) * (MLSTM_HEAD_DIM ** -0.5)
    mv = z[..., OFF_M_V:OFF_M_I].reshape(B, S, MLSTM_HEADS, MLSTM_HEAD_DIM)
    ig = (z[..., OFF_M_I:OFF_M_F] + b_ig).astype(jnp.float32)
    lf = jax.nn.log_sigmoid((z[..., OFF_M_F:OFF_M_O] + b_fg).astype(jnp.float32))
    og = jax.nn.sigmoid(z[..., OFF_M_O:] + b_og)
    new_conv = xpad[:, S:]
    return fq, fk, fv, flogf, mq, mk, mv, ig, lf, og, new_conv


def _fox_prompt(q, k, v, logf):
    B, S, H, Dh = q.shape
    nb = S // Q_BLOCK
    L = jnp.cumsum(logf, axis=1).transpose(0, 2, 1)
    qb = q.reshape(B, nb, Q_BLOCK, H, Dh).swapaxes(0, 1)
    Lb = L.reshape(B, H, nb, Q_BLOCK).transpose(2, 0, 1, 3)
    k_pos = jnp.arange(S)
    scale = Dh ** -0.5

    def block(args):
        q_i, L_i, i = args
        s = jnp.einsum('bqhd,bkhd->bhqk', q_i, k).astype(jnp.float32) * scale
        s = s + L_i[..., :, None] - L[:, :, None, :]
        q_pos = i * Q_BLOCK + jnp.arange(Q_BLOCK)
        s = jnp.where(k_pos[None, :] <= q_pos[:, None], s, -jnp.inf)
        p = jax.nn.softmax(s, axis=-1)
        return jnp.einsum('bhqk,bkhd->bqhd', p.astype(v.dtype), v)

    o = lax.map(block, (qb, Lb, jnp.arange(nb)))
    return o.swapaxes(0, 1).reshape(B, S, H * Dh)


def _fox_sample(q, k_new, v_new, logf_new, k_past, v_past, logf_past):
    B, T, H, Dh = q.shape
    P = k_past.shape[1]
    scale = Dh ** -0.5
    Pc = jnp.cumsum(logf_past.astype(jnp.float32), axis=1)
    R = (Pc[:, -1:, :] - Pc).transpose(0, 2, 1)
    Ln = jnp.cumsum(logf_new, axis=1).transpose(0, 2, 1)
    s_past = jnp.einsum('bqhd,bkhd->bhqk', q, k_past).astype(jnp.float32) * scale
    s_past = s_past + Ln[..., :, None] + R[:, :, None, :]
    s_new = jnp.einsum('bqhd,bkhd->bhqk', q, k_new).astype(jnp.float32) * scale
    s_new = s_new + Ln[..., :, None] - Ln[..., None, :]
    causal = jnp.tril(jnp.ones((T, T), dtype=bool))
    s_new = jnp.where(causal, s_new, -jnp.inf)
    p = jax.nn.softmax(jnp.concatenate([s_past, s_new], axis=-1), axis=-1).astype(v_new.dtype)
    o = jnp.einsum('bhqk,bkhd->bqhd', p[..., :P], v_past) + jnp.einsum('bhqk,bkhd->bqhd', p[..., P:], v_new)
    return o.reshape(B, T, H * Dh)


def _mlstm_chunk(carry, inp):
    C, n, m = carry
    q, k, v, ig, lf = inp
    q = q.astype(jnp.float32)
    k = k.astype(jnp.float32)
    v = v.astype(jnp.float32)
    L = q.shape[1]
    b = jnp.cumsum(lf, axis=1).transpose(0, 2, 1)
    i_g = ig.transpose(0, 2, 1)
    causal = jnp.tril(jnp.ones((L, L), dtype=bool))
    Dm = jnp.where(causal, b[..., :, None] - b[..., None, :] + i_g[..., None, :], -jnp.inf)
    inter = m[..., None] + b
    m_t = jnp.maximum(inter, jnp.max(Dm, axis=-1))
    W = jnp.exp(Dm - m_t[..., None])
    a = jnp.exp(inter - m_t)
    Wqk = W * jnp.einsum('blhd,bshd->bhls', q, k)
    num = jnp.einsum('bhls,bshe->blhe', Wqk, v) + jnp.einsum('blhd,bhde->blhe', q, C) * a.transpose(0, 2, 1)[..., None]
    nq = jnp.sum(Wqk, axis=-1) + jnp.einsum('blhd,bhd->bhl', q, n) * a
    den = jnp.maximum(jnp.abs(nq), jnp.exp(-m_t)).transpose(0, 2, 1)[..., None]
    h = num / den
    m_end = m_t[..., -1]
    g_state = jnp.exp(inter[..., -1] - m_end)
    w_key = jnp.exp(b[..., -1:] - b + i_g - m_end[..., None])
    C_new = g_state[..., None, None] * C + jnp.einsum('bhs,bshd,bshe->bhde', w_key, k, v)
    n_new = g_state[..., None] * n + jnp.einsum('bhs,bshd->bhd', w_key, k)
    return (C_new, n_new, m_end), h


def _mlstm_prompt(q, k, v, ig, lf):
    B, S = q.shape[0], q.shape[1]
    nc = S // MLSTM_CHUNK

    def to_chunks(a):
        return a.reshape((B, nc, MLSTM_CHUNK) + a.shape[2:]).swapaxes(0, 1)

    init = (jnp.zeros((B, MLSTM_HEADS, MLSTM_HEAD_DIM, MLSTM_HEAD_DIM), jnp.float32),
            jnp.zeros((B, MLSTM_HEADS, MLSTM_HEAD_DIM), jnp.float32),
            jnp.zeros((B, MLSTM_HEADS), jnp.float32))
    (C, n, m), h = lax.scan(_mlstm_chunk, init, (to_chunks(q), to_chunks(k), to_chunks(v), to_chunks(ig), to_chunks(lf)))
    h = h.swapaxes(0, 1).reshape(B, S, MLSTM_HEADS, MLSTM_HEAD_DIM)
    return (C, n, m), h


def _block_out(x, fox_h, m_h, og, mlstm_norm_w, w_o, ln1_g, ln1_b, w1, w2, ln2_g, ln2_b):
    B, S, _ = x.shape
    mu = jnp.mean(m_h, -1, keepdims=True)
    var = jnp.mean(jnp.square(m_h - mu), -1, keepdims=True)
    mn = ((m_h - mu) * lax.rsqrt(var + LN_EPS)).reshape(B, S, MLSTM_WIDTH) * mlstm_norm_w * og
    mix = jnp.concatenate([fox_h, mn.astype(x.dtype)], axis=-1) @ w_o
    x1 = _layer_norm(DN_ALPHA * x + mix, ln1_g, ln1_b)
    hid = jnp.square(jax.nn.relu(x1 @ w1))
    return _layer_norm(DN_ALPHA * x1 + hid @ w2, ln2_g, ln2_b)


def setup_inputs(seed: int = 0) -> dict:
    key = jax.random.key(seed)
    ks = jax.random.split(key, 32)
    nrm = jax.random.normal
    n_pages = PAST_LEN // PAGE_SIZE
    n_used = DEC_BATCH * n_pages
    n_pool = n_used + n_used // 4
    x_prompt = nrm(ks[0], (BATCH, SEQ, D_MODEL), jnp.float32)
    x_sample = nrm(ks[1], (DEC_BATCH, DEC_SEQ, D_MODEL), jnp.float32)
    cache_k = nrm(ks[2], (DEPTH, n_pool, PAGE_SIZE, FOX_HEADS, FOX_HEAD_DIM), jnp.float32)
    cache_v = nrm(ks[3], (DEPTH, n_pool, PAGE_SIZE, FOX_HEADS, FOX_HEAD_DIM), jnp.float32)
    cache_logf = jax.nn.log_sigmoid(FOX_FORGET_BIAS + nrm(ks[4], (DEPTH, n_pool, PAGE_SIZE, FOX_HEADS), jnp.float32))
    state_C = 0.1 * nrm(ks[5], (DEPTH, DEC_BATCH, MLSTM_HEADS, MLSTM_HEAD_DIM, MLSTM_HEAD_DIM), jnp.float32)
    state_n = 0.1 * nrm(ks[6], (DEPTH, DEC_BATCH, MLSTM_HEADS, MLSTM_HEAD_DIM), jnp.float32)
    state_m = nrm(ks[7], (DEPTH, DEC_BATCH, MLSTM_HEADS), jnp.float32)
    state_conv = nrm(ks[8], (DEPTH, DEC_BATCH, CONV_WIDTH - 1, 2 * MLSTM_WIDTH), jnp.float32)
    page_table = jax.random.permutation(ks[9], n_pool)[:n_used].reshape(DEC_BATCH, n_pages).astype(jnp.int32)
    col_scale = jnp.ones((IN_COLS,), jnp.float32).at[OFF_FOX_V:OFF_FOX_F].set(DN_BETA).at[OFF_M_V:OFF_M_I].set(DN_BETA)
    w_in = nrm(ks[10], (DEPTH, D_MODEL, IN_COLS), jnp.float32) * (D_MODEL ** -0.5) * col_scale
    b_fox_f = FOX_FORGET_BIAS + 0.1 * nrm(ks[11], (DEPTH, FOX_HEADS), jnp.float32)
    b_ig = 0.1 * nrm(ks[12], (DEPTH, MLSTM_HEADS), jnp.float32)
    b_fg = jnp.linspace(3.0, 6.0, MLSTM_HEADS, dtype=jnp.float32) + 0.1 * nrm(ks[13], (DEPTH, MLSTM_HEADS), jnp.float32)
    b_og = 0.02 * nrm(ks[14], (DEPTH, MLSTM_WIDTH), jnp.float32)
    conv_w = nrm(ks[15], (DEPTH, CONV_WIDTH, 2 * MLSTM_WIDTH), jnp.float32) * (CONV_WIDTH ** -0.5)
    conv_b = 0.02 * nrm(ks[16], (DEPTH, 2 * MLSTM_WIDTH), jnp.float32)
    mlstm_norm_w = 1.0 + 0.02 * nrm(ks[17], (DEPTH, MLSTM_WIDTH), jnp.float32)
    w_o = nrm(ks[18], (DEPTH, MIX_WIDTH, D_MODEL), jnp.float32) * (MIX_WIDTH ** -0.5) * DN_BETA
    ln1_g = 1.0 + 0.02 * nrm(ks[19], (DEPTH, D_MODEL), jnp.float32)
    ln1_b = 0.02 * nrm(ks[20], (DEPTH, D_MODEL), jnp.float32)
    w1 = nrm(ks[21], (DEPTH, D_MODEL, D_FF), jnp.float32) * (D_MODEL ** -0.5)
    w2 = nrm(ks[22], (DEPTH, D_FF, D_MODEL), jnp.float32) * (D_FF ** -0.5) * DN_BETA
    ln2_g = 1.0 + 0.02 * nrm(ks[23], (DEPTH, D_MODEL), jnp.float32)
    ln2_b = 0.02 * nrm(ks[24], (DEPTH, D_MODEL), jnp.float32)
    return {'x_prompt': x_prompt, 'x_sample': x_sample, 'cache_k': cache_k, 'cache_v': cache_v,
            'cache_logf': cache_logf, 'state_C': state_C, 'state_n': state_n, 'state_m': state_m,
            'state_conv': state_conv, 'page_table': page_table, 'w_in': w_in, 'b_fox_f': b_fox_f,
            'b_ig': b_ig, 'b_fg': b_fg, 'b_og': b_og, 'conv_w': conv_w, 'conv_b': conv_b,
            'mlstm_norm_w': mlstm_norm_w, 'w_o': w_o, 'ln1_g': ln1_g, 'ln1_b': ln1_b, 'w1': w1, 'w2': w2,
            'ln2_g': ln2_g, 'ln2_b': ln2_b}


def reference(x_prompt, x_sample, cache_k, cache_v, cache_logf, state_C, state_n, state_m, state_conv,
              page_table, w_in, b_fox_f, b_ig, b_fg, b_og, conv_w, conv_b, mlstm_norm_w, w_o,
              ln1_g, ln1_b, w1, w2, ln2_g, ln2_b):
    xp, xs = x_prompt, x_sample
    B = xp.shape[0]
    DB = xs.shape[0]
    n_pages = page_table.shape[1]
    past = n_pages * PAGE_SIZE
    kp_l, vp_l, lfp_l, Cp_l, np_l, mp_l, cp_l = [], [], [], [], [], [], []
    ks_l, vs_l, lfs_l, Cs_l, ns_l, ms_l, cs_l = [], [], [], [], [], [], []
    for l in range(DEPTH):
        gates = (w_in[l], b_fox_f[l], b_ig[l], b_fg[l], b_og[l], conv_w[l], conv_b[l])
        outw = (mlstm_norm_w[l], w_o[l], ln1_g[l], ln1_b[l], w1[l], w2[l], ln2_g[l], ln2_b[l])
        conv0 = jnp.zeros((B, CONV_WIDTH - 1, 2 * MLSTM_WIDTH), xp.dtype)
        fq, fk, fv, flogf, mq, mk, mv, ig, lf, og, conv_new = _mixer_inputs(xp, conv0, *gates)
        fox_h = _fox_prompt(fq, fk, fv, flogf)
        (C_p, n_p, m_p), m_h = _mlstm_prompt(mq, mk, mv, ig, lf)
        xp = _block_out(xp, fox_h, m_h, og, *outw)
        kp_l.append(fk); vp_l.append(fv); lfp_l.append(flogf)
        Cp_l.append(C_p); np_l.append(n_p); mp_l.append(m_p); cp_l.append(conv_new)
        sq, sk, sv, slogf, smq, smk, smv, sig, slf, sog, sconv_new = _mixer_inputs(xs, state_conv[l], *gates)
        k_past = cache_k[l][page_table].reshape(DB, past, FOX_HEADS, FOX_HEAD_DIM)
        v_past = cache_v[l][page_table].reshape(DB, past, FOX_HEADS, FOX_HEAD_DIM)
        logf_past = cache_logf[l][page_table].reshape(DB, past, FOX_HEADS)
        sfox_h = _fox_sample(sq, sk, sv, slogf, k_past, v_past, logf_past)
        carry0 = (state_C[l].astype(jnp.float32), state_n[l].astype(jnp.float32), state_m[l].astype(jnp.float32))
        (C_s, n_s, m_s), sm_h = _mlstm_chunk(carry0, (smq, smk, smv, sig, slf))
        xs = _block_out(xs, sfox_h, sm_h, sog, *outw)
        ks_l.append(sk); vs_l.append(sv); lfs_l.append(slogf)
        Cs_l.append(C_s); ns_l.append(n_s); ms_l.append(m_s); cs_l.append(sconv_new)
    y_prompt = xp
    y_sample = xs
    k_prompt = jnp.stack(kp_l); v_prompt = jnp.stack(vp_l); logf_prompt = jnp.stack(lfp_l)
    C_prompt = jnp.stack(Cp_l); n_prompt = jnp.stack(np_l); m_prompt = jnp.stack(mp_l); conv_prompt = jnp.stack(cp_l)
    k_sample = jnp.stack(ks_l); v_sample = jnp.stack(vs_l); logf_sample = jnp.stack(lfs_l)
    C_sample = jnp.stack(Cs_l); n_sample = jnp.stack(ns_l); m_sample = jnp.stack(ms_l); conv_sample = jnp.stack(cs_l)
    return (y_prompt, y_sample, k_prompt, v_prompt, logf_prompt, C_prompt, n_prompt, m_prompt, conv_prompt,
            k_sample, v_sample, logf_sample, C_sample, n_sample, m_sample, conv_sample)
```

```python
import os
import numpy as np
DBG = os.environ.get('KDBG', '')
SKIP = os.environ.get('KSKIP', '')
from contextlib import ExitStack
import concourse.bass as bass
import concourse.mybir as mybir
from concourse.bass_utils import run_bass_kernel_spmd

F32 = mybir.dt.float32
BF16 = mybir.dt.bfloat16
I32 = mybir.dt.int32
AF = mybir.ActivationFunctionType
ALU = mybir.AluOpType
AX = mybir.AxisListType

D = 1024
S = 4096
TS = 512
NST = S // TS
KC = 8
FQ, FK, FV, FF, MQK, MV, MI, MF, MO, INC = 0, 512, 1024, 1536, 1544, 2568, 3080, 3084, 3088, 3600
DFF = 4096
ALPHA = 2.0 ** 0.25
EPS = 1e-5
NEG = -30000.0
NSQ = 4
TD = 8
NSTOK = NSQ * TD
NPG = 128
NTOT = S + NSTOK
KSCALE = 128.0 ** -0.5
QSCALE = 64.0 ** -0.5


class Buf:
    __slots__ = ("name", "w", "r", "dsem", "dcnt")

    def __init__(self, name):
        self.name = name
        self.w = None
        self.r = []
        self.dsem = None
        self.dcnt = 0


class Op:
    __slots__ = ("eng", "fn", "deps", "dma", "tok", "inc", "done")


class Prog:
    def __init__(self, nc, es):
        self.nc = nc
        self.es = es
        self.E = {"pe": nc.tensor, "act": nc.scalar, "dve": nc.vector, "pool": nc.gpsimd, "sp": nc.sync}
        self.ops = {k: [] for k in self.E}
        self.sem = {k: es.enter_context(nc.semaphore("sem_" + k)) for k in self.E}
        self.cnt = {k: 0 for k in self.E}
        self.dbufs = []
        self.nb = 0
        self.flip = 0
        self.grp = None

    def buf(self, name="b"):
        self.nb += 1
        return Buf(f"{name}_{self.nb}")

    def bufs(self, n, name="b"):
        return [self.buf(name) for _ in range(n)]

    def _add(self, o, r, w):
        deps = []

        def add(d, kind):
            if d is None or d.done:
                return
            if d.dma and o.dma and d.tok[0] is o.tok[0]:
                return
            if (not d.dma) and d.eng == o.eng and not o.dma:
                if o.eng == "pe":
                    return
            if d not in deps:
                deps.append(d)
                d.inc = True

        for b in r:
            add(b.w, "raw")
        for b in w:
            add(b.w, "waw")
            for x in b.r:
                add(x, "war")
        o.deps = deps
        o.done = False
        for b in r:
            b.r.append(o)
        for b in w:
            b.w = o
            b.r = []
        self.ops[o.eng].append(o)

    def op(self, eng, fn, r=(), w=()):
        o = Op()
        o.eng = eng
        o.dma = False
        o.inc = False
        o.fn = fn
        o.tok = None
        self._add(o, r, w)
        return o

    def dmaf(self, eng, mk, r, w, sb):
        o = Op()
        o.eng = eng
        o.dma = True
        o.inc = False
        if sb.dsem is None:
            sb.dsem = self.es.enter_context(self.nc.semaphore("d_" + sb.name))
            self.dbufs.append(sb)
        sb.dcnt += 16
        sem = sb.dsem
        o.tok = (sem, sb.dcnt)
        o.fn = lambda e: mk(e).then_inc(sem, 16)
        self._add(o, r, w)
        if self.grp is not None:
            self.grp.append((o, sb))
        return o

    def group_begin(self):
        self.grp = []

    def group_end(self):
        for (o, sb) in self.grp:
            o.tok = (sb.dsem, sb.dcnt)
        self.grp = None

    def dma(self, eng, out, in_, r=(), w=(), sb=None, **kw):
        return self.dmaf(eng, lambda e: e.dma_start(out=out, in_=in_, **kw), r, w, sb)

    def emit(self):
        lasts = []
        for k, lst in self.ops.items():
            for o in reversed(lst):
                if not o.dma and o.fn is not None:
                    o.inc = True
                    lasts.append(o)
                    break
        dtoks = []
        for b in self.dbufs:
            t = Op()
            t.dma = True
            t.tok = (b.dsem, b.dcnt)
            t.done = False
            dtoks.append(t)
        for k in self.E:
            o = Op()
            o.eng = k
            o.dma = False
            o.inc = False
            o.fn = None
            o.tok = None
            o.done = False
            o.deps = [x for x in lasts if x.eng != k] + dtoks
            self.ops[k].append(o)
        for k, lst in self.ops.items():
            for o in lst:
                if not o.dma and o.inc and o.fn is not None:
                    self.cnt[k] += 1
                    o.tok = (self.sem[k], self.cnt[k])
        prog = self
        with self.nc.Block() as block:
            def run(k):
                def f(e):
                    waited = {}
                    for o in prog.ops[k]:
                        for d in o.deps:
                            sem, val = d.tok
                            if waited.get(id(sem), 0) < val:
                                e.wait_ge(sem, val)
                                waited[id(sem)] = val
                        if o.fn is not None:
                            ins = o.fn(e)
                            if (not o.dma) and o.inc:
                                ins.then_inc(prog.sem[k], 1)
                return f
            block.tensor(run("pe"))
            block.scalar(run("act"))
            block.vector(run("dve"))
            block.gpsimd(run("pool"))
            block.sync(run("sp"))
        n = 0
        for k in self.ops:
            for o in self.ops[k]:
                o.done = True
                o.fn = None
                o.deps = None
            n += len(self.ops[k])
            self.ops[k] = []
        return n

    def mm(self, out, lhsT, rhs, start, stop, r, w):
        self.op("pe", lambda e: e.matmul(out, lhsT, rhs, start=start, stop=stop), r, w)

    def tr(self, out, in_, ident, r, w):
        self.op("pe", lambda e: e.transpose(out, in_, ident), r, w)

    def ev(self):
        self.flip ^= 1
        return "act" if self.flip else "dve"

    def cp(self, eng, out, in_, r, w):
        if eng == "act":
            self.op("act", lambda e: e.copy(out, in_), r, w)
        else:
            self.op(eng, lambda e: e.tensor_copy(out=out, in_=in_), r, w)

    def act(self, out, in_, func, r, w, bias=None, scale=None):
        kw = {}
        if bias is not None:
            kw["bias"] = bias
        if scale is not None:
            kw["scale"] = scale
        self.op("act", lambda e: e.activation(out, in_, func, **kw), r, w)

    def tt(self, eng, out, in0, in1, op, r, w):
        self.op(eng, lambda e: e.tensor_tensor(out=out, in0=in0, in1=in1, op=op), r, w)

    def ts(self, eng, out, in0, s1, s2, op0, op1, r, w):
        if s2 is None:
            self.op(eng, lambda e: e.tensor_scalar(out=out, in0=in0, scalar1=s1, scalar2=None, op0=op0), r, w)
        else:
            self.op(eng, lambda e: e.tensor_scalar(out=out, in0=in0, scalar1=s1, scalar2=s2, op0=op0, op1=op1), r, w)

    def stt(self, eng, out, in0, scalar, in1, op0, op1, r, w):
        self.op(eng, lambda e: e.scalar_tensor_tensor(out=out, in0=in0, scalar=scalar, in1=in1, op0=op0, op1=op1), r, w)

    def memset(self, eng, ap, val, w):
        self.op(eng, lambda e: e.memset(ap, val), (), w)


class PsumPool:
    def __init__(self, pg, tiles):
        self.t = tiles
        self.b = [pg.buf("ps") for _ in tiles]
        self.i = 0

    def next(self):
        i = self.i
        self.i = (i + 1) % len(self.t)
        return self.t[i], self.b[i]


class Rot:
    def __init__(self, pg, tiles, name="rot"):
        self.t = tiles
        self.b = [pg.buf(name) for _ in tiles]
        self.i = 0

    def next(self):
        i = self.i
        self.i = (i + 1) % len(self.t)
        return self.t[i], self.b[i]


def build_program(debug=False, with_cache=True):
    nc = bass.Bass("TRN2", target_bir_lowering=False)
    es = ExitStack()

    def din(name, shape, dt=F32):
        return nc.dram_tensor(name, list(shape), dt, kind="ExternalInput").ap()

    def dout(name, shape, dt=F32):
        return nc.dram_tensor(name, list(shape), dt, kind="ExternalOutput").ap()

    x_d = din("x", [S, D])
    xs_d = din("xs", [NSTOK, D])
    if with_cache:
        ckv_d = din("cache_kv", [5120 * 128, 1024])
        clf_d = din("cache_logf", [5120 * 128, 8])
    sC_d = din("state_C", [NSQ, 4, 128, 128])
    sn_d = din("state_n", [NSQ, 4, 128])
    sm_d = din("state_m", [NSQ, 4])
    sconv_d = din("state_conv", [NSQ, 3, 1024])
    pt_d = din("page_table", [NSQ, NPG], I32)
    win_d = din("w_in", [D, INC])
    bff_d = din("b_fox_f", [1, 8])
    big_d = din("b_ig", [1, 4])
    bfg_d = din("b_fg", [1, 4])
    bog_d = din("b_og", [1, 512])
    cw_d = din("conv_w", [4, 1024])
    cb_d = din("conv_b", [1, 1024])
    mnw_d = din("mlstm_norm_w", [1, 512])
    wo_d = din("w_o", [D, D])
    l1g_d = din("ln1_g", [1, D])
    l1b_d = din("ln1_b", [1, D])
    w1_d = din("w1", [D, DFF])
    w2_d = din("w2", [DFF, D])
    l2g_d = din("ln2_g", [1, D])
    l2b_d = din("ln2_b", [1, D])

    y_o = dout("o_y", [S, D])
    ys_o = dout("o_ys", [NSTOK, D])
    k_o = dout("o_k", [S, 512])
    v_o = dout("o_v", [S, 512])
    lf_o = dout("o_logf", [S, 8])
    C_o = dout("o_C", [4, 128, 128])
    n_o = dout("o_n", [4, 128])
    m_o = dout("o_m", [1, 4])
    conv_o = dout("o_conv", [3, 1024])
    ksm_o = dout("o_ks", [NSTOK, 512])
    vsm_o = dout("o_vs", [NSTOK, 512])
    lfs_o = dout("o_logfs", [NSTOK, 8])
    Cs_o = dout("o_Cs", [NSQ, 4, 128, 128])
    ns_o = dout("o_ns", [NSQ, 4, 128])
    ms_o = dout("o_ms", [NSQ, 4])
    convs_o = dout("o_convs", [NSQ, 3, 1024])

    mix_d = nc.dram_tensor("mix_scratch", [128, 8, NTOT], BF16, kind=("ExternalOutput" if debug else "Internal")).ap()
    mix_db = None

    pg = Prog(nc, es)
    mix_db = pg.buf("mixd")

    def sb(name, shape, dt=F32, stack=None):
        return (stack or es).enter_context(nc.sbuf_tensor(name, list(shape), dt))

    ident_f = sb("ident_f", [128, 128])
    ident_b = sb("ident_b", [128, 128], BF16)
    mask01 = sb("mask01", [128, 128])
    ones_f = sb("ones_f", [128, 128])
    cB = pg.buf("const")

    def mk_consts():
        pg.memset("pool", ident_f[:], 0.0, [cB])
        pg.op("pool", lambda e: e.affine_select(out=ident_f[:], in_=ident_f[:], pattern=[[-1, 128]],
                                                 compare_op=ALU.not_equal, fill=1.0, base=0, channel_multiplier=1),
              [cB], [cB])
        pg.cp("pool", ident_b[:], ident_f[:], [cB], [cB])
        pg.memset("pool", ones_f[:], 1.0, [cB])
        pg.memset("pool", mask01[:], 1.0, [cB])
        pg.op("pool", lambda e: e.affine_select(out=mask01[:], in_=mask01[:], pattern=[[1, 128]],
                                                 compare_op=ALU.is_ge, fill=0.0, base=0, channel_multiplier=-1),
              [cB], [cB])

    mk_consts()


    def load_xT(src_d, row0, T, col0, xs_rot, psT, xT, xTb):
        xt, xb = xs_rot.next()
        pg.dma("sp", xt[0:T, :], src_d[row0:row0 + T, :], [], [xb], sb=xb)
        for g in range(2):
            pt, pb = psT.next()
            for cc in range(4):
                c = g * 4 + cc
                pg.tr(pt[:, cc * 128:cc * 128 + T], xt[0:T, c * 128:(c + 1) * 128], ident_f[0:T, 0:T], [xb, cB], [pb])
            pg.cp(pg.ev(), xT[:, g * 4:g * 4 + 4, col0:col0 + T],
                  pt[:, :].rearrange("p (c t) -> p c t", c=4)[:, :, 0:T], [pb], [xTb])
        return xt, xb

    LSL = int(os.environ.get("LSL", "9"))

    def log_sigmoid(out, src, a, b, tb, r, w):
        if LSL >= 1:
            pg.stt("dve", a, src, -1.0, src, ALU.mult, ALU.max, r, [tb])
        if LSL >= 2:
            pg.act(a, a, AF.Exp, [tb], [tb], scale=-1.0)
        if LSL >= 3:
            pg.act(a, a, AF.Ln, [tb], [tb], bias=one_c[0:a.shape[0], 0:1])
        if LSL >= 4:
            pg.ts("dve", b, src, 0.0, None, ALU.min, None, r, [tb])
        if LSL >= 5:
            pg.tt("dve", out, b, a, ALU.subtract, [tb], w)

    one_c = sb("one_c", [128, 1])
    eps_c = sb("eps_c", [128, 1])
    pg.memset("pool", one_c[:], 1.0, [cB])
    pg.memset("pool", eps_c[:], EPS, [cB])

    esA = ExitStack()

    def sA(name, shape, dt=F32):
        return esA.enter_context(nc.sbuf_tensor(name, list(shape), dt))

    def pA(name, dt=F32, n=512):
        return esA.enter_context(nc.psum_tensor(name, [128, n], dt))

    winA = sA("winA", [128, KC, 1544], BF16)
    winA_b = pg.buf("winA")
    wv = win_d.rearrange("(c p) n -> p c n", p=128)
    for c in range(KC):
        pg.dma("pool", winA[:, c, :], wv[:, c, 0:1544], [], [winA_b], sb=winA_b)
    ka = [sA(f"ka{h}", [70, S], BF16) for h in range(8)]
    ka_b = [[pg.buf("ka") for j in range(NST)] for h in range(8)]
    qa = [sA(f"qa{h}", [70, TS], BF16) for h in range(8)]
    qa_b = [pg.buf("qa") for h in range(8)]
    Vst = sA("Vst", [128, 32, 8, 128], BF16)
    Vst_b = [pg.buf("V") for t in range(32)]
    maskb = sA("maskb", [128, 4, 512], BF16)
    xsA = Rot(pg, [sA(f"xsA{i}", [128, D]) for i in range(1)], "xsA")
    xT_t = [sA(f"xTA{i}", [128, KC, TS], BF16) for i in range(1)] * 2
    xT_bs = [pg.buf("xT")] * 2
    stg = Rot(pg, [sA(f"stgA{i}", [128, 512]) for i in range(2)], "stgA")
    stgf = Rot(pg, [sA(f"stgfA{i}", [128, 8]) for i in range(2)], "stgfA")
    tmpA = sA("tmpA", [128, 512])
    tmpB = sA("tmpB", [128, 512])
    maskf = tmpA
    tmp_b = pg.buf("tmpA")
    tmpC = tmpA[8:16, :] if False else sA("tmpC", [8, 512])
    Lt = sA("Lt", [8, TS])
    Lcar = sA("Lcar", [8, 1])
    LP = sA("LP", [8, 3, TS], BF16)
    LN = sA("LN", [8, 3, TS], BF16)
    Lres = sA("Lres", [8, TS])
    L_b = pg.buf("L")
    semQA = pg.buf("semQA")
    semKA = pg.buf("semKA")
    bff_t = sA("bff_t", [128, 8])
    bff_c = sA("bff_c", [8, 1])
    PT = Rot(pg, [sA(f"PT{i}", [128, TS], BF16) for i in range(2)], "PT")
    rl = sA("rl", [64, TS])
    rl_b = pg.buf("rl")
    foxT = Rot(pg, [sA(f"foxT{i}", [128, 4, TS], BF16) for i in range(1)], "foxT")
    ps_mm = PsumPool(pg, [pA(f"psA_mm{i}") for i in range(2)])
    ps_T = PsumPool(pg, [pA(f"psA_T{i}") for i in range(1)])
    ps_s = PsumPool(pg, [pA(f"psA_s{i}") for i in range(3)])
    ps_o = PsumPool(pg, [pA(f"psA_o{i}") for i in range(2)])

    pg.dma("sp", bff_t[:, :], bff_d[0:1, :].partition_broadcast(128), [], [cB], sb=cB)
    pg.dma("sp", bff_c[:, :], bff_d.rearrange("o h -> h o"), [], [cB], sb=cB, allow_slow_non_contiguous=True)
    for h in range(8):
        pg.memset("pool", ka[h][64:70, :], 1.0, [ka_b[h][j] for j in range(NST)])
        pg.memset("pool", qa[h][64:70, :], 1.0, [qa_b[h]])
    pg.memset("pool", Vst[:, :, :, 64:128], 1.0, Vst_b)
    pg.memset("pool", Lcar[:], 0.0, [L_b])
    for r_ in range(4):
        pg.memset("pool", maskf[:], 0.0, [tmp_b])
        pg.op("pool", lambda e, r_=r_: e.affine_select(out=maskf[:], in_=maskf[:], pattern=[[1, 512]],
                                                        compare_op=ALU.is_ge, fill=NEG, base=-128 * r_,
                                                        channel_multiplier=-1), [tmp_b], [tmp_b])
        pg.cp("pool", maskb[:, r_, :], maskf[:], [tmp_b], [cB])

    def vaug(t, h):
        return Vst[:, t, h, :]

    for j in range((NST if 'j1' not in DBG else (0 if 'noloop' in DBG else 1)) if 'A' not in SKIP else 0):
        xT = xT_t[j % 2]
        xTb = xT_bs[j % 2]
        for tt in range(4):
            load_xT(x_d, j * TS + tt * 128, 128, tt * 128, xsA, ps_T, xT, xTb)
        pt, pb = ps_mm.next()
        for c in range(KC):
            pg.mm(pt[0:8, :], winA[:, c, FF:FF + 8], xT[:, c, :], c == 0, c == KC - 1, [xTb, winA_b], [pb])
        pg.ts("dve", tmpB[0:8, :], pt[0:8, :], bff_c[0:8, 0:1], None, ALU.add, None, [pb, cB], [tmp_b])
        log_sigmoid(Lres[0:8, :], tmpB[0:8, :], tmpA[0:8, :], tmpC[0:8, :], tmp_b, [tmp_b], [L_b])
        pg.op("dve", lambda e: e.tensor_tensor_scan(out=Lt[0:8, :], data0=one_c[0:8, 0:1].broadcast_to([8, TS]),
                                                    data1=Lres[0:8, :], initial=Lcar[0:8, 0:1], op0=ALU.mult, op1=ALU.add),
              [L_b, cB], [L_b])
        pg.cp("dve", Lcar[0:8, 0:1], Lt[0:8, TS - 1:TS], [L_b], [L_b])
        pg.cp("dve", LP[0:8, 0, :], Lt[0:8, :], [L_b], [L_b])
        pg.tt("dve", Lres[0:8, :], Lt[0:8, :], LP[0:8, 0, :], ALU.subtract, [L_b], [L_b])
        pg.cp("dve", LP[0:8, 1, :], Lres[0:8, :], [L_b], [L_b])
        pg.tt("dve", Lres[0:8, :], Lres[0:8, :], LP[0:8, 1, :], ALU.subtract, [L_b], [L_b])
        pg.cp("dve", LP[0:8, 2, :], Lres[0:8, :], [L_b], [L_b])
        pg.ts("dve", LN[0:8, :, :], LP[0:8, :, :], -1.0, None, ALU.mult, None, [L_b], [L_b])
        pg.group_begin()
        for h in range(0 if 'noL' not in DBG else 99, 8):
            pg.dma("sp", qa[h][64:67, :], LP[h:h + 1, :, :], [L_b], [qa_b[h]], sb=semQA)
            pg.dma("sp", ka[h][67:70, j * TS:(j + 1) * TS], LN[h:h + 1, :, :], [L_b], [ka_b[h][j]], sb=semQA)
        pg.group_end()
        for tt in range(4 if 'notok' not in DBG else 0):
            t = j * 4 + tt
            for (c0, kind) in ((FK, "k"), (FV, "v")):
                pt, pb = ps_mm.next()
                for c in range(KC):
                    pg.mm(pt[:, :], xT[:, c, tt * 128:(tt + 1) * 128], winA[:, c, c0:c0 + 512], c == 0, c == KC - 1,
                          [xTb, winA_b], [pb])
                st, stb = stg.next()
                pg.cp("act", st[:, :], pt[:, :], [pb], [stb])
                od = k_o if kind == "k" else v_o
                if "nodma" not in DBG:
                    pg.dma("sp", od[t * 128:(t + 1) * 128, :], st[:, :], [stb], [], sb=stb)
                if kind == "v" and "nov" not in DBG:
                    if True:
                        pg.cp("dve", Vst[:, t, :, 0:64], st[:, :].rearrange("p (h d) -> p h d", h=8), [stb], [Vst_b[t]])
                    else:
                        pg.cp("act" if "vact" in DBG else "dve", Vst[:, t, :, 0:64], pt[:, :].rearrange("p (h d) -> p h d", h=8), [pb], [Vst_b[t]])
            if 'skipf' in DBG:
                continue
            pt, pb = ps_mm.next()
            for c in range(KC):
                pg.mm(pt[:, 0:8], xT[:, c, tt * 128:(tt + 1) * 128], winA[:, c, FF:FF + 8], c == 0, c == KC - 1,
                      [xTb, winA_b], [pb])
            sf, sfb = stgf.next()
            pg.tt("dve", tmpA[:, 0:8], pt[:, 0:8], bff_t[:, :], ALU.add, [pb, cB], [tmp_b])
            log_sigmoid(sf[:, :], tmpA[:, 0:8], tmpA[:, 8:16], tmpA[:, 16:24], tmp_b, [tmp_b], [sfb])
            if "nolfdma" not in DBG:
                pg.dma("sp", lf_o[t * 128:(t + 1) * 128, :], sf[:, :], [sfb], [], sb=sfb)
        if 'nofm' in DBG:
            continue
        for pp in range(4):
            pt, pb = ps_mm.next()
            for c in range(KC):
                pg.mm(pt[:, :], winA[:, c, FQ + pp * 128:FQ + (pp + 1) * 128], xT[:, c, :], c == 0, c == KC - 1,
                      [xTb, winA_b], [pb])
            pg.op("act", lambda e, pt=pt, pp=pp: e.mul(qa[2 * pp][0:64, :], pt[0:64, :], QSCALE), [pb], [qa_b[2 * pp]])
            pg.ts("dve", qa[2 * pp + 1][0:64, :], pt[64:128, :], QSCALE, None, ALU.mult, None, [pb], [qa_b[2 * pp + 1]])
            pt, pb = ps_mm.next()
            for c in range(KC):
                pg.mm(pt[:, :], winA[:, c, FK + pp * 128:FK + (pp + 1) * 128], xT[:, c, :], c == 0, c == KC - 1,
                      [xTb, winA_b], [pb])
            pg.cp("act", ka[2 * pp][0:64, j * TS:(j + 1) * TS], pt[0:64, :], [pb], [ka_b[2 * pp][j]])
            pg.cp("dve", ka[2 * pp + 1][0:64, j * TS:(j + 1) * TS], pt[64:128, :], [pb], [ka_b[2 * pp + 1][j]])
        fx, fxb = foxT.next()
        for h in range(0 if 'noattn' not in DBG else 99, 8):
            po, pob = ps_o.next()
            nk = 4 * j + 4
            pend = []

            def do_pv(i, ptile, ptb, po=po, pob=pob, h=h, nk=nk):
                pg.mm(po[:, :], vaug(i, h), ptile[:, :], i == 0, i == nk - 1, [Vst_b[i], ptb], [pob])

            for i in range(nk):
                r_ = i - 4 * j
                pss, psb = ps_s.next()
                pg.mm(pss[:, :], ka[h][0:70, i * 128:(i + 1) * 128], qa[h][0:70, :], True, r_ < 0,
                      [ka_b[h][i // 4], qa_b[h]], [psb])
                if r_ >= 0:
                    pg.mm(pss[:, :], ident_b[:, :], maskb[:, r_, :], False, True, [cB], [psb])
                ptile, ptb = PT.next()
                pg.act(ptile[:, :], pss[:, :], AF.Exp, [psb], [ptb])
                pend.append((i, ptile, ptb))
                if len(pend) > 1:
                    do_pv(*pend.pop(0))
            while pend:
                do_pv(*pend.pop(0))
            pg.op("dve", lambda e, po=po: e.reciprocal(rl[0:64, :], po[64:128, :]), [pob], [rl_b])
            pg.tt("dve", fx[(h % 2) * 64:(h % 2) * 64 + 64, h // 2, :], po[0:64, :], rl[0:64, :], ALU.mult,
                  [pob, rl_b], [fxb])
        pg.dma("sp", mix_d[:, 0:4, j * TS:(j + 1) * TS], fx[:, :, :], [fxb], [mix_db], sb=fxb)
    nA = pg.emit()
    esA.close()

    woS = sb("woS", [128, KC, D], BF16)
    w1a = sb("w1a", [128, KC, DFF // 2], BF16)
    wo_b, w1_b, w2_b = pg.buf("wo"), pg.buf("w1"), pg.buf("w2")
    wov = wo_d.rearrange("(c p) n -> p c n", p=128)
    w1v = w1_d.rearrange("(c p) n -> p c n", p=128)
    w2v = w2_d.rearrange("(c p) n -> p c n", p=128)
    esB = ExitStack()

    def sB(name, shape, dt=F32):
        return esB.enter_context(nc.sbuf_tensor(name, list(shape), dt))

    def pB(name, dt=F32, n=512):
        return esB.enter_context(nc.psum_tensor(name, [128, n], dt))

    NB = INC - MQK
    winB = sB("winB", [128, KC, NB], BF16)
    winB_b = pg.buf("winB")
    for c in range(KC):
        pg.dma("pool", winB[:, c, 0:1024], wv[:, c, MQK:MQK + 1024], [], [winB_b], sb=winB_b)
        pg.dma("pool", winB[:, c, 1024:NB], wv[:, c, MQK + 1024:INC], [], [winB_b], sb=winB_b)
    for c in range(KC):
        pg.dma("pool", woS[:, c, :], wov[:, c, :], [], [wo_b], sb=wo_b)
    for c in range(KC):
        pg.dma("pool", w1a[:, c, :], w1v[:, c, 0:2048], [], [w1_b], sb=w1_b)
    cQK, cV, cI, cF, cO = 0, MV - MQK, MI - MQK, MF - MQK, MO - MQK
    xsB = Rot(pg, [sB(f"xsB{i}", [128, D]) for i in range(1)], "xsB")
    xTB = sB("xTB", [128, KC, TS], BF16)
    xTB_b = pg.buf("xTB")
    zc = [sB(f"zc{b}", [128, 3 + TS]) for b in range(8)]
    zc_b = [pg.buf("zc") for b in range(8)]
    qkT = [sB(f"qkT{b}", [128, TS], BF16) for b in range(8)]
    qkT_b = [pg.buf("qkT") for b in range(8)]
    cacc = sB("cacc", [128, TS])
    csig = sB("csig", [128, TS])
    cacc_b = pg.buf("cacc")
    cwT = sB("cwT", [128, 8, 4])
    cbT = sB("cbT", [128, 8])
    big_c = sB("big_c", [4, 1])
    bfg_c = sB("bfg_c", [4, 1])
    bog_t = sB("bog_t", [128, 512])
    mnw_t = sB("mnw_t", [128, 512])
    gB = pg.buf("gates")
    semOB = pg.buf("semOB")
    g_ig = sB("g_ig", [4, TS])
    g_lf = sB("g_lf", [4, TS])
    g_B = sB("g_B", [4, TS])
    g_u = sB("g_u", [4, TS])
    g_M = sB("g_M", [4, TS])
    g_t1 = sB("g_t1", [4, TS])
    g_t2 = sB("g_t2", [4, TS])
    g_wk = sB("g_wk", [4, TS])
    g_fl = sB("g_fl", [4, TS])
    g_car = sB("g_car", [4, 4])
    g_Rc = sB("g_Rc", [4, 4])
    g_Rp = sB("g_Rp", [4, 4])
    g_g = sB("g_g", [4, 4])
    g_gd = sB("g_gd", [4, 16])
    g_m = sB("g_m", [4, 1])
    tsc = sB("tsc", [128, 4, 8])
    tsc_b = pg.buf("tsc")
    gb = sB("gb", [128, 16])
    gb_b = pg.buf("gb")
    vaugs = Rot(pg, [sB(f"vaug{i}", [128, 4, 130], BF16) for i in range(2)], "vaug")
    ogs = Rot(pg, [sB(f"og{i}", [128, 512]) for i in range(1)], "og")
    ogtmp = sB("ogtmp", [128, 512])
    ogtmp_b = pg.buf("ogtmp")
    ktoks = Rot(pg, [sB(f"ktok{i}", [128, 4, 128], BF16) for i in range(2)], "ktok")
    ATs = Rot(pg, [sB(f"AT{i}", [128, 128], BF16) for i in range(2)], "AT")
    qgs = Rot(pg, [sB(f"qg{i}", [128, 128], BF16) for i in range(2)], "qg")
    numS = Rot(pg, [sB(f"numS{i}", [128, 130]) for i in range(2)], "numS")
    Cf = [sB(f"Cf{h}", [128, 130]) for h in range(4)]
    Cb = [sB(f"Cb{h}", [128, 130], BF16) for h in range(4)]
    Cf_b = [pg.buf("Cf") for h in range(4)]
    Cb_b = [pg.buf("Cb") for h in range(4)]
    hbufs = Rot(pg, [sB(f"hbuf{i}", [128, 512]) for i in range(2)], "hbuf")
    hsm = sB("hsm", [128, 8])
    hsm_b = pg.buf("hsm")
    stats = sB("stats", [128, 4, 6])
    mvs = sB("mvs", [128, 4, 4])
    mnb = Rot(pg, [sB(f"mnb{i}", [128, 512], BF16) for i in range(2)], "mnb")
    mnf = sB("mnf", [128, 512])
    mnf_b = pg.buf("mnf")
    mnT = Rot(pg, [sB(f"mnT{i}", [128, 4, TS], BF16) for i in range(2)], "mnT")
    psB_mm = PsumPool(pg, [pB(f"psB_mm{i}") for i in range(2)])
    psB_T = PsumPool(pg, [pB(f"psB_T{i}") for i in range(2)])
    psB_g = PsumPool(pg, [pB("psB_g")])
    psB_b = PsumPool(pg, [pB("psB_b", BF16, 1024)])
    psB_s = PsumPool(pg, [pB("psB_s")])
    psB_u = PsumPool(pg, [pB("psB_u")])

    for jj in range(4):
        pg.dma("sp", cwT[:, :, jj], cw_d[jj:jj + 1, :].rearrange("o (b p) -> p (o b)", p=128), [], [cB], sb=cB,
               allow_slow_non_contiguous=True)
    pg.dma("sp", cbT[:, :], cb_d.rearrange("o (b p) -> p (o b)", p=128), [], [cB], sb=cB, allow_slow_non_contiguous=True)
    pg.dma("sp", big_c[:, :], big_d.rearrange("o h -> h o"), [], [cB], sb=cB, allow_slow_non_contiguous=True)
    pg.dma("sp", bfg_c[:, :], bfg_d.rearrange("o h -> h o"), [], [cB], sb=cB, allow_slow_non_contiguous=True)
    pg.dma("sp", bog_t[:, :], bog_d[0:1, :].partition_broadcast(128), [], [cB], sb=cB)
    pg.dma("sp", mnw_t[:, :], mnw_d[0:1, :].partition_broadcast(128), [], [cB], sb=cB)
    for b in range(8):
        pg.memset("pool", zc[b][:, 0:3], 0.0, [zc_b[b]])
    pg.memset("pool", g_car[:, :], 0.0, [gB])
    for hh in range(2):
        vt = vaugs.t[hh]
        pg.memset("pool", vt[:, :, :], 0.0, [vaugs.b[hh]])

    for j in range((NST if 'j1' not in DBG else 1) if 'B' not in SKIP else 0):
        for tt in range(4):
            load_xT(x_d, j * TS + tt * 128, 128, tt * 128, xsB, psB_T, xTB, xTB_b)
        pgt, pgb = psB_g.next()
        for c in range(KC):
            pg.mm(pgt[0:4, :], winB[:, c, cI:cI + 4], xTB[:, c, :], c == 0, c == KC - 1, [xTB_b, winB_b], [pgb])
        pg.ts("dve", g_ig[:, :], pgt[0:4, :], big_c[0:4, 0:1], None, ALU.add, None, [pgb, cB], [gB])
        pgt, pgb = psB_g.next()
        for c in range(KC):
            pg.mm(pgt[0:4, :], winB[:, c, cF:cF + 4], xTB[:, c, :], c == 0, c == KC - 1, [xTB_b, winB_b], [pgb])
        pg.ts("dve", g_t1[:, :], pgt[0:4, :], bfg_c[0:4, 0:1], None, ALU.add, None, [pgb, cB], [gB])
        log_sigmoid(g_lf[:, :], g_t1[:, :], g_t2[:, :], g_M[:, :], gB, [gB], [gB])
        pg.op("dve", lambda e: e.tensor_tensor_scan(out=g_B[:, :], data0=one_c[0:4, 0:1].broadcast_to([4, TS]), data1=g_lf[:, :],
                                                    initial=g_car[0:4, 0:1], op0=ALU.mult, op1=ALU.add), [gB, cB], [gB])
        pg.tt("dve", g_u[:, :], g_ig[:, :], g_B[:, :], ALU.subtract, [gB], [gB])
        pg.op("dve", lambda e: e.tensor_tensor_scan(out=g_M[:, :], data0=one_c[0:4, 0:1].broadcast_to([4, TS]), data1=g_u[:, :],
                                                    initial=g_car[0:4, 1:2], op0=ALU.mult, op1=ALU.max), [gB, cB], [gB])
        pg.cp("dve", g_Rp[:, 0:1], g_car[:, 1:2], [gB], [gB])
        pg.cp("dve", g_Rc[:, :], g_M[:, :].rearrange("p (c t) -> p c t", c=4)[:, :, 127], [gB], [gB])
        pg.cp("dve", g_Rp[:, 1:4], g_Rc[:, 0:3], [gB], [gB])
        pg.cp("dve", g_car[:, 0:1], g_B[:, TS - 1:TS], [gB], [gB])
        pg.cp("dve", g_car[:, 1:2], g_M[:, TS - 1:TS], [gB], [gB])
        Rb = g_Rc[:, :].unsqueeze(2).broadcast_to([4, 4, 128])
        pg.tt("dve", g_t1[:, :].rearrange("p (c t) -> p c t", c=4), g_u[:, :].rearrange("p (c t) -> p c t", c=4), Rb,
              ALU.subtract, [gB], [gB])
        pg.act(g_wk[:, :], g_t1[:, :], AF.Exp, [gB], [gB])
        pg.tt("dve", g_t2[:, :].rearrange("p (c t) -> p c t", c=4), g_B[:, :].rearrange("p (c t) -> p c t", c=4), Rb,
              ALU.add, [gB], [gB])
        pg.act(g_fl[:, :], g_t2[:, :], AF.Exp, [gB], [gB], scale=-1.0)
        pg.tt("dve", g_g[:, :], g_Rp[:, :], g_Rc[:, :], ALU.subtract, [gB], [gB])
        pg.act(g_g[:, :], g_g[:, :], AF.Exp, [gB], [gB])
        for h2 in range(4):
            pg.ts("dve", g_gd[:, h2 * 4:(h2 + 1) * 4], g_g[:, :], ident_f[0:4, h2:h2 + 1], None, ALU.mult, None, [gB, cB], [gB])
        pgt, pgb = psB_g.next()
        for cc in range(4):
            pg.tr(pgt[:, cc * 8:cc * 8 + 4], g_wk[0:4, cc * 128:(cc + 1) * 128], ident_f[0:4, 0:4], [gB, cB], [pgb])
            pg.tr(pgt[:, cc * 8 + 4:cc * 8 + 8], g_fl[0:4, cc * 128:(cc + 1) * 128], ident_f[0:4, 0:4], [gB, cB], [pgb])
        pg.mm(pgt[:, 64:80], ones_f[0:4, 0:128], g_gd[0:4, 0:16], True, True, [gB, cB], [pgb])
        pg.cp("dve", tsc[:, :, :], pgt[:, 0:32].rearrange("p (c k) -> p c k", c=4), [pgb], [tsc_b])
        pg.cp("dve", gb[:, :], pgt[:, 64:80], [pgb], [gb_b])
        for blk in range(8):
            pt, pb = psB_mm.next()
            for c in range(KC):
                pg.mm(pt[:, :], winB[:, c, cQK + blk * 128:cQK + (blk + 1) * 128], xTB[:, c, :], c == 0, c == KC - 1,
                      [xTB_b, winB_b], [pb])
            pg.cp("act", zc[blk][:, 3:3 + TS], pt[:, :], [pb], [zc_b[blk]])
            pg.ts("dve", cacc[:, :], zc[blk][:, 3:3 + TS], cwT[:, blk, 3:4], cbT[:, blk:blk + 1], ALU.mult, ALU.add,
                  [zc_b[blk], cB], [cacc_b])
            for jj in (2, 1, 0):
                pg.stt("dve", cacc[:, :], zc[blk][:, jj:jj + TS], cwT[:, blk, jj:jj + 1], cacc[:, :], ALU.mult, ALU.add,
                       [zc_b[blk], cB, cacc_b], [cacc_b])
            pg.act(csig[:, :], cacc[:, :], AF.Sigmoid, [cacc_b], [cacc_b])
            pg.stt("dve", qkT[blk][:, :], cacc[:, :], (1.0 if blk < 4 else KSCALE), csig[:, :], ALU.mult, ALU.mult,
                   [cacc_b], [qkT_b[blk]])
            pg.cp("act", zc[blk][:, 0:3], zc[blk][:, TS:TS + 3], [zc_b[blk]], [zc_b[blk]])
            if j == NST - 1:
                pg.dma("sp", conv_o[:, blk * 128:(blk + 1) * 128].rearrange("t f -> f t"), zc[blk][:, 0:3], [zc_b[blk]], [],
                       sb=semOB, allow_slow_non_contiguous=True)
        mt, mtb = mnT.next()
        for tt in range(4):
            t = j * 4 + tt
            sl = slice(tt * 128, (tt + 1) * 128)
            first = (t == 0)
            va, vab = vaugs.next()
            pt, pb = psB_mm.next()
            for c in range(KC):
                pg.mm(pt[:, :], xTB[:, c, sl], winB[:, c, cV:cV + 512], c == 0, c == KC - 1, [xTB_b, winB_b], [pb])
            for hm in range(4):
                pg.ts("dve", va[:, hm, 0:128], pt[:, hm * 128:(hm + 1) * 128], tsc[:, tt, hm:hm + 1], None, ALU.mult, None,
                      [pb, tsc_b], [vab])
            pg.cp("dve", va[:, :, 128:129], tsc[:, tt, 0:4].unsqueeze(2), [tsc_b], [vab])
            ogt, ogb = ogs.next()
            pt, pb = psB_mm.next()
            for c in range(KC):
                pg.mm(pt[:, :], xTB[:, c, sl], winB[:, c, cO:cO + 512], c == 0, c == KC - 1, [xTB_b, winB_b], [pb])
            pg.tt("dve", ogtmp[:, :], pt[:, :], bog_t[:, :], ALU.add, [pb, cB], [ogtmp_b])
            pg.act(ogt[:, :], ogtmp[:, :], AF.Sigmoid, [ogtmp_b], [ogb])
            kt, ktb = ktoks.next()
            pbt, pbb = psB_b.next()
            for hm in range(4):
                pg.tr(pbt[:, hm * 128:(hm + 1) * 128], qkT[4 + hm][:, sl], ident_b[:, :], [qkT_b[4 + hm], cB], [pbb])
            pg.cp("act", kt[:, :, :], pbt[:, 0:512].rearrange("p (h d) -> p h d", h=4), [pbb], [ktb])
            hb, hbb = hbufs.next()
            for hm in range(4):
                gcol = gb[:, hm * 4 + tt:hm * 4 + tt + 1]
                pss, psb = psB_s.next()
                pg.mm(pss[:, 0:128], qkT[4 + hm][:, sl], qkT[hm][:, sl], True, True, [qkT_b[4 + hm], qkT_b[hm]], [psb])
                at, atb = ATs.next()
                pg.tt("dve", at[:, :], pss[:, 0:128], mask01[:, :], ALU.mult, [psb, cB], [atb])
                pg.mm(pss[:, 256:385], at[:, :], va[:, hm, 0:129], True, first, [atb, vab], [psb])
                if not first:
                    qgt, qgb = qgs.next()
                    pg.ts("dve", qgt[:, :], qkT[hm][:, sl], gcol, None, ALU.mult, None, [qkT_b[hm], gb_b], [qgb])
                    pg.mm(pss[:, 256:385], qgt[:, :], Cb[hm][:, 0:129], False, True, [qgb, Cb_b[hm]], [psb])
                ns_, nsb = numS.next()
                pg.cp("act", ns_[:, 0:129], pss[:, 256:385], [psb], [nsb])
                psu, pub = psB_u.next()
                pg.mm(psu[:, 0:129], kt[:, hm, :], va[:, hm, 0:129], True, True, [ktb, vab], [pub])
                if first:
                    pg.cp("dve", Cf[hm][:, 0:129], psu[:, 0:129], [pub], [Cf_b[hm]])
                else:
                    pg.stt("dve", Cf[hm][:, 0:129], Cf[hm][:, 0:129], gcol, psu[:, 0:129], ALU.mult, ALU.add,
                           [pub, gb_b, Cf_b[hm]], [Cf_b[hm]])
                pg.cp("act", Cb[hm][:, 0:129], Cf[hm][:, 0:129], [Cf_b[hm]], [Cb_b[hm]])
                pg.stt("dve", hsm[:, 0:1], ns_[:, 128:129], -1.0, ns_[:, 128:129], ALU.mult, ALU.max, [nsb], [hsm_b])
                pg.tt("dve", hsm[:, 1:2], hsm[:, 0:1], tsc[:, tt, 4 + hm:5 + hm], ALU.max, [hsm_b, tsc_b], [hsm_b])
                pg.op("dve", lambda e: e.reciprocal(hsm[:, 2:3], hsm[:, 1:2]), [hsm_b], [hsm_b])
                pg.ts("dve", hb[:, hm * 128:(hm + 1) * 128], ns_[:, 0:128], hsm[:, 2:3], None, ALU.mult, None,
                      [nsb, hsm_b], [hbb])
            for hm in range(4):
                pg.op("dve", lambda e, hm=hm, hb=hb: e.bn_stats(stats[:, hm, :], hb[:, hm * 128:(hm + 1) * 128]), [hbb], [hsm_b])
                pg.op("dve", lambda e, hm=hm: e.bn_aggr(mvs[:, hm, 0:2], stats[:, hm, :]), [hsm_b], [hsm_b])
            pg.act(mvs[:, :, 2:3], mvs[:, :, 1:2], AF.Ln, [hsm_b], [hsm_b], bias=eps_c[:, 0:1])
            pg.act(mvs[:, :, 2:3], mvs[:, :, 2:3], AF.Exp, [hsm_b], [hsm_b], scale=-0.5)
            for hm in range(4):
                pg.ts("dve", mnf[:, hm * 128:(hm + 1) * 128], hb[:, hm * 128:(hm + 1) * 128], mvs[:, hm, 0:1], mvs[:, hm, 2:3],
                      ALU.subtract, ALU.mult, [hbb, hsm_b], [mnf_b])
            pg.tt("pool", mnf[:, :], mnf[:, :], mnw_t[:, :], ALU.mult, [mnf_b, cB], [mnf_b])
            mb, mbb = mnb.next()
            pg.tt("pool", mb[:, :], mnf[:, :], ogt[:, :], ALU.mult, [mnf_b, ogb], [mbb])
            pbt, pbb = psB_b.next()
            for hm in range(4):
                pg.tr(pbt[:, hm * 128:(hm + 1) * 128], mb[:, hm * 128:(hm + 1) * 128], ident_b[:, :], [mbb, cB], [pbb])
            pg.cp("act", mt[:, :, sl], pbt[:, 0:512].rearrange("p (h d) -> p h d", h=4), [pbb], [mtb])
        pg.dma("sp", mix_d[:, 4:8, j * TS:(j + 1) * TS], mt[:, :, :], [mtb], [mix_db], sb=mtb)
    for hm in range(4):
        pg.dma("sp", C_o[hm, :, :], Cf[hm][:, 0:128], [Cf_b[hm]], [], sb=semOB)
        pg.dma("sp", n_o[hm:hm + 1, :].rearrange("o d -> d o"), Cf[hm][:, 128:129], [Cf_b[hm]], [], sb=semOB,
               allow_slow_non_contiguous=True)
    pg.tt("dve", g_m[:, :], g_car[:, 0:1], g_car[:, 1:2], ALU.add, [gB], [gB])
    pg.dma("sp", m_o.rearrange("o h -> h o"), g_m[:, :], [gB], [], sb=semOB, allow_slow_non_contiguous=True)
    nB = pg.emit()
    esB.close()

    if 'nosample' not in DBG:
        esS = ExitStack()

        def sS(name, shape, dt=F32):
            return esS.enter_context(nc.sbuf_tensor(name, list(shape), dt))

        def pS(name, dt=F32, n=512):
            return esS.enter_context(nc.psum_tensor(name, [128, n], dt))

        NT = NSTOK
        winS = sS("winS", [128, KC, INC], BF16)
        winS_b = pg.buf("winS")
        for c in range(KC):
            for (a0, a1) in ((0, 1544), (1544, 2568), (2568, INC)):
                pg.dma("pool", winS[:, c, a0:a1], wv[:, c, a0:a1], [], [winS_b], sb=winS_b)
        semS = pg.buf("semS")
        semSo = pg.buf("semSo")
        xsS = Rot(pg, [sS("xsS0", [128, D])], "xsS")
        xsT = sS("xsT", [128, KC, NT], BF16)
        xsT_b = pg.buf("xsT")
        psS_mm = PsumPool(pg, [pS("psS_mm0")])
        psS_T = PsumPool(pg, [pS("psS_T0")])
        psS_g = psS_T
        psS_b = PsumPool(pg, [pS(f"psS_b{i}", BF16, 1024) for i in range(2)])
        psS_s = PsumPool(pg, [pS(f"psS_s{i}") for i in range(2)])
        psS_o = PsumPool(pg, [pS("psS_o")])
        psS_l = PsumPool(pg, [pS("psS_l")])
        sc = pg.buf("sconst")
        bffS = sS("bffS", [128, 8])
        bogS = sS("bogS", [128, 512])
        mnwS = sS("mnwS", [128, 512])
        cwS = sS("cwS", [128, 8, 4])
        cbS = sS("cbS", [128, 8])
        bigS = sS("bigS", [4, 1])
        bfgS = sS("bfgS", [4, 1])
        smS = sS("smS", [4, NSQ])
        maskU = sS("maskU", [128, 128])
        ones_b = sS("ones_b", [128, 1], BF16)
        pg.group_begin()
        pg.dma("sp", bffS[:, :], bff_d[0:1, :].partition_broadcast(128), [], [sc], sb=semS)
        pg.dma("sp", bogS[:, :], bog_d[0:1, :].partition_broadcast(128), [], [sc], sb=semS)
        pg.dma("sp", mnwS[:, :], mnw_d[0:1, :].partition_broadcast(128), [], [sc], sb=semS)
        for jj in range(4):
            pg.dma("sp", cwS[:, :, jj], cw_d[jj:jj + 1, :].rearrange("o (b p) -> p (o b)", p=128), [], [sc], sb=semS,
                   allow_slow_non_contiguous=True)
        pg.dma("sp", cbS[:, :], cb_d.rearrange("o (b p) -> p (o b)", p=128), [], [sc], sb=semS, allow_slow_non_contiguous=True)
        pg.dma("sp", bigS[:, :], big_d.rearrange("o h -> h o"), [], [sc], sb=semS, allow_slow_non_contiguous=True)
        pg.dma("sp", bfgS[:, :], bfg_d.rearrange("o h -> h o"), [], [sc], sb=semS, allow_slow_non_contiguous=True)
        pg.dma("sp", smS[:, :], sm_d.rearrange("b h -> h b"), [], [sc], sb=semS, allow_slow_non_contiguous=True)
        pg.group_end()
        pg.ts("pool", maskU[:, :], mask01[:, :], -1.0, 1.0, ALU.mult, ALU.add, [cB], [sc])
        pg.memset("pool", ones_b[:, :], 1.0, [sc])

        load_xT(xs_d, 0, NT, 0, xsS, psS_T, xsT, xsT_b)

        bdq = sS("bdq", [128, NSQ, 4, 16], BF16)
        bdq_b = pg.buf("bdq")
        pg.memset("pool", bdq[:, :, :, :], 0.0, [bdq_b])
        kTn = sS("kTn", [128, 4, NT], BF16)
        kTn_b = pg.buf("kTn")
        for pp in range(4):
            pt, pb = psS_mm.next()
            for c in range(KC):
                pg.mm(pt[:, 0:NT], winS[:, c, FQ + pp * 128:FQ + (pp + 1) * 128], xsT[:, c, :], c == 0, c == KC - 1,
                      [xsT_b, winS_b], [pb])
            pg.ts("dve", bdq[0:64, :, pp, 0:8], pt[0:64, 0:NT].rearrange("p (b t) -> p b t", b=NSQ), QSCALE, None, ALU.mult, None,
                  [pb], [bdq_b])
            pg.ts("dve", bdq[64:128, :, pp, 8:16], pt[64:128, 0:NT].rearrange("p (b t) -> p b t", b=NSQ), QSCALE, None, ALU.mult,
                  None, [pb], [bdq_b])
            pt, pb = psS_mm.next()
            for c in range(KC):
                pg.mm(pt[:, 0:NT], winS[:, c, FK + pp * 128:FK + (pp + 1) * 128], xsT[:, c, :], c == 0, c == KC - 1,
                      [xsT_b, winS_b], [pb])
            pg.cp("act", kTn[:, pp, :], pt[:, 0:NT], [pb], [kTn_b])
        gS = pg.buf("gS")
        q_ig = sS("q_ig", [4, NT]); q_lf = sS("q_lf", [4, NT]); q_B = sS("q_B", [4, NT]); q_u = sS("q_u", [4, NT])
        q_M = sS("q_M", [4, NT]); q_t1 = sS("q_t1", [4, NT]); q_t2 = sS("q_t2", [4, NT]); q_wk = sS("q_wk", [4, NT])
        q_fl = sS("q_fl", [4, NT]); q_R = sS("q_R", [4, NSQ]); q_g = sS("q_g", [4, NSQ]); q_gd = sS("q_gd", [4, 16])
        q_m = sS("q_m", [4, NSQ])
        pgt, pgb = psS_g.next()
        for c in range(KC):
            pg.mm(pgt[0:4, 0:NT], winS[:, c, MI:MI + 4], xsT[:, c, :], c == 0, c == KC - 1, [xsT_b, winS_b], [pgb])
        pg.ts("dve", q_ig[:, :], pgt[0:4, 0:NT], bigS[0:4, 0:1], None, ALU.add, None, [pgb, sc], [gS])
        pgt, pgb = psS_g.next()
        for c in range(KC):
            pg.mm(pgt[0:4, 0:NT], winS[:, c, MF:MF + 4], xsT[:, c, :], c == 0, c == KC - 1, [xsT_b, winS_b], [pgb])
        pg.ts("dve", q_t1[:, :], pgt[0:4, 0:NT], bfgS[0:4, 0:1], None, ALU.add, None, [pgb, sc], [gS])
        log_sigmoid(q_lf[:, :], q_t1[:, :], q_t2[:, :], q_M[:, :], gS, [gS], [gS])
        for b in range(NSQ):
            cs = slice(b * TD, (b + 1) * TD)
            pg.op("dve", lambda e, cs=cs: e.tensor_tensor_scan(out=q_B[:, cs], data0=one_c[0:4, 0:1].broadcast_to([4, TD]),
                                                                data1=q_lf[:, cs], initial=0.0, op0=ALU.mult, op1=ALU.add),
                  [gS, cB], [gS])
        pg.tt("dve", q_u[:, :], q_ig[:, :], q_B[:, :], ALU.subtract, [gS], [gS])
        for b in range(NSQ):
            cs = slice(b * TD, (b + 1) * TD)
            pg.op("dve", lambda e, cs=cs, b=b: e.tensor_tensor_scan(out=q_M[:, cs], data0=one_c[0:4, 0:1].broadcast_to([4, TD]),
                                                                     data1=q_u[:, cs], initial=smS[0:4, b:b + 1],
                                                                     op0=ALU.mult, op1=ALU.max), [gS, cB, sc], [gS])
        pg.cp("dve", q_R[:, :], q_M[:, :].rearrange("p (b t) -> p b t", b=NSQ)[:, :, TD - 1], [gS], [gS])
        RbS = q_R[:, :].unsqueeze(2).broadcast_to([4, NSQ, TD])
        pg.tt("dve", q_t1[:, :].rearrange("p (b t) -> p b t", b=NSQ), q_u[:, :].rearrange("p (b t) -> p b t", b=NSQ), RbS,
              ALU.subtract, [gS], [gS])
        pg.act(q_wk[:, :], q_t1[:, :], AF.Exp, [gS], [gS])
        pg.tt("dve", q_t2[:, :].rearrange("p (b t) -> p b t", b=NSQ), q_B[:, :].rearrange("p (b t) -> p b t", b=NSQ), RbS,
              ALU.add, [gS], [gS])
        pg.act(q_fl[:, :], q_t2[:, :], AF.Exp, [gS], [gS], scale=-1.0)
        pg.tt("dve", q_g[:, :], smS[:, :], q_R[:, :], ALU.subtract, [gS, sc], [gS])
        pg.act(q_g[:, :], q_g[:, :], AF.Exp, [gS], [gS])
        pg.tt("dve", q_m[:, :], q_R[:, :], q_B[:, :].rearrange("p (b t) -> p b t", b=NSQ)[:, :, TD - 1], ALU.add, [gS], [gS])
        pg.dma("sp", ms_o.rearrange("b h -> h b"), q_m[:, :], [gS], [], sb=semSo, allow_slow_non_contiguous=True)
        for h2 in range(4):
            pg.ts("dve", q_gd[:, h2 * 4:(h2 + 1) * 4], q_g[:, :], ident_f[0:4, h2:h2 + 1], None, ALU.mult, None, [gS, cB], [gS])
        tscS = sS("tscS", [8, NSQ, 8])
        gbS = sS("gbS", [128, 16])
        tscS_b = pg.buf("tscS")
        pgt, pgb = psS_g.next()
        for b in range(NSQ):
            pg.tr(pgt[0:TD, b * 8:b * 8 + 4], q_wk[0:4, b * TD:(b + 1) * TD], ident_f[0:4, 0:4], [gS, cB], [pgb])
            pg.tr(pgt[0:TD, b * 8 + 4:b * 8 + 8], q_fl[0:4, b * TD:(b + 1) * TD], ident_f[0:4, 0:4], [gS, cB], [pgb])
        pg.mm(pgt[:, 64:80], ones_f[0:4, 0:128], q_gd[0:4, 0:16], True, True, [gS, cB], [pgb])
        pg.cp("dve", tscS[:, :, :], pgt[0:TD, 0:32].rearrange("p (b k) -> p b k", b=NSQ), [pgb], [tscS_b])
        pg.cp("dve", gbS[:, :], pgt[:, 64:80], [pgb], [tscS_b])
        zcS = [sS(f"zcS{b}", [128, NSQ, 3 + TD]) for b in range(8)]
        zcS_b = [pg.buf("zcS") for b in range(8)]
        qkS = [sS(f"qkS{b}", [128, NT], BF16) for b in range(8)]
        qkS_b = [pg.buf("qkS") for b in range(8)]
        caccS = sS("caccS", [128, NSQ, TD]); csigS = sS("csigS", [128, NSQ, TD])
        caccS_b = pg.buf("caccS")
        semZ = pg.buf("semZ")
        pg.group_begin()
        for blk in range(8):
            for b in range(NSQ):
                pg.dma("sp", zcS[blk][:, b, 0:3], sconv_d[b, :, blk * 128:(blk + 1) * 128].rearrange("t f -> f t"), [], [zcS_b[blk]],
                       sb=semZ, allow_slow_non_contiguous=True)
        pg.group_end()
        for blk in range(8):
            pt, pb = psS_mm.next()
            for c in range(KC):
                pg.mm(pt[:, 0:NT], winS[:, c, MQK + blk * 128:MQK + (blk + 1) * 128], xsT[:, c, :], c == 0, c == KC - 1,
                      [xsT_b, winS_b], [pb])
            pg.cp("act", zcS[blk][:, :, 3:3 + TD], pt[:, 0:NT].rearrange("p (b t) -> p b t", b=NSQ), [pb], [zcS_b[blk]])
            pg.ts("dve", caccS[:, :, :], zcS[blk][:, :, 3:3 + TD], cwS[:, blk, 3:4], cbS[:, blk:blk + 1], ALU.mult, ALU.add,
                  [zcS_b[blk], sc], [caccS_b])
            for jj in (2, 1, 0):
                pg.stt("dve", caccS[:, :, :], zcS[blk][:, :, jj:jj + TD], cwS[:, blk, jj:jj + 1], caccS[:, :, :], ALU.mult, ALU.add,
                       [zcS_b[blk], sc, caccS_b], [caccS_b])
            pg.act(csigS[:, :, :], caccS[:, :, :], AF.Sigmoid, [caccS_b], [caccS_b])
            pg.stt("dve", qkS[blk][:, :].rearrange("p (b t) -> p b t", b=NSQ), caccS[:, :, :], (1.0 if blk < 4 else KSCALE),
                   csigS[:, :, :], ALU.mult, ALU.mult, [caccS_b], [qkS_b[blk]])
            for b in range(NSQ):
                pg.dma("sp", convs_o[b, :, blk * 128:(blk + 1) * 128].rearrange("t f -> f t"), zcS[blk][:, b, TD:TD + 3],
                       [zcS_b[blk]], [], sb=semSo, allow_slow_non_contiguous=True)

        knS = sS("knS", [8, NSQ, 512]); vnS = sS("vnS", [8, NSQ, 512]); vnB = sS("vnB", [8, NSQ, 512], BF16)
        lfnS = sS("lfnS", [8, NSQ, 8]); ltmp = sS("ltmp", [8, 32])
        vaS = sS("vaS", [8, NSQ, 4, 130], BF16); ogS = sS("ogS", [8, NSQ, 512]); ogtS = sS("ogtS", [8, 512])
        ktS = sS("ktS", [8, NSQ, 4, 128], BF16)
        tokS_b = [pg.buf("tokS") for b in range(NSQ)]
        ltmp_b = pg.buf("ltmp")
        pg.memset("pool", vaS[:, :, :, :], 0.0, tokS_b)
        lfn_bs = []
        for b in range(NSQ):
            cs = slice(b * TD, (b + 1) * TD)
            tb = tokS_b[b]
            for (c0, dst, od) in ((FK, knS, ksm_o), (FV, vnS, vsm_o)):
                pt, pb = psS_mm.next()
                for c in range(KC):
                    pg.mm(pt[0:TD, :], xsT[:, c, cs], winS[:, c, c0:c0 + 512], c == 0, c == KC - 1, [xsT_b, winS_b], [pb])
                ob_ = pg.buf("kvn")
                pg.cp("act", dst[:, b, :], pt[0:TD, :], [pb], [ob_])
                pg.dma("sp", od[b * TD:(b + 1) * TD, :], dst[:, b, :], [ob_], [], sb=semSo)
            pg.cp("dve", vnB[:, b, :], vnS[:, b, :], [ob_], [tb])
            pt, pb = psS_mm.next()
            for c in range(KC):
                pg.mm(pt[0:TD, 0:8], xsT[:, c, cs], winS[:, c, FF:FF + 8], c == 0, c == KC - 1, [xsT_b, winS_b], [pb])
            pg.tt("dve", ltmp[:, 0:8], pt[0:TD, 0:8], bffS[0:TD, :], ALU.add, [pb, sc], [ltmp_b])
            lfn_b = pg.buf("lfn")
            lfn_bs.append(lfn_b)
            log_sigmoid(lfnS[:, b, :], ltmp[:, 0:8], ltmp[:, 8:16], ltmp[:, 16:24], ltmp_b, [ltmp_b], [lfn_b])
            pg.dma("sp", lfs_o[b * TD:(b + 1) * TD, :], lfnS[:, b, :], [lfn_b], [], sb=semSo)
            pt, pb = psS_mm.next()
            for c in range(KC):
                pg.mm(pt[0:TD, :], xsT[:, c, cs], winS[:, c, MV:MV + 512], c == 0, c == KC - 1, [xsT_b, winS_b], [pb])
            for hm in range(4):
                pg.ts("dve", vaS[:, b, hm, 0:128], pt[0:TD, hm * 128:(hm + 1) * 128], tscS[:, b, hm:hm + 1], None, ALU.mult, None,
                      [pb, tscS_b], [tb])
            pg.cp("dve", vaS[:, b, :, 128:129], tscS[:, b, 0:4].unsqueeze(2), [tscS_b], [tb])
            pt, pb = psS_mm.next()
            for c in range(KC):
                pg.mm(pt[0:TD, :], xsT[:, c, cs], winS[:, c, MO:MO + 512], c == 0, c == KC - 1, [xsT_b, winS_b], [pb])
            pg.tt("dve", ogtS[:, :], pt[0:TD, :], bogS[0:TD, :], ALU.add, [pb, sc], [ltmp_b])
            pg.act(ogS[:, b, :], ogtS[:, :], AF.Sigmoid, [ltmp_b], [tb])
            pbt, pbb = psS_b.next()
            for hm in range(4):
                pg.tr(pbt[0:TD, hm * 128:(hm + 1) * 128], qkS[4 + hm][:, cs], ident_b[:, :], [qkS_b[4 + hm], cB], [pbb])
            pg.cp("act", ktS[:, b, :, :], pbt[0:TD, 0:512].rearrange("p (h d) -> p h d", h=4), [pbb], [tb])

        CfS = Rot(pg, [sS(f"CfS{i}", [128, 130]) for i in range(2)], "CfS")
        CbS = Rot(pg, [sS(f"CbS{i}", [128, 130], BF16) for i in range(2)], "CbS")
        CnS = Rot(pg, [sS(f"CnS{i}", [128, 130]) for i in range(2)], "CnS")
        ATS = Rot(pg, [sS(f"ATS{i}", [8, 8], BF16) for i in range(2)], "ATS")
        qgS = Rot(pg, [sS(f"qgS{i}", [128, 8], BF16) for i in range(2)], "qgS")
        nsS = Rot(pg, [sS(f"nsS{i}", [8, 130]) for i in range(2)], "nsS")
        hbS = Rot(pg, [sS(f"hbS{i}", [8, 512]) for i in range(2)], "hbS")
        hsmS = sS("hsmS", [8, 8]); statS = sS("statS", [8, 4, 6]); mvS = sS("mvS", [8, 4, 4]); mnfS = sS("mnfS", [8, 512])
        hsmS_b = pg.buf("hsmS"); mnfS_b = pg.buf("mnfS")
        mnbS = Rot(pg, [sS(f"mnbS{i}", [8, 512], BF16) for i in range(2)], "mnbS")
        mixS = sS("mixS", [128, 8, NT], BF16)
        mixS_b = pg.buf("mixS")
        for b in range(NSQ):
            cs = slice(b * TD, (b + 1) * TD)
            tb = tokS_b[b]
            hb, hbb = hbS.next()
            for hm in range(4):
                cf, cfb = CfS.next()
                pg.group_begin()
                pg.dma("sp", cf[:, 0:128], sC_d[b, hm, :, :], [], [cfb], sb=cfb)
                pg.dma("sp", cf[:, 128:129], sn_d[b, hm:hm + 1, :].rearrange("o d -> d o"), [], [cfb], sb=cfb,
                       allow_slow_non_contiguous=True)
                pg.group_end()
                cb_, cbb = CbS.next()
                pg.cp("act", cb_[:, 0:129], cf[:, 0:129], [cfb], [cbb])
                gcol = gbS[:, hm * 4 + b:hm * 4 + b + 1]
                pss, psb = psS_s.next()
                pg.mm(pss[0:TD, 0:TD], qkS[4 + hm][:, cs], qkS[hm][:, cs], True, True, [qkS_b[4 + hm], qkS_b[hm]], [psb])
                at, atb = ATS.next()
                pg.tt("dve", at[:, :], pss[0:TD, 0:TD], mask01[0:TD, 0:TD], ALU.mult, [psb, cB], [atb])
                qgt, qgb = qgS.next()
                pg.ts("dve", qgt[:, :], qkS[hm][:, cs], gcol, None, ALU.mult, None, [qkS_b[hm], tscS_b], [qgb])
                pg.mm(pss[0:TD, 256:385], at[:, :], vaS[:, b, hm, 0:129], True, False, [atb, tb], [psb])
                pg.mm(pss[0:TD, 256:385], qgt[:, :], cb_[:, 0:129], False, True, [qgb, cbb], [psb])
                ns_, nsb = nsS.next()
                pg.cp("act", ns_[:, 0:129], pss[0:TD, 256:385], [psb], [nsb])
                psu, pub = psS_s.next()
                pg.mm(psu[:, 0:129], ktS[:, b, hm, :], vaS[:, b, hm, 0:129], True, True, [tb], [pub])
                cn, cnb = CnS.next()
                pg.stt("dve", cn[:, 0:129], cf[:, 0:129], gcol, psu[:, 0:129], ALU.mult, ALU.add, [pub, tscS_b, cfb], [cnb])
                pg.group_begin()
                pg.dma("sp", Cs_o[b, hm, :, :], cn[:, 0:128], [cnb], [], sb=cnb)
                pg.dma("sp", ns_o[b, hm:hm + 1, :].rearrange("o d -> d o"), cn[:, 128:129], [cnb], [], sb=cnb,
                       allow_slow_non_contiguous=True)
                pg.group_end()
                pg.stt("dve", hsmS[:, 0:1], ns_[:, 128:129], -1.0, ns_[:, 128:129], ALU.mult, ALU.max, [nsb], [hsmS_b])
                pg.tt("dve", hsmS[:, 1:2], hsmS[:, 0:1], tscS[:, b, 4 + hm:5 + hm], ALU.max, [hsmS_b, tscS_b], [hsmS_b])
                pg.op("dve", lambda e: e.reciprocal(hsmS[:, 2:3], hsmS[:, 1:2]), [hsmS_b], [hsmS_b])
                pg.ts("dve", hb[:, hm * 128:(hm + 1) * 128], ns_[:, 0:128], hsmS[:, 2:3], None, ALU.mult, None, [nsb, hsmS_b], [hbb])
            for hm in range(4):
                pg.op("dve", lambda e, hm=hm, hb=hb: e.bn_stats(statS[:, hm, :], hb[:, hm * 128:(hm + 1) * 128]), [hbb], [hsmS_b])
                pg.op("dve", lambda e, hm=hm: e.bn_aggr(mvS[:, hm, 0:2], statS[:, hm, :]), [hsmS_b], [hsmS_b])
            pg.act(mvS[:, :, 2:3], mvS[:, :, 1:2], AF.Ln, [hsmS_b], [hsmS_b], bias=eps_c[0:TD, 0:1])
            pg.act(mvS[:, :, 2:3], mvS[:, :, 2:3], AF.Exp, [hsmS_b], [hsmS_b], scale=-0.5)
            for hm in range(4):
                pg.ts("dve", mnfS[:, hm * 128:(hm + 1) * 128], hb[:, hm * 128:(hm + 1) * 128], mvS[:, hm, 0:1], mvS[:, hm, 2:3],
                      ALU.subtract, ALU.mult, [hbb, hsmS_b], [mnfS_b])
            pg.tt("pool", mnfS[:, :], mnfS[:, :], mnwS[0:TD, :], ALU.mult, [mnfS_b, sc], [mnfS_b])
            mb, mbb = mnbS.next()
            pg.tt("pool", mb[:, :], mnfS[:, :], ogS[:, b, :], ALU.mult, [mnfS_b, tb], [mbb])
            pbt, pbb = psS_b.next()
            for hm in range(4):
                pg.tr(pbt[:, hm * 8:hm * 8 + TD], mb[:, hm * 128:(hm + 1) * 128], ident_b[0:TD, 0:TD], [mbb, cB], [pbb])
            pg.cp("act", mixS[:, 4:8, cs], pbt[:, 0:32].rearrange("p (h t) -> p h t", h=4), [pbb], [mixS_b])

        if with_cache:
            ptI = sS("ptI", [128, NPG], I32); ptF = sS("ptF", [128, NPG]); pcI = sS("pcI", [128, 1], I32); pcF = sS("pcF", [128, 1])
            idxF = sS("idxF", [128, NPG]); idxI = sS("idxI", [128, NPG], I32); pgI = sS("pgI", [128, 1], I32)
            idx_b = pg.buf("idx")
            lfp = sS("lfp", [128, 1024]); lfc = sS("lfc", [128, 8, 128]); ltot = sS("ltot", [128, 8]); llat = sS("llat", [128, 8])
            Rpg = sS("Rpg", [128, 8, 128]); RkT = sS("RkT", [128, 8, 128]); LnL = sS("LnL", [128, 8]); LnS = sS("LnS", [8, 8])
            bnew = sS("bnew", [8, 8])
            R_b = pg.buf("Rb")
            KVp = Rot(pg, [sS(f"KVp{i}", [128, 1024], BF16) for i in range(5)], "KVp")
            KTs = Rot(pg, [sS(f"KTs{i}", [128, 4, 128], BF16) for i in range(3)], "KTs")
            sTs = Rot(pg, [sS(f"sTs{i}", [128, 64]) for i in range(3)], "sTs")
            PTs = Rot(pg, [sS(f"PTs{i}", [128, 64], BF16) for i in range(3)], "PTs")
            oS = sS("oS", [8, 512]); lrow = sS("lrow", [1, 64]); lq = sS("lq", [8, 8]); rlq = sS("rlq", [8, 8])
            foS = sS("foS", [8, 512], BF16)
            fin_b = pg.buf("fin")
            pg.op("pool", lambda e: e.iota(pcI[:, :], pattern=[[0, 1]], base=0, channel_multiplier=1), [], [idx_b])
            pg.cp("dve", pcF[:, :], pcI[:, :], [idx_b], [idx_b])
            clfv = clf_d.rearrange("(pg t) h -> pg (t h)", t=128)
            for b in range(NSQ):
                cs = slice(b * TD, (b + 1) * TD)
                tb = tokS_b[b]
                pg.group_begin()
                pg.dma("sp", ptI[:, :], pt_d[b:b + 1, :].partition_broadcast(128), [], [idx_b], sb=idx_b)
                pg.dma("sp", pgI[:, :], pt_d[b:b + 1, :].rearrange("o p -> p o"), [], [idx_b], sb=idx_b, allow_slow_non_contiguous=True)
                pg.group_end()
                pg.cp("dve", ptF[:, :], ptI[:, :], [idx_b], [idx_b])
                pg.ts("dve", idxF[:, :], ptF[:, :], 128.0, pcF[:, 0:1], ALU.mult, ALU.add, [idx_b], [idx_b])
                pg.cp("dve", idxI[:, :], idxF[:, :], [idx_b], [idx_b])
                pg.dmaf("pool", lambda e: e.indirect_dma_start(out=lfp[:, :], out_offset=None, in_=clfv,
                                                              in_offset=bass.IndirectOffsetOnAxis(ap=pgI[:, 0:1], axis=0)),
                        [idx_b], [R_b], R_b)
                lf3 = lfp[:, :].rearrange("p (t h) -> p h t", h=8)
                for h in range(8):
                    pg.op("dve", lambda e, h=h: e.tensor_tensor_scan(out=lfc[:, h, :], data0=one_c[:, 0:1].broadcast_to([128, 128]),
                                                                      data1=lf3[:, h, :], initial=0.0, op0=ALU.mult, op1=ALU.add),
                          [R_b, cB], [R_b])
                pg.cp("dve", ltot[:, :], lfc[:, :, 127], [R_b], [R_b])
                pgt, pgb = psS_g.next()
                pg.mm(pgt[:, 0:8], maskU[:, :], ltot[:, :], True, True, [R_b, sc], [pgb])
                pg.mm(pgt[:, 8:16], ones_f[0:TD, 0:128], lfnS[:, b, :], True, True, [lfn_bs[b], cB], [pgb])
                pg.mm(pgt[0:TD, 16:24], mask01[0:TD, 0:TD], lfnS[:, b, :], True, True, [lfn_bs[b], cB], [pgb])
                pg.tt("dve", llat[:, :], pgt[:, 0:8], ltot[:, :], ALU.add, [pgb, R_b], [R_b])
                pg.cp("dve", LnL[:, :], pgt[:, 8:16], [pgb], [R_b])
                pg.cp("dve", LnS[:, :], pgt[0:TD, 16:24], [pgb], [R_b])
                pg.tt("dve", Rpg[:, :, :], llat[:, :].unsqueeze(2).broadcast_to([128, 8, 128]), lfc[:, :, :], ALU.subtract, [R_b], [R_b])
                for h in range(8):
                    ptt, ptb = psS_T.next()
                    pg.tr(ptt[:, 0:128], Rpg[:, h, :], ident_f[:, :], [R_b, cB], [ptb])
                    pg.ts("dve", RkT[:, h, :], ptt[:, 0:128], LnL[:, h:h + 1], None, ALU.add, None, [ptb, R_b], [R_b])
                pg.tt("dve", bnew[:, :], LnL[0:TD, :], LnS[:, :], ALU.subtract, [R_b], [R_b])
                po, pob = psS_o.next()
                pl, plb = psS_l.next()
                npg_ = NPG if 'p4' not in DBG else 4
                st1 = {}
                st2 = {}

                def stage1(p_, b=b):
                    kvp, kpb = KVp.next()
                    pg.dmaf("pool", lambda e, kvp=kvp, p_=p_: e.indirect_dma_start(
                        out=kvp[:, :], out_offset=None, in_=ckv_d, in_offset=bass.IndirectOffsetOnAxis(ap=idxI[:, p_:p_ + 1], axis=0)),
                        [idx_b], [kpb], kpb)
                    pbt, pbb = psS_b.next()
                    for pp in range(4):
                        pg.tr(pbt[:, pp * 128:(pp + 1) * 128], kvp[:, pp * 128:(pp + 1) * 128], ident_b[:, :], [kpb, cB], [pbb])
                    kt_, ktb_ = KTs.next()
                    pg.cp("dve", kt_[:, :, :], pbt[:, 0:512].rearrange("p (a t) -> p a t", a=4), [pbb], [ktb_])
                    st1[p_] = (kvp, kpb, kt_, ktb_)

                def stage2(p_, b=b):
                    kvp, kpb, kt_, ktb_ = st1.pop(p_)
                    pss, psb = psS_s.next()
                    for pp in range(4):
                        pg.mm(pss[:, pp * 16:(pp + 1) * 16], kt_[:, pp, :], bdq[:, b, pp, :], True, True, [ktb_, bdq_b], [psb])
                    st_, stb_ = sTs.next()
                    pg.tt("dve", st_[:, :].rearrange("p (h q) -> p h q", h=8), pss[:, 0:64].rearrange("p (h q) -> p h q", h=8),
                          RkT[:, :, p_].unsqueeze(2).broadcast_to([128, 8, 8]), ALU.add, [psb, R_b], [stb_])
                    pt_, ptb_ = PTs.next()
                    pg.act(pt_[:, :], st_[:, :], AF.Exp, [stb_], [ptb_])
                    st2[p_] = (kvp, kpb, pt_, ptb_)

                def stage3(p_, po=po, pob=pob, pl=pl, plb=plb):
                    kvp, kpb, pt_, ptb_ = st2.pop(p_)
                    for h in range(8):
                        pg.mm(po[0:TD, h * 64:(h + 1) * 64], pt_[:, h * 8:(h + 1) * 8], kvp[:, 512 + h * 64:512 + (h + 1) * 64],
                              p_ == 0, False, [ptb_, kpb], [pob])
                    pg.mm(pl[0:1, 0:64], ones_b[:, 0:1], pt_[:, :], p_ == 0, False, [ptb_, sc], [plb])

                for t_ in range(npg_ + 2):
                    if t_ < npg_:
                        stage1(t_)
                    if 0 <= t_ - 1 < npg_:
                        stage2(t_ - 1)
                    if 0 <= t_ - 2 < npg_:
                        stage3(t_ - 2)
                pss, psb = psS_s.next()
                for pp in range(4):
                    pg.mm(pss[0:TD, pp * 16:(pp + 1) * 16], kTn[:, pp, cs], bdq[:, b, pp, :], True, True, [kTn_b, bdq_b], [psb])
                st_, stb_ = sTs.next()
                pg.tt("dve", st_[0:TD, :].rearrange("p (h q) -> p h q", h=8), pss[0:TD, 0:64].rearrange("p (h q) -> p h q", h=8),
                      bnew[:, :].unsqueeze(2).broadcast_to([TD, 8, 8]), ALU.add, [psb, R_b], [stb_])
                pg.act(st_[0:TD, :], st_[0:TD, :], AF.Exp, [stb_], [stb_])
                pt_, ptb_ = PTs.next()
                pg.tt("dve", pt_[0:TD, :].rearrange("p (h q) -> p h q", h=8), st_[0:TD, :].rearrange("p (h q) -> p h q", h=8),
                      mask01[0:TD, 0:TD].unsqueeze(1).broadcast_to([TD, 8, TD]), ALU.mult, [stb_, cB], [ptb_])
                for h in range(8):
                    pg.mm(po[0:TD, h * 64:(h + 1) * 64], pt_[0:TD, h * 8:(h + 1) * 8], vnB[:, b, h * 64:(h + 1) * 64], False, True,
                          [ptb_, tb], [pob])
                pg.mm(pl[0:1, 0:64], ones_b[0:TD, 0:1], pt_[0:TD, :], False, True, [ptb_, sc], [plb])
                pg.cp("act", oS[:, :], po[0:TD, :], [pob], [fin_b])
                pg.cp("dve", lrow[:, :], pl[0:1, 0:64], [plb], [fin_b])
                lr_h = lrow.tensor if hasattr(lrow, "tensor") else lrow
                pg.group_begin()
                for q_ in range(TD):
                    pg.dma("sp", lq[q_:q_ + 1, :], bass.AP(lr_h, lrow[:, :].offset + q_, [[lrow[:, :].ap[0][0], 1], [8, 8]]),
                           [fin_b], [fin_b], sb=fin_b, allow_slow_non_contiguous=True)
                pg.group_end()
                pg.op("dve", lambda e: e.reciprocal(rlq[:, :], lq[:, :]), [fin_b], [fin_b])
                pg.tt("dve", foS[:, :].rearrange("p (h d) -> p h d", h=8), oS[:, :].rearrange("p (h d) -> p h d", h=8),
                      rlq[:, :].unsqueeze(2).broadcast_to([TD, 8, 64]), ALU.mult, [fin_b], [fin_b])
                pbt, pbb = psS_b.next()
                for c4 in range(4):
                    pg.tr(pbt[:, c4 * 8:c4 * 8 + TD], foS[:, c4 * 128:(c4 + 1) * 128], ident_b[0:TD, 0:TD], [fin_b, cB], [pbb])
                pg.cp("act", mixS[:, 0:4, cs], pbt[:, 0:32].rearrange("p (h t) -> p h t", h=4), [pbb], [mixS_b])
        pg.dma("sp", mix_d[:, :, S:S + NT], mixS[:, :, :], [mixS_b], [mix_db], sb=mixS_b)
        nS = pg.emit()
        esS.close()

    esC = ExitStack()

    def sC(name, shape, dt=F32):
        return esC.enter_context(nc.sbuf_tensor(name, list(shape), dt))

    def pC(name, dt=F32, n=512):
        return esC.enter_context(nc.psum_tensor(name, [128, n], dt))

    TB = 256
    w1b = sC("w1b", [128, KC, DFF // 2], BF16)
    w1b_b = pg.buf("w1b")
    for c in range(KC):
        pg.dma("pool", w1b[:, c, :], w1v[:, c, 2048:4096], [], [w1b_b], sb=w1b_b)
    w2S = sC("w2S", [128, 32, D], BF16)
    for c in range(32):
        pg.dma("pool", w2S[:, c, :], w2v[:, c, :], [], [w2_b], sb=w2_b)
    lnp = sC("lnp", [128, 4, D])
    for ii, dd in enumerate((l1g_d, l1b_d, l2g_d, l2b_d)):
        pg.dma("sp", lnp[:, ii, :], dd[0:1, :].partition_broadcast(128), [], [cB], sb=cB)
    mixC = sC("mixC", [128, KC, TB], BF16)
    mixC_b = pg.buf("mixC")
    xr = Rot(pg, [sC("xr0", [128, D])], "xr")
    x1s = [sC(f"x1s{i}", [128, D]) for i in range(2)]
    x1s_b = [pg.buf("x1s") for i in range(2)]
    x1T = sC("x1T", [128, KC, TB], BF16)
    x1T_b = pg.buf("x1T")
    hidT = sC("hidT", [128, 32, TB], BF16)
    hidT_b = [pg.buf("hidT") for f in range(32)]
    rtmp = Rot(pg, [sC(f"rtmp{i}", [128, TB]) for i in range(2)], "rtmp")
    lstat = sC("lstat", [128, 2, 6])
    lmv = sC("lmv", [128, 4])
    lst_b = pg.buf("lstat")
    psC_mm = PsumPool(pg, [pC(f"psC_mm{i}") for i in range(4)])
    psC_T = PsumPool(pg, [pC(f"psC_T{i}") for i in range(2)])

    def ln_inplace(xa, T, gi, xb):
        for h2 in range(2):
            pg.op("dve", lambda e, h2=h2: e.bn_stats(lstat[0:T, h2, :], xa[:, h2 * 512:(h2 + 1) * 512]), [xb], [lst_b])
        pg.op("dve", lambda e: e.bn_aggr(lmv[0:T, 0:2], lstat[0:T, :, :]), [lst_b], [lst_b])
        pg.act(lmv[0:T, 2:3], lmv[0:T, 1:2], AF.Ln, [lst_b], [lst_b], bias=eps_c[0:T, 0:1])
        pg.act(lmv[0:T, 2:3], lmv[0:T, 2:3], AF.Exp, [lst_b], [lst_b], scale=-0.5)
        pg.ts("dve", xa, xa, lmv[0:T, 0:1], lmv[0:T, 2:3], ALU.subtract, ALU.mult, [xb, lst_b], [xb])
        pg.tt("pool", xa, xa, lnp[0:T, gi, :], ALU.mult, [xb, cB], [xb])
        pg.tt("pool", xa, xa, lnp[0:T, gi + 1, :], ALU.add, [xb, cB], [xb])

    blocks = []
    nblk = (S // TB) if 'j1' not in DBG else 2
    if 'C' in SKIP:
        nblk = 0
    for bi in range(nblk):
        blocks.append((x_d, y_o, bi * TB, bi * TB, 2, 128))
    if 'nosample' not in DBG:
        blocks.append((xs_d, ys_o, 0, S, 1, NSTOK))
    for (src_d, out_d, row0, col0, ntl, T) in blocks:
        W = ntl * T if T == 128 else T
        pg.dma("sp", mixC[:, :, 0:W], mix_d[:, :, col0:col0 + W], [mix_db], [mixC_b], sb=mixC_b)
        for tl in range(ntl):
            sl = slice(tl * T, (tl + 1) * T)
            xt, xb = xr.next()
            pg.dma("sp", xt[0:T, :], src_d[row0 + tl * T:row0 + (tl + 1) * T, :], [], [xb], sb=xb)
            xa = x1s[tl][0:T, :]
            for n2 in range(2):
                pt, pb = psC_mm.next()
                for c in range(KC):
                    pg.mm(pt[0:T, :], mixC[:, c, sl], woS[:, c, n2 * 512:(n2 + 1) * 512], c == 0, c == KC - 1,
                          [mixC_b, wo_b], [pb])
                pg.stt("dve", xa[:, n2 * 512:(n2 + 1) * 512], xt[0:T, n2 * 512:(n2 + 1) * 512], ALPHA, pt[0:T, :],
                       ALU.mult, ALU.add, [xb, pb], [x1s_b[tl]])
            ln_inplace(xa, T, 0, x1s_b[tl])
            for g in range(2):
                ptt, ptb = psC_T.next()
                for cc in range(4):
                    c = g * 4 + cc
                    pg.tr(ptt[:, cc * 128:cc * 128 + T], xa[:, c * 128:(c + 1) * 128], ident_f[0:T, 0:T], [x1s_b[tl], cB], [ptb])
                pg.cp(pg.ev(), x1T[:, g * 4:g * 4 + 4, tl * T:(tl + 1) * T],
                      ptt[:, :].rearrange("p (c t) -> p c t", c=4)[:, :, 0:T], [ptb], [x1T_b])
        for f in range(32):
            pt, pb = psC_mm.next()
            for c in range(KC):
                w1t = w1a if f < 16 else w1b
                fo = (f % 16) * 128
                pg.mm(pt[:, 0:W], w1t[:, c, fo:fo + 128], x1T[:, c, 0:W], c == 0, c == KC - 1,
                      [x1T_b, w1_b if f < 16 else w1b_b], [pb])
            rt, rtb = rtmp.next()
            pg.act(rt[:, 0:W], pt[:, 0:W], AF.Relu, [pb], [rtb])
            pg.tt("dve" if f % 2 == 0 else "pool", hidT[:, f, 0:W], rt[:, 0:W], rt[:, 0:W], ALU.mult, [rtb], [hidT_b[f]])
        for tl in range(ntl):
            sl = slice(tl * T, (tl + 1) * T)
            xa = x1s[tl][0:T, :]
            for n2 in range(2):
                pt, pb = psC_mm.next()
                for f in range(32):
                    pg.mm(pt[0:T, :], hidT[:, f, sl], w2S[:, f, n2 * 512:(n2 + 1) * 512], f == 0, f == 31,
                          [hidT_b[f], w2_b], [pb])
                pg.stt("dve", xa[:, n2 * 512:(n2 + 1) * 512], xa[:, n2 * 512:(n2 + 1) * 512], ALPHA, pt[0:T, :],
                       ALU.mult, ALU.add, [x1s_b[tl], pb], [x1s_b[tl]])
            ln_inplace(xa, T, 2, x1s_b[tl])
            pg.dma("sp", out_d[row0 + tl * T:row0 + (tl + 1) * T, :], xa, [x1s_b[tl]], [], sb=x1s_b[tl])
    nC = pg.emit()
    esC.close()
    return nc, es, pg, dict(nA=nA, nB=nB, nC=nC)


_CACHE = {}


def kernel(**inputs):
    n = 8
    nc, es, pg, info = build_program(debug=False, with_cache=True)
    I = {k: np.asarray(v) for k, v in inputs.items()}
    in_maps = []
    ckv = np.concatenate([I["cache_k"].reshape(5120 * 128, 512), I["cache_v"].reshape(5120 * 128, 512)], axis=1)
    clf = np.ascontiguousarray(I["cache_logf"]).reshape(5120 * 128, 8)
    for c in range(n):
        sl = slice(c * NSQ, (c + 1) * NSQ)
        in_maps.append({
            "x": np.ascontiguousarray(I["x_prompt"][c]),
            "xs": np.ascontiguousarray(I["x_sample"][sl].reshape(NSTOK, D)),
            "state_C": np.ascontiguousarray(I["state_C"][0, sl]),
            "state_n": np.ascontiguousarray(I["state_n"][0, sl]),
            "state_m": np.ascontiguousarray(I["state_m"][0, sl]),
            "state_conv": np.ascontiguousarray(I["state_conv"][0, sl]),
            "page_table": np.ascontiguousarray(I["page_table"][sl]),
            "cache_kv": ckv, "cache_logf": clf,
            "w_in": I["w_in"][0], "b_fox_f": I["b_fox_f"], "b_ig": I["b_ig"], "b_fg": I["b_fg"],
            "b_og": I["b_og"], "conv_w": I["conv_w"][0], "conv_b": I["conv_b"],
            "mlstm_norm_w": I["mlstm_norm_w"], "w_o": I["w_o"][0], "ln1_g": I["ln1_g"], "ln1_b": I["ln1_b"],
            "w1": I["w1"][0], "w2": I["w2"][0], "ln2_g": I["ln2_g"], "ln2_b": I["ln2_b"],
        })
    res = run_bass_kernel_spmd(nc, in_maps, core_ids=list(range(n)))
    R = res.results

    def cat(name, shape):
        return np.stack([R[c][name] for c in range(n)], 0).reshape(shape).astype(np.float32)

    y = cat("o_y", (8, S, D))
    ys = cat("o_ys", (32, TD, D))
    k = cat("o_k", (1, 8, S, 8, 64))
    v = cat("o_v", (1, 8, S, 8, 64))
    lf = cat("o_logf", (1, 8, S, 8))
    C = cat("o_C", (1, 8, 4, 128, 128))
    nn = cat("o_n", (1, 8, 4, 128))
    m = cat("o_m", (1, 8, 4))
    conv = cat("o_conv", (1, 8, 3, 1024))
    ks = cat("o_ks", (1, 32, TD, 8, 64))
    vs = cat("o_vs", (1, 32, TD, 8, 64))
    lfs = cat("o_logfs", (1, 32, TD, 8))
    Cs = cat("o_Cs", (1, 32, 4, 128, 128))
    ns = cat("o_ns", (1, 32, 4, 128))
    ms = cat("o_ms", (1, 32, 4))
    convs = cat("o_convs", (1, 32, 3, 1024))
    return (y, ys, k, v, lf, C, nn, m, conv, ks, vs, lfs, Cs, ns, ms, convs)
```

```python
import os
import numpy as np
DBG = os.environ.get('KDBG', '')
SKIP = os.environ.get('KSKIP', '')
SAME_ENG_FREE = tuple(x for x in os.environ.get('KSEF', '').split(',') if x)
from contextlib import ExitStack
import concourse.bass as bass
import concourse.mybir as mybir
from concourse.bass_utils import run_bass_kernel_spmd

F32 = mybir.dt.float32
BF16 = mybir.dt.bfloat16
I32 = mybir.dt.int32
AF = mybir.ActivationFunctionType
ALU = mybir.AluOpType
AX = mybir.AxisListType

D = 1024
S = 4096
TS = 512
NST = S // TS
KC = 8
FQ, FK, FV, FF, MQK, MV, MI, MF, MO, INC = 0, 512, 1024, 1536, 1544, 2568, 3080, 3084, 3088, 3600
DFF = 4096
ALPHA = 2.0 ** 0.25
EPS = 1e-5
NEG = -30000.0
NSQ = 4
TD = 8
NSTOK = NSQ * TD
NPG = 128
NTOT = S + NSTOK
KSCALE = 128.0 ** -0.5
QSCALE = 64.0 ** -0.5


class Buf:
    __slots__ = ("name", "w", "r", "dsem", "dcnt")

    def __init__(self, name):
        self.name = name
        self.w = None
        self.r = []
        self.dsem = None
        self.dcnt = 0


class Op:
    __slots__ = ("eng", "fn", "deps", "dma", "tok", "inc", "done")


class Prog:
    def __init__(self, nc, es):
        self.nc = nc
        self.es = es
        self.E = {"pe": nc.tensor, "act": nc.scalar, "dve": nc.vector, "pool": nc.gpsimd, "sp": nc.sync}
        self.ops = {k: [] for k in self.E}
        self.sem = {k: es.enter_context(nc.semaphore("sem_" + k)) for k in self.E}
        self.cnt = {k: 0 for k in self.E}
        self.dbufs = []
        self.nb = 0
        self.flip = 0
        self.grp = None

    def buf(self, name="b"):
        self.nb += 1
        return Buf(f"{name}_{self.nb}")

    def bufs(self, n, name="b"):
        return [self.buf(name) for _ in range(n)]

    def _add(self, o, r, w):
        deps = []

        def add(d, kind):
            if d is None or d.done:
                return
            if d.dma and o.dma and d.tok[0] is o.tok[0]:
                return
            if (not d.dma) and d.eng == o.eng and not o.dma:
                if o.eng == "pe":
                    return
                if o.eng in SAME_ENG_FREE:
                    return
            if d not in deps:
                deps.append(d)
                d.inc = True

        for b in r:
            add(b.w, "raw")
        for b in w:
            add(b.w, "waw")
            for x in b.r:
                add(x, "war")
        o.deps = deps
        o.done = False
        for b in r:
            b.r.append(o)
        for b in w:
            b.w = o
            b.r = []
        self.ops[o.eng].append(o)

    def op(self, eng, fn, r=(), w=()):
        o = Op()
        o.eng = eng
        o.dma = False
        o.inc = False
        o.fn = fn
        o.tok = None
        self._add(o, r, w)
        return o

    def dmaf(self, eng, mk, r, w, sb):
        o = Op()
        o.eng = eng
        o.dma = True
        o.inc = False
        if sb.dsem is None:
            sb.dsem = self.es.enter_context(self.nc.semaphore("d_" + sb.name))
            self.dbufs.append(sb)
        sb.dcnt += 16
        sem = sb.dsem
        o.tok = (sem, sb.dcnt)
        o.fn = lambda e: mk(e).then_inc(sem, 16)
        self._add(o, r, w)
        if self.grp is not None:
            self.grp.append((o, sb))
        return o

    def group_begin(self):
        self.grp = []

    def group_end(self):
        for (o, sb) in self.grp:
            o.tok = (sb.dsem, sb.dcnt)
        self.grp = None

    def dma(self, eng, out, in_, r=(), w=(), sb=None, **kw):
        return self.dmaf(eng, lambda e: e.dma_start(out=out, in_=in_, **kw), r, w, sb)

    def emit(self):
        lasts = []
        for k, lst in self.ops.items():
            for o in reversed(lst):
                if not o.dma and o.fn is not None:
                    o.inc = True
                    lasts.append(o)
                    break
        dtoks = []
        for b in self.dbufs:
            t = Op()
            t.dma = True
            t.tok = (b.dsem, b.dcnt)
            t.done = False
            dtoks.append(t)
        for k in self.E:
            o = Op()
            o.eng = k
            o.dma = False
            o.inc = False
            o.fn = None
            o.tok = None
            o.done = False
            o.deps = [x for x in lasts if x.eng != k] + dtoks
            self.ops[k].append(o)
        for k, lst in self.ops.items():
            for o in lst:
                if not o.dma and o.inc and o.fn is not None:
                    self.cnt[k] += 1
                    o.tok = (self.sem[k], self.cnt[k])
        prog = self
        with self.nc.Block() as block:
            def run(k):
                def f(e):
                    waited = {}
                    for o in prog.ops[k]:
                        for d in o.deps:
                            sem, val = d.tok
                            if waited.get(id(sem), 0) < val:
                                e.wait_ge(sem, val)
                                waited[id(sem)] = val
                        if o.fn is not None:
                            ins = o.fn(e)
                            if (not o.dma) and o.inc:
                                ins.then_inc(prog.sem[k], 1)
                return f
            block.tensor(run("pe"))
            block.scalar(run("act"))
            block.vector(run("dve"))
            block.gpsimd(run("pool"))
            block.sync(run("sp"))
        n = 0
        for k in self.ops:
            for o in self.ops[k]:
                o.done = True
                o.fn = None
                o.deps = None
            n += len(self.ops[k])
            self.ops[k] = []
        return n

    def mm(self, out, lhsT, rhs, start, stop, r, w):
        self.op("pe", lambda e: e.matmul(out, lhsT, rhs, start=start, stop=stop), r, w)

    def tr(self, out, in_, ident, r, w):
        self.op("pe", lambda e: e.transpose(out, in_, ident), r, w)

    def ev(self):
        self.flip ^= 1
        return "act" if self.flip else "dve"

    def cp(self, eng, out, in_, r, w):
        if eng == "act":
            self.op("act", lambda e: e.copy(out, in_), r, w)
        else:
            self.op(eng, lambda e: e.tensor_copy(out=out, in_=in_), r, w)

    def act(self, out, in_, func, r, w, bias=None, scale=None):
        kw = {}
        if bias is not None:
            kw["bias"] = bias
        if scale is not None:
            kw["scale"] = scale
        self.op("act", lambda e: e.activation(out, in_, func, **kw), r, w)

    def tt(self, eng, out, in0, in1, op, r, w):
        self.op(eng, lambda e: e.tensor_tensor(out=out, in0=in0, in1=in1, op=op), r, w)

    def ts(self, eng, out, in0, s1, s2, op0, op1, r, w):
        if s2 is None:
            self.op(eng, lambda e: e.tensor_scalar(out=out, in0=in0, scalar1=s1, scalar2=None, op0=op0), r, w)
        else:
            self.op(eng, lambda e: e.tensor_scalar(out=out, in0=in0, scalar1=s1, scalar2=s2, op0=op0, op1=op1), r, w)

    def stt(self, eng, out, in0, scalar, in1, op0, op1, r, w):
        self.op(eng, lambda e: e.scalar_tensor_tensor(out=out, in0=in0, scalar=scalar, in1=in1, op0=op0, op1=op1), r, w)

    def memset(self, eng, ap, val, w):
        self.op(eng, lambda e: e.memset(ap, val), (), w)


class PsumPool:
    def __init__(self, pg, tiles):
        self.t = tiles
        self.b = [pg.buf("ps") for _ in tiles]
        self.i = 0

    def next(self):
        i = self.i
        self.i = (i + 1) % len(self.t)
        return self.t[i], self.b[i]


class Rot:
    def __init__(self, pg, tiles, name="rot"):
        self.t = tiles
        self.b = [pg.buf(name) for _ in tiles]
        self.i = 0

    def next(self):
        i = self.i
        self.i = (i + 1) % len(self.t)
        return self.t[i], self.b[i]


def build_program(debug=False, with_cache=True):
    nc = bass.Bass("TRN2", target_bir_lowering=False)
    es = ExitStack()

    def din(name, shape, dt=F32):
        return nc.dram_tensor(name, list(shape), dt, kind="ExternalInput").ap()

    def dout(name, shape, dt=F32):
        return nc.dram_tensor(name, list(shape), dt, kind="ExternalOutput").ap()

    x_d = din("x", [S, D])
    xs_d = din("xs", [NSTOK, D])
    if with_cache:
        ckv_d = din("cache_kv", [5120 * 128, 1024])
        clf_d = din("cache_logf", [5120 * 128, 8])
    sC_d = din("state_C", [NSQ, 4, 128, 128])
    sn_d = din("state_n", [NSQ, 4, 128])
    sm_d = din("state_m", [NSQ, 4])
    sconv_d = din("state_conv", [NSQ, 3, 1024])
    pt_d = din("page_table", [NSQ, NPG], I32)
    win_d = din("w_in", [D, INC])
    bff_d = din("b_fox_f", [1, 8])
    big_d = din("b_ig", [1, 4])
    bfg_d = din("b_fg", [1, 4])
    bog_d = din("b_og", [1, 512])
    cw_d = din("conv_w", [4, 1024])
    cb_d = din("conv_b", [1, 1024])
    mnw_d = din("mlstm_norm_w", [1, 512])
    wo_d = din("w_o", [D, D])
    l1g_d = din("ln1_g", [1, D])
    l1b_d = din("ln1_b", [1, D])
    w1_d = din("w1", [D, DFF])
    w2_d = din("w2", [DFF, D])
    l2g_d = din("ln2_g", [1, D])
    l2b_d = din("ln2_b", [1, D])

    y_o = dout("o_y", [S, D])
    ys_o = dout("o_ys", [NSTOK, D])
    k_o = dout("o_k", [S, 512])
    v_o = dout("o_v", [S, 512])
    lf_o = dout("o_logf", [S, 8])
    C_o = dout("o_C", [4, 128, 128])
    n_o = dout("o_n", [4, 128])
    m_o = dout("o_m", [1, 4])
    conv_o = dout("o_conv", [3, 1024])
    ksm_o = dout("o_ks", [NSTOK, 512])
    vsm_o = dout("o_vs", [NSTOK, 512])
    lfs_o = dout("o_logfs", [NSTOK, 8])
    Cs_o = dout("o_Cs", [NSQ, 4, 128, 128])
    ns_o = dout("o_ns", [NSQ, 4, 128])
    ms_o = dout("o_ms", [NSQ, 4])
    convs_o = dout("o_convs", [NSQ, 3, 1024])

    mix_d = nc.dram_tensor("mix_scratch", [128, 8, NTOT], BF16, kind=("ExternalOutput" if debug else "Internal")).ap()
    mix_db = None

    pg = Prog(nc, es)
    mix_db = pg.buf("mixd")

    def sb(name, shape, dt=F32, stack=None):
        return (stack or es).enter_context(nc.sbuf_tensor(name, list(shape), dt))

    ident_f = sb("ident_f", [128, 128])
    ident_b = sb("ident_b", [128, 128], BF16)
    mask01 = sb("mask01", [128, 128])
    ones_f = sb("ones_f", [128, 128])
    cB = pg.buf("const")

    def mk_consts():
        pg.memset("pool", ident_f[:], 0.0, [cB])
        pg.op("pool", lambda e: e.affine_select(out=ident_f[:], in_=ident_f[:], pattern=[[-1, 128]],
                                                 compare_op=ALU.not_equal, fill=1.0, base=0, channel_multiplier=1),
              [cB], [cB])
        pg.cp("pool", ident_b[:], ident_f[:], [cB], [cB])
        pg.memset("pool", ones_f[:], 1.0, [cB])
        pg.memset("pool", mask01[:], 1.0, [cB])
        pg.op("pool", lambda e: e.affine_select(out=mask01[:], in_=mask01[:], pattern=[[1, 128]],
                                                 compare_op=ALU.is_ge, fill=0.0, base=0, channel_multiplier=-1),
              [cB], [cB])

    mk_consts()


    def load_xT(src_d, row0, T, col0, xs_rot, psT, xT, xTb):
        xt, xb = xs_rot.next()
        pg.dma("sp", xt[0:T, :], src_d[row0:row0 + T, :], [], [xb], sb=xb)
        for g in range(2):
            pt, pb = psT.next()
            for cc in range(4):
                c = g * 4 + cc
                pg.tr(pt[:, cc * 128:cc * 128 + T], xt[0:T, c * 128:(c + 1) * 128], ident_f[0:T, 0:T], [xb, cB], [pb])
            pg.cp(pg.ev(), xT[:, g * 4:g * 4 + 4, col0:col0 + T],
                  pt[:, :].rearrange("p (c t) -> p c t", c=4)[:, :, 0:T], [pb], [xTb])
        return xt, xb

    LSL = int(os.environ.get("LSL", "9"))

    def log_sigmoid(out, src, a, b, tb, r, w):
        if LSL >= 1:
            pg.stt("dve", a, src, -1.0, src, ALU.mult, ALU.max, r, [tb])
        if LSL >= 2:
            pg.act(a, a, AF.Exp, [tb], [tb], scale=-1.0)
        if LSL >= 3:
            pg.act(a, a, AF.Ln, [tb], [tb], bias=one_c[0:a.shape[0], 0:1])
        if LSL >= 4:
            pg.ts("dve", b, src, 0.0, None, ALU.min, None, r, [tb])
        if LSL >= 5:
            pg.tt("dve", out, b, a, ALU.subtract, [tb], w)

    one_c = sb("one_c", [128, 1])
    eps_c = sb("eps_c", [128, 1])
    pg.memset("pool", one_c[:], 1.0, [cB])
    pg.memset("pool", eps_c[:], EPS, [cB])

    esA = ExitStack()

    def sA(name, shape, dt=F32):
        return esA.enter_context(nc.sbuf_tensor(name, list(shape), dt))

    def pA(name, dt=F32, n=512):
        return esA.enter_context(nc.psum_tensor(name, [128, n], dt))

    winA = sA("winA", [128, KC, 1544], BF16)
    winA_b = pg.buf("winA")
    wv = win_d.rearrange("(c p) n -> p c n", p=128)
    for c in range(KC):
        pg.dma("pool", winA[:, c, :], wv[:, c, 0:1544], [], [winA_b], sb=winA_b)
    ka = [sA(f"ka{h}", [70, S], BF16) for h in range(8)]
    ka_b = [[pg.buf("ka") for j in range(NST)] for h in range(8)]
    qa = [sA(f"qa{h}", [70, TS], BF16) for h in range(8)]
    qa_b = [pg.buf("qa") for h in range(8)]
    Vst = sA("Vst", [128, 32, 8, 128], BF16)
    Vst_b = [pg.buf("V") for t in range(32)]
    maskb = sA("maskb", [128, 4, 512], BF16)
    xsA = Rot(pg, [sA(f"xsA{i}", [128, D]) for i in range(1)], "xsA")
    xT_t = [sA(f"xTA{i}", [128, KC, TS], BF16) for i in range(1)] * 2
    xT_bs = [pg.buf("xT")] * 2
    stg = Rot(pg, [sA(f"stgA{i}", [128, 512]) for i in range(2)], "stgA")
    stgf = Rot(pg, [sA(f"stgfA{i}", [128, 8]) for i in range(2)], "stgfA")
    tmpA = sA("tmpA", [128, 512])
    tmpB = sA("tmpB", [128, 512])
    maskf = tmpA
    tmp_b = pg.buf("tmpA")
    tmpC = tmpA[8:16, :] if False else sA("tmpC", [8, 512])
    Lt = sA("Lt", [8, TS])
    Lcar = sA("Lcar", [8, 1])
    LP = sA("LP", [8, 3, TS], BF16)
    LN = sA("LN", [8, 3, TS], BF16)
    Lres = sA("Lres", [8, TS])
    L_b = pg.buf("L")
    semQA = pg.buf("semQA")
    semKA = pg.buf("semKA")
    bff_t = sA("bff_t", [128, 8])
    bff_c = sA("bff_c", [8, 1])
    PT = Rot(pg, [sA(f"PT{i}", [128, TS], BF16) for i in range(2)], "PT")
    rl = sA("rl", [64, TS])
    rl_b = pg.buf("rl")
    foxT = Rot(pg, [sA(f"foxT{i}", [128, 4, TS], BF16) for i in range(1)], "foxT")
    ps_mm = PsumPool(pg, [pA(f"psA_mm{i}") for i in range(2)])
    ps_T = PsumPool(pg, [pA(f"psA_T{i}") for i in range(1)])
    ps_s = PsumPool(pg, [pA(f"psA_s{i}") for i in range(3)])
    ps_o = PsumPool(pg, [pA(f"psA_o{i}") for i in range(2)])

    pg.dma("sp", bff_t[:, :], bff_d[0:1, :].partition_broadcast(128), [], [cB], sb=cB)
    pg.dma("sp", bff_c[:, :], bff_d.rearrange("o h -> h o"), [], [cB], sb=cB, allow_slow_non_contiguous=True)
    for h in range(8):
        pg.memset("pool", ka[h][64:70, :], 1.0, [ka_b[h][j] for j in range(NST)])
        pg.memset("pool", qa[h][64:70, :], 1.0, [qa_b[h]])
    pg.memset("pool", Vst[:, :, :, 64:128], 1.0, Vst_b)
    pg.memset("pool", Lcar[:], 0.0, [L_b])
    for r_ in range(4):
        pg.memset("pool", maskf[:], 0.0, [tmp_b])
        pg.op("pool", lambda e, r_=r_: e.affine_select(out=maskf[:], in_=maskf[:], pattern=[[1, 512]],
                                                        compare_op=ALU.is_ge, fill=NEG, base=-128 * r_,
                                                        channel_multiplier=-1), [tmp_b], [tmp_b])
        pg.cp("pool", maskb[:, r_, :], maskf[:], [tmp_b], [cB])

    def vaug(t, h):
        return Vst[:, t, h, :]

    for j in range((NST if 'j1' not in DBG else (0 if 'noloop' in DBG else 1)) if 'A' not in SKIP else 0):
        xT = xT_t[j % 2]
        xTb = xT_bs[j % 2]
        for tt in range(4):
            load_xT(x_d, j * TS + tt * 128, 128, tt * 128, xsA, ps_T, xT, xTb)
        pt, pb = ps_mm.next()
        for c in range(KC):
            pg.mm(pt[0:8, :], winA[:, c, FF:FF + 8], xT[:, c, :], c == 0, c == KC - 1, [xTb, winA_b], [pb])
        pg.ts("dve", tmpB[0:8, :], pt[0:8, :], bff_c[0:8, 0:1], None, ALU.add, None, [pb, cB], [tmp_b])
        log_sigmoid(Lres[0:8, :], tmpB[0:8, :], tmpA[0:8, :], tmpC[0:8, :], tmp_b, [tmp_b], [L_b])
        pg.op("dve", lambda e: e.tensor_tensor_scan(out=Lt[0:8, :], data0=one_c[0:8, 0:1].broadcast_to([8, TS]),
                                                    data1=Lres[0:8, :], initial=Lcar[0:8, 0:1], op0=ALU.mult, op1=ALU.add),
              [L_b, cB], [L_b])
        pg.cp("dve", Lcar[0:8, 0:1], Lt[0:8, TS - 1:TS], [L_b], [L_b])
        pg.cp("dve", LP[0:8, 0, :], Lt[0:8, :], [L_b], [L_b])
        pg.tt("dve", Lres[0:8, :], Lt[0:8, :], LP[0:8, 0, :], ALU.subtract, [L_b], [L_b])
        pg.cp("dve", LP[0:8, 1, :], Lres[0:8, :], [L_b], [L_b])
        pg.tt("dve", Lres[0:8, :], Lres[0:8, :], LP[0:8, 1, :], ALU.subtract, [L_b], [L_b])
        pg.cp("dve", LP[0:8, 2, :], Lres[0:8, :], [L_b], [L_b])
        pg.ts("dve", LN[0:8, :, :], LP[0:8, :, :], -1.0, None, ALU.mult, None, [L_b], [L_b])
        pg.group_begin()
        for h in range(0 if 'noL' not in DBG else 99, 8):
            pg.dma("sp", qa[h][64:67, :], LP[h:h + 1, :, :], [L_b], [qa_b[h]], sb=semQA)
            pg.dma("sp", ka[h][67:70, j * TS:(j + 1) * TS], LN[h:h + 1, :, :], [L_b], [ka_b[h][j]], sb=semQA)
        pg.group_end()
        for tt in range(4 if 'notok' not in DBG else 0):
            t = j * 4 + tt
            for (c0, kind) in ((FK, "k"), (FV, "v")):
                pt, pb = ps_mm.next()
                for c in range(KC):
                    pg.mm(pt[:, :], xT[:, c, tt * 128:(tt + 1) * 128], winA[:, c, c0:c0 + 512], c == 0, c == KC - 1,
                          [xTb, winA_b], [pb])
                st, stb = stg.next()
                pg.cp("act", st[:, :], pt[:, :], [pb], [stb])
                od = k_o if kind == "k" else v_o
                if "nodma" not in DBG:
                    pg.dma("sp", od[t * 128:(t + 1) * 128, :], st[:, :], [stb], [], sb=stb)
                if kind == "v" and "nov" not in DBG:
                    if True:
                        pg.cp("dve", Vst[:, t, :, 0:64], st[:, :].rearrange("p (h d) -> p h d", h=8), [stb], [Vst_b[t]])
                    else:
                        pg.cp("act" if "vact" in DBG else "dve", Vst[:, t, :, 0:64], pt[:, :].rearrange("p (h d) -> p h d", h=8), [pb], [Vst_b[t]])
            if 'skipf' in DBG:
                continue
            pt, pb = ps_mm.next()
            for c in range(KC):
                pg.mm(pt[:, 0:8], xT[:, c, tt * 128:(tt + 1) * 128], winA[:, c, FF:FF + 8], c == 0, c == KC - 1,
                      [xTb, winA_b], [pb])
            sf, sfb = stgf.next()
            pg.tt("dve", tmpA[:, 0:8], pt[:, 0:8], bff_t[:, :], ALU.add, [pb, cB], [tmp_b])
            log_sigmoid(sf[:, :], tmpA[:, 0:8], tmpA[:, 8:16], tmpA[:, 16:24], tmp_b, [tmp_b], [sfb])
            if "nolfdma" not in DBG:
                pg.dma("sp", lf_o[t * 128:(t + 1) * 128, :], sf[:, :], [sfb], [], sb=sfb)
        if 'nofm' in DBG:
            continue
        for pp in range(4):
            pt, pb = ps_mm.next()
            for c in range(KC):
                pg.mm(pt[:, :], winA[:, c, FQ + pp * 128:FQ + (pp + 1) * 128], xT[:, c, :], c == 0, c == KC - 1,
                      [xTb, winA_b], [pb])
            pg.op("act", lambda e, pt=pt, pp=pp: e.mul(qa[2 * pp][0:64, :], pt[0:64, :], QSCALE), [pb], [qa_b[2 * pp]])
            pg.ts("dve", qa[2 * pp + 1][0:64, :], pt[64:128, :], QSCALE, None, ALU.mult, None, [pb], [qa_b[2 * pp + 1]])
            pt, pb = ps_mm.next()
            for c in range(KC):
                pg.mm(pt[:, :], winA[:, c, FK + pp * 128:FK + (pp + 1) * 128], xT[:, c, :], c == 0, c == KC - 1,
                      [xTb, winA_b], [pb])
            pg.cp("act", ka[2 * pp][0:64, j * TS:(j + 1) * TS], pt[0:64, :], [pb], [ka_b[2 * pp][j]])
            pg.cp("dve", ka[2 * pp + 1][0:64, j * TS:(j + 1) * TS], pt[64:128, :], [pb], [ka_b[2 * pp + 1][j]])
        fx, fxb = foxT.next()
        for h in range(0 if 'noattn' not in DBG else 99, 8):
            po, pob = ps_o.next()
            nk = 4 * j + 4
            pend = []

            def do_pv(i, ptile, ptb, po=po, pob=pob, h=h, nk=nk):
                pg.mm(po[:, :], vaug(i, h), ptile[:, :], i == 0, i == nk - 1, [Vst_b[i], ptb], [pob])

            for i in range(nk):
                r_ = i - 4 * j
                pss, psb = ps_s.next()
                pg.mm(pss[:, :], ka[h][0:70, i * 128:(i + 1) * 128], qa[h][0:70, :], True, r_ < 0,
                      [ka_b[h][i // 4], qa_b[h]], [psb])
                if r_ >= 0:
                    pg.mm(pss[:, :], ident_b[:, :], maskb[:, r_, :], False, True, [cB], [psb])
                ptile, ptb = PT.next()
                pg.act(ptile[:, :], pss[:, :], AF.Exp, [psb], [ptb])
                pend.append((i, ptile, ptb))
                if len(pend) > 1:
                    do_pv(*pend.pop(0))
            while pend:
                do_pv(*pend.pop(0))
            pg.op("dve", lambda e, po=po: e.reciprocal(rl[0:64, :], po[64:128, :]), [pob], [rl_b])
            pg.tt("dve", fx[(h % 2) * 64:(h % 2) * 64 + 64, h // 2, :], po[0:64, :], rl[0:64, :], ALU.mult,
                  [pob, rl_b], [fxb])
        pg.dma("sp", mix_d[:, 0:4, j * TS:(j + 1) * TS], fx[:, :, :], [fxb], [mix_db], sb=fxb)
    nA = pg.emit()
    esA.close()

    woS = sb("woS", [128, KC, D], BF16)
    w1a = sb("w1a", [128, KC, DFF // 2], BF16)
    wo_b, w1_b, w2_b = pg.buf("wo"), pg.buf("w1"), pg.buf("w2")
    wov = wo_d.rearrange("(c p) n -> p c n", p=128)
    w1v = w1_d.rearrange("(c p) n -> p c n", p=128)
    w2v = w2_d.rearrange("(c p) n -> p c n", p=128)
    esB = ExitStack()

    def sB(name, shape, dt=F32):
        return esB.enter_context(nc.sbuf_tensor(name, list(shape), dt))

    def pB(name, dt=F32, n=512):
        return esB.enter_context(nc.psum_tensor(name, [128, n], dt))

    NB = INC - MQK
    winB = sB("winB", [128, KC, NB], BF16)
    winB_b = pg.buf("winB")
    for c in range(KC):
        pg.dma("pool", winB[:, c, 0:1024], wv[:, c, MQK:MQK + 1024], [], [winB_b], sb=winB_b)
        pg.dma("pool", winB[:, c, 1024:NB], wv[:, c, MQK + 1024:INC], [], [winB_b], sb=winB_b)
    for c in range(KC):
        pg.dma("pool", woS[:, c, :], wov[:, c, :], [], [wo_b], sb=wo_b)
    for c in range(KC):
        pg.dma("pool", w1a[:, c, :], w1v[:, c, 0:2048], [], [w1_b], sb=w1_b)
    cQK, cV, cI, cF, cO = 0, MV - MQK, MI - MQK, MF - MQK, MO - MQK
    xsB = Rot(pg, [sB(f"xsB{i}", [128, D]) for i in range(1)], "xsB")
    xTB = sB("xTB", [128, KC, TS], BF16)
    xTB_b = pg.buf("xTB")
    zc = [sB(f"zc{b}", [128, 3 + TS]) for b in range(8)]
    zc_b = [pg.buf("zc") for b in range(8)]
    qkT = [sB(f"qkT{b}", [128, TS], BF16) for b in range(8)]
    qkT_b = [pg.buf("qkT") for b in range(8)]
    cacc = sB("cacc", [128, TS])
    csig = sB("csig", [128, TS])
    cacc_b = pg.buf("cacc")
    cwT = sB("cwT", [128, 8, 4])
    cbT = sB("cbT", [128, 8])
    big_c = sB("big_c", [4, 1])
    bfg_c = sB("bfg_c", [4, 1])
    bog_t = sB("bog_t", [128, 512])
    mnw_t = sB("mnw_t", [128, 512])
    gB = pg.buf("gates")
    semOB = pg.buf("semOB")
    g_ig = sB("g_ig", [4, TS])
    g_lf = sB("g_lf", [4, TS])
    g_B = sB("g_B", [4, TS])
    g_u = sB("g_u", [4, TS])
    g_M = sB("g_M", [4, TS])
    g_t1 = sB("g_t1", [4, TS])
    g_t2 = sB("g_t2", [4, TS])
    g_wk = sB("g_wk", [4, TS])
    g_fl = sB("g_fl", [4, TS])
    g_car = sB("g_car", [4, 4])
    g_Rc = sB("g_Rc", [4, 4])
    g_Rp = sB("g_Rp", [4, 4])
    g_g = sB("g_g", [4, 4])
    g_gd = sB("g_gd", [4, 16])
    g_m = sB("g_m", [4, 1])
    tsc = sB("tsc", [128, 4, 8])
    tsc_b = pg.buf("tsc")
    gb = sB("gb", [128, 16])
    gb_b = pg.buf("gb")
    vaugs = Rot(pg, [sB(f"vaug{i}", [128, 4, 130], BF16) for i in range(2)], "vaug")
    ogs = Rot(pg, [sB(f"og{i}", [128, 512]) for i in range(1)], "og")
    ogtmp = sB("ogtmp", [128, 512])
    ogtmp_b = pg.buf("ogtmp")
    ktoks = Rot(pg, [sB(f"ktok{i}", [128, 4, 128], BF16) for i in range(2)], "ktok")
    ATs = Rot(pg, [sB(f"AT{i}", [128, 128], BF16) for i in range(2)], "AT")
    qgs = Rot(pg, [sB(f"qg{i}", [128, 128], BF16) for i in range(2)], "qg")
    numS = Rot(pg, [sB(f"numS{i}", [128, 130]) for i in range(2)], "numS")
    Cf = [sB(f"Cf{h}", [128, 130]) for h in range(4)]
    Cb = [sB(f"Cb{h}", [128, 130], BF16) for h in range(4)]
    Cf_b = [pg.buf("Cf") for h in range(4)]
    Cb_b = [pg.buf("Cb") for h in range(4)]
    hbufs = Rot(pg, [sB(f"hbuf{i}", [128, 512]) for i in range(2)], "hbuf")
    hsm = sB("hsm", [128, 8])
    hsm_b = pg.buf("hsm")
    stats = sB("stats", [128, 4, 6])
    mvs = sB("mvs", [128, 4, 4])
    mnb = Rot(pg, [sB(f"mnb{i}", [128, 512], BF16) for i in range(2)], "mnb")
    mnf = sB("mnf", [128, 512])
    mnf_b = pg.buf("mnf")
    mnT = Rot(pg, [sB(f"mnT{i}", [128, 4, TS], BF16) for i in range(2)], "mnT")
    psB_mm = PsumPool(pg, [pB(f"psB_mm{i}") for i in range(2)])
    psB_T = PsumPool(pg, [pB(f"psB_T{i}") for i in range(2)])
    psB_g = PsumPool(pg, [pB("psB_g")])
    psB_b = PsumPool(pg, [pB("psB_b", BF16, 1024)])
    psB_s = PsumPool(pg, [pB("psB_s")])
    psB_u = PsumPool(pg, [pB("psB_u")])

    for jj in range(4):
        pg.dma("sp", cwT[:, :, jj], cw_d[jj:jj + 1, :].rearrange("o (b p) -> p (o b)", p=128), [], [cB], sb=cB,
               allow_slow_non_contiguous=True)
    pg.dma("sp", cbT[:, :], cb_d.rearrange("o (b p) -> p (o b)", p=128), [], [cB], sb=cB, allow_slow_non_contiguous=True)
    pg.dma("sp", big_c[:, :], big_d.rearrange("o h -> h o"), [], [cB], sb=cB, allow_slow_non_contiguous=True)
    pg.dma("sp", bfg_c[:, :], bfg_d.rearrange("o h -> h o"), [], [cB], sb=cB, allow_slow_non_contiguous=True)
    pg.dma("sp", bog_t[:, :], bog_d[0:1, :].partition_broadcast(128), [], [cB], sb=cB)
    pg.dma("sp", mnw_t[:, :], mnw_d[0:1, :].partition_broadcast(128), [], [cB], sb=cB)
    for b in range(8):
        pg.memset("pool", zc[b][:, 0:3], 0.0, [zc_b[b]])
    pg.memset("pool", g_car[:, :], 0.0, [gB])
    for hh in range(2):
        vt = vaugs.t[hh]
        pg.memset("pool", vt[:, :, :], 0.0, [vaugs.b[hh]])

    for j in range((NST if 'j1' not in DBG else 1) if 'B' not in SKIP else 0):
        for tt in range(4):
            load_xT(x_d, j * TS + tt * 128, 128, tt * 128, xsB, psB_T, xTB, xTB_b)
        pgt, pgb = psB_g.next()
        for c in range(KC):
            pg.mm(pgt[0:4, :], winB[:, c, cI:cI + 4], xTB[:, c, :], c == 0, c == KC - 1, [xTB_b, winB_b], [pgb])
        pg.ts("dve", g_ig[:, :], pgt[0:4, :], big_c[0:4, 0:1], None, ALU.add, None, [pgb, cB], [gB])
        pgt, pgb = psB_g.next()
        for c in range(KC):
            pg.mm(pgt[0:4, :], winB[:, c, cF:cF + 4], xTB[:, c, :], c == 0, c == KC - 1, [xTB_b, winB_b], [pgb])
        pg.ts("dve", g_t1[:, :], pgt[0:4, :], bfg_c[0:4, 0:1], None, ALU.add, None, [pgb, cB], [gB])
        log_sigmoid(g_lf[:, :], g_t1[:, :], g_t2[:, :], g_M[:, :], gB, [gB], [gB])
        pg.op("dve", lambda e: e.tensor_tensor_scan(out=g_B[:, :], data0=one_c[0:4, 0:1].broadcast_to([4, TS]), data1=g_lf[:, :],
                                                    initial=g_car[0:4, 0:1], op0=ALU.mult, op1=ALU.add), [gB, cB], [gB])
        pg.tt("dve", g_u[:, :], g_ig[:, :], g_B[:, :], ALU.subtract, [gB], [gB])
        pg.op("dve", lambda e: e.tensor_tensor_scan(out=g_M[:, :], data0=one_c[0:4, 0:1].broadcast_to([4, TS]), data1=g_u[:, :],
                                                    initial=g_car[0:4, 1:2], op0=ALU.mult, op1=ALU.max), [gB, cB], [gB])
        pg.cp("dve", g_Rp[:, 0:1], g_car[:, 1:2], [gB], [gB])
        pg.cp("dve", g_Rc[:, :], g_M[:, :].rearrange("p (c t) -> p c t", c=4)[:, :, 127], [gB], [gB])
        pg.cp("dve", g_Rp[:, 1:4], g_Rc[:, 0:3], [gB], [gB])
        pg.cp("dve", g_car[:, 0:1], g_B[:, TS - 1:TS], [gB], [gB])
        pg.cp("dve", g_car[:, 1:2], g_M[:, TS - 1:TS], [gB], [gB])
        Rb = g_Rc[:, :].unsqueeze(2).broadcast_to([4, 4, 128])
        pg.tt("dve", g_t1[:, :].rearrange("p (c t) -> p c t", c=4), g_u[:, :].rearrange("p (c t) -> p c t", c=4), Rb,
              ALU.subtract, [gB], [gB])
        pg.act(g_wk[:, :], g_t1[:, :], AF.Exp, [gB], [gB])
        pg.tt("dve", g_t2[:, :].rearrange("p (c t) -> p c t", c=4), g_B[:, :].rearrange("p (c t) -> p c t", c=4), Rb,
              ALU.add, [gB], [gB])
        pg.act(g_fl[:, :], g_t2[:, :], AF.Exp, [gB], [gB], scale=-1.0)
        pg.tt("dve", g_g[:, :], g_Rp[:, :], g_Rc[:, :], ALU.subtract, [gB], [gB])
        pg.act(g_g[:, :], g_g[:, :], AF.Exp, [gB], [gB])
        for h2 in range(4):
            pg.ts("dve", g_gd[:, h2 * 4:(h2 + 1) * 4], g_g[:, :], ident_f[0:4, h2:h2 + 1], None, ALU.mult, None, [gB, cB], [gB])
        pgt, pgb = psB_g.next()
        for cc in range(4):
            pg.tr(pgt[:, cc * 8:cc * 8 + 4], g_wk[0:4, cc * 128:(cc + 1) * 128], ident_f[0:4, 0:4], [gB, cB], [pgb])
            pg.tr(pgt[:, cc * 8 + 4:cc * 8 + 8], g_fl[0:4, cc * 128:(cc + 1) * 128], ident_f[0:4, 0:4], [gB, cB], [pgb])
        pg.mm(pgt[:, 64:80], ones_f[0:4, 0:128], g_gd[0:4, 0:16], True, True, [gB, cB], [pgb])
        pg.cp("dve", tsc[:, :, :], pgt[:, 0:32].rearrange("p (c k) -> p c k", c=4), [pgb], [tsc_b])
        pg.cp("dve", gb[:, :], pgt[:, 64:80], [pgb], [gb_b])
        for blk in range(8):
            pt, pb = psB_mm.next()
            for c in range(KC):
                pg.mm(pt[:, :], winB[:, c, cQK + blk * 128:cQK + (blk + 1) * 128], xTB[:, c, :], c == 0, c == KC - 1,
                      [xTB_b, winB_b], [pb])
            pg.cp("act", zc[blk][:, 3:3 + TS], pt[:, :], [pb], [zc_b[blk]])
            pg.ts("dve", cacc[:, :], zc[blk][:, 3:3 + TS], cwT[:, blk, 3:4], cbT[:, blk:blk + 1], ALU.mult, ALU.add,
                  [zc_b[blk], cB], [cacc_b])
            for jj in (2, 1, 0):
                pg.stt("dve", cacc[:, :], zc[blk][:, jj:jj + TS], cwT[:, blk, jj:jj + 1], cacc[:, :], ALU.mult, ALU.add,
                       [zc_b[blk], cB, cacc_b], [cacc_b])
            pg.act(csig[:, :], cacc[:, :], AF.Sigmoid, [cacc_b], [cacc_b])
            pg.stt("dve", qkT[blk][:, :], cacc[:, :], (1.0 if blk < 4 else KSCALE), csig[:, :], ALU.mult, ALU.mult,
                   [cacc_b], [qkT_b[blk]])
            pg.cp("act", zc[blk][:, 0:3], zc[blk][:, TS:TS + 3], [zc_b[blk]], [zc_b[blk]])
            if j == NST - 1:
                pg.dma("sp", conv_o[:, blk * 128:(blk + 1) * 128].rearrange("t f -> f t"), zc[blk][:, 0:3], [zc_b[blk]], [],
                       sb=semOB, allow_slow_non_contiguous=True)
        mt, mtb = mnT.next()
        for tt in range(4):
            t = j * 4 + tt
            sl = slice(tt * 128, (tt + 1) * 128)
            first = (t == 0)
            va, vab = vaugs.next()
            pt, pb = psB_mm.next()
            for c in range(KC):
                pg.mm(pt[:, :], xTB[:, c, sl], winB[:, c, cV:cV + 512], c == 0, c == KC - 1, [xTB_b, winB_b], [pb])
            for hm in range(4):
                pg.ts("dve", va[:, hm, 0:128], pt[:, hm * 128:(hm + 1) * 128], tsc[:, tt, hm:hm + 1], None, ALU.mult, None,
                      [pb, tsc_b], [vab])
            pg.cp("dve", va[:, :, 128:129], tsc[:, tt, 0:4].unsqueeze(2), [tsc_b], [vab])
            ogt, ogb = ogs.next()
            pt, pb = psB_mm.next()
            for c in range(KC):
                pg.mm(pt[:, :], xTB[:, c, sl], winB[:, c, cO:cO + 512], c == 0, c == KC - 1, [xTB_b, winB_b], [pb])
            pg.tt("dve", ogtmp[:, :], pt[:, :], bog_t[:, :], ALU.add, [pb, cB], [ogtmp_b])
            pg.act(ogt[:, :], ogtmp[:, :], AF.Sigmoid, [ogtmp_b], [ogb])
            kt, ktb = ktoks.next()
            pbt, pbb = psB_b.next()
            for hm in range(4):
                pg.tr(pbt[:, hm * 128:(hm + 1) * 128], qkT[4 + hm][:, sl], ident_b[:, :], [qkT_b[4 + hm], cB], [pbb])
            pg.cp("act", kt[:, :, :], pbt[:, 0:512].rearrange("p (h d) -> p h d", h=4), [pbb], [ktb])
            hb, hbb = hbufs.next()
            for hm in range(4):
                gcol = gb[:, hm * 4 + tt:hm * 4 + tt + 1]
                pss, psb = psB_s.next()
                pg.mm(pss[:, 0:128], qkT[4 + hm][:, sl], qkT[hm][:, sl], True, True, [qkT_b[4 + hm], qkT_b[hm]], [psb])
                at, atb = ATs.next()
                pg.tt("dve", at[:, :], pss[:, 0:128], mask01[:, :], ALU.mult, [psb, cB], [atb])
                pg.mm(pss[:, 256:385], at[:, :], va[:, hm, 0:129], True, first, [atb, vab], [psb])
                if not first:
                    qgt, qgb = qgs.next()
                    pg.ts("dve", qgt[:, :], qkT[hm][:, sl], gcol, None, ALU.mult, None, [qkT_b[hm], gb_b], [qgb])
                    pg.mm(pss[:, 256:385], qgt[:, :], Cb[hm][:, 0:129], False, True, [qgb, Cb_b[hm]], [psb])
                ns_, nsb = numS.next()
                pg.cp("act", ns_[:, 0:129], pss[:, 256:385], [psb], [nsb])
                psu, pub = psB_u.next()
                pg.mm(psu[:, 0:129], kt[:, hm, :], va[:, hm, 0:129], True, True, [ktb, vab], [pub])
                if first:
                    pg.cp("dve", Cf[hm][:, 0:129], psu[:, 0:129], [pub], [Cf_b[hm]])
                else:
                    pg.stt("dve", Cf[hm][:, 0:129], Cf[hm][:, 0:129], gcol, psu[:, 0:129], ALU.mult, ALU.add,
                           [pub, gb_b, Cf_b[hm]], [Cf_b[hm]])
                pg.cp("act", Cb[hm][:, 0:129], Cf[hm][:, 0:129], [Cf_b[hm]], [Cb_b[hm]])
                pg.stt("dve", hsm[:, 0:1], ns_[:, 128:129], -1.0, ns_[:, 128:129], ALU.mult, ALU.max, [nsb], [hsm_b])
                pg.tt("dve", hsm[:, 1:2], hsm[:, 0:1], tsc[:, tt, 4 + hm:5 + hm], ALU.max, [hsm_b, tsc_b], [hsm_b])
                pg.op("dve", lambda e: e.reciprocal(hsm[:, 2:3], hsm[:, 1:2]), [hsm_b], [hsm_b])
                pg.ts("dve", hb[:, hm * 128:(hm + 1) * 128], ns_[:, 0:128], hsm[:, 2:3], None, ALU.mult, None,
                      [nsb, hsm_b], [hbb])
            for hm in range(4):
                pg.op("dve", lambda e, hm=hm, hb=hb: e.bn_stats(stats[:, hm, :], hb[:, hm * 128:(hm + 1) * 128]), [hbb], [hsm_b])
                pg.op("dve", lambda e, hm=hm: e.bn_aggr(mvs[:, hm, 0:2], stats[:, hm, :]), [hsm_b], [hsm_b])
            pg.act(mvs[:, :, 2:3], mvs[:, :, 1:2], AF.Ln, [hsm_b], [hsm_b], bias=eps_c[:, 0:1])
            pg.act(mvs[:, :, 2:3], mvs[:, :, 2:3], AF.Exp, [hsm_b], [hsm_b], scale=-0.5)
            for hm in range(4):
                pg.ts("dve", mnf[:, hm * 128:(hm + 1) * 128], hb[:, hm * 128:(hm + 1) * 128], mvs[:, hm, 0:1], mvs[:, hm, 2:3],
                      ALU.subtract, ALU.mult, [hbb, hsm_b], [mnf_b])
            pg.tt("pool", mnf[:, :], mnf[:, :], mnw_t[:, :], ALU.mult, [mnf_b, cB], [mnf_b])
            mb, mbb = mnb.next()
            pg.tt("pool", mb[:, :], mnf[:, :], ogt[:, :], ALU.mult, [mnf_b, ogb], [mbb])
            pbt, pbb = psB_b.next()
            for hm in range(4):
                pg.tr(pbt[:, hm * 128:(hm + 1) * 128], mb[:, hm * 128:(hm + 1) * 128], ident_b[:, :], [mbb, cB], [pbb])
            pg.cp("act", mt[:, :, sl], pbt[:, 0:512].rearrange("p (h d) -> p h d", h=4), [pbb], [mtb])
        pg.dma("sp", mix_d[:, 4:8, j * TS:(j + 1) * TS], mt[:, :, :], [mtb], [mix_db], sb=mtb)
    for hm in range(4):
        pg.dma("sp", C_o[hm, :, :], Cf[hm][:, 0:128], [Cf_b[hm]], [], sb=semOB)
        pg.dma("sp", n_o[hm:hm + 1, :].rearrange("o d -> d o"), Cf[hm][:, 128:129], [Cf_b[hm]], [], sb=semOB,
               allow_slow_non_contiguous=True)
    pg.tt("dve", g_m[:, :], g_car[:, 0:1], g_car[:, 1:2], ALU.add, [gB], [gB])
    pg.dma("sp", m_o.rearrange("o h -> h o"), g_m[:, :], [gB], [], sb=semOB, allow_slow_non_contiguous=True)
    nB = pg.emit()
    esB.close()

    if 'nosample' not in DBG:
        esS = ExitStack()

        def sS(name, shape, dt=F32):
            return esS.enter_context(nc.sbuf_tensor(name, list(shape), dt))

        def pS(name, dt=F32, n=512):
            return esS.enter_context(nc.psum_tensor(name, [128, n], dt))

        NT = NSTOK
        winS = sS("winS", [128, KC, INC], BF16)
        winS_b = pg.buf("winS")
        for c in range(KC):
            for (a0, a1) in ((0, 1544), (1544, 2568), (2568, INC)):
                pg.dma("pool", winS[:, c, a0:a1], wv[:, c, a0:a1], [], [winS_b], sb=winS_b)
        semS = pg.buf("semS")
        semSo = pg.buf("semSo")
        xsS = Rot(pg, [sS("xsS0", [128, D])], "xsS")
        xsT = sS("xsT", [128, KC, NT], BF16)
        xsT_b = pg.buf("xsT")
        psS_mm = PsumPool(pg, [pS("psS_mm0")])
        psS_T = PsumPool(pg, [pS("psS_T0")])
        psS_g = psS_T
        psS_b = PsumPool(pg, [pS(f"psS_b{i}", BF16, 1024) for i in range(2)])
        psS_s = PsumPool(pg, [pS(f"psS_s{i}") for i in range(2)])
        psS_o = PsumPool(pg, [pS("psS_o")])
        psS_l = PsumPool(pg, [pS("psS_l")])
        sc = pg.buf("sconst")
        bffS = sS("bffS", [128, 8])
        bogS = sS("bogS", [128, 512])
        mnwS = sS("mnwS", [128, 512])
        cwS = sS("cwS", [128, 8, 4])
        cbS = sS("cbS", [128, 8])
        bigS = sS("bigS", [4, 1])
        bfgS = sS("bfgS", [4, 1])
        smS = sS("smS", [4, NSQ])
        maskU = sS("maskU", [128, 128])
        ones_b = sS("ones_b", [128, 1], BF16)
        pg.group_begin()
        pg.dma("sp", bffS[:, :], bff_d[0:1, :].partition_broadcast(128), [], [sc], sb=semS)
        pg.dma("sp", bogS[:, :], bog_d[0:1, :].partition_broadcast(128), [], [sc], sb=semS)
        pg.dma("sp", mnwS[:, :], mnw_d[0:1, :].partition_broadcast(128), [], [sc], sb=semS)
        for jj in range(4):
            pg.dma("sp", cwS[:, :, jj], cw_d[jj:jj + 1, :].rearrange("o (b p) -> p (o b)", p=128), [], [sc], sb=semS,
                   allow_slow_non_contiguous=True)
        pg.dma("sp", cbS[:, :], cb_d.rearrange("o (b p) -> p (o b)", p=128), [], [sc], sb=semS, allow_slow_non_contiguous=True)
        pg.dma("sp", bigS[:, :], big_d.rearrange("o h -> h o"), [], [sc], sb=semS, allow_slow_non_contiguous=True)
        pg.dma("sp", bfgS[:, :], bfg_d.rearrange("o h -> h o"), [], [sc], sb=semS, allow_slow_non_contiguous=True)
        pg.dma("sp", smS[:, :], sm_d.rearrange("b h -> h b"), [], [sc], sb=semS, allow_slow_non_contiguous=True)
        pg.group_end()
        pg.ts("pool", maskU[:, :], mask01[:, :], -1.0, 1.0, ALU.mult, ALU.add, [cB], [sc])
        pg.memset("pool", ones_b[:, :], 1.0, [sc])

        load_xT(xs_d, 0, NT, 0, xsS, psS_T, xsT, xsT_b)

        bdq = sS("bdq", [128, NSQ, 4, 16], BF16)
        bdq_b = pg.buf("bdq")
        pg.memset("pool", bdq[:, :, :, :], 0.0, [bdq_b])
        kTn = sS("kTn", [128, 4, NT], BF16)
        kTn_b = pg.buf("kTn")
        for pp in range(4):
            pt, pb = psS_mm.next()
            for c in range(KC):
                pg.mm(pt[:, 0:NT], winS[:, c, FQ + pp * 128:FQ + (pp + 1) * 128], xsT[:, c, :], c == 0, c == KC - 1,
                      [xsT_b, winS_b], [pb])
            pg.ts("dve", bdq[0:64, :, pp, 0:8], pt[0:64, 0:NT].rearrange("p (b t) -> p b t", b=NSQ), QSCALE, None, ALU.mult, None,
                  [pb], [bdq_b])
            pg.ts("dve", bdq[64:128, :, pp, 8:16], pt[64:128, 0:NT].rearrange("p (b t) -> p b t", b=NSQ), QSCALE, None, ALU.mult,
                  None, [pb], [bdq_b])
            pt, pb = psS_mm.next()
            for c in range(KC):
                pg.mm(pt[:, 0:NT], winS[:, c, FK + pp * 128:FK + (pp + 1) * 128], xsT[:, c, :], c == 0, c == KC - 1,
                      [xsT_b, winS_b], [pb])
            pg.cp("act", kTn[:, pp, :], pt[:, 0:NT], [pb], [kTn_b])
        gS = pg.buf("gS")
        q_ig = sS("q_ig", [4, NT]); q_lf = sS("q_lf", [4, NT]); q_B = sS("q_B", [4, NT]); q_u = sS("q_u", [4, NT])
        q_M = sS("q_M", [4, NT]); q_t1 = sS("q_t1", [4, NT]); q_t2 = sS("q_t2", [4, NT]); q_wk = sS("q_wk", [4, NT])
        q_fl = sS("q_fl", [4, NT]); q_R = sS("q_R", [4, NSQ]); q_g = sS("q_g", [4, NSQ]); q_gd = sS("q_gd", [4, 16])
        q_m = sS("q_m", [4, NSQ])
        pgt, pgb = psS_g.next()
        for c in range(KC):
            pg.mm(pgt[0:4, 0:NT], winS[:, c, MI:MI + 4], xsT[:, c, :], c == 0, c == KC - 1, [xsT_b, winS_b], [pgb])
        pg.ts("dve", q_ig[:, :], pgt[0:4, 0:NT], bigS[0:4, 0:1], None, ALU.add, None, [pgb, sc], [gS])
        pgt, pgb = psS_g.next()
        for c in range(KC):
            pg.mm(pgt[0:4, 0:NT], winS[:, c, MF:MF + 4], xsT[:, c, :], c == 0, c == KC - 1, [xsT_b, winS_b], [pgb])
        pg.ts("dve", q_t1[:, :], pgt[0:4, 0:NT], bfgS[0:4, 0:1], None, ALU.add, None, [pgb, sc], [gS])
        log_sigmoid(q_lf[:, :], q_t1[:, :], q_t2[:, :], q_M[:, :], gS, [gS], [gS])
        for b in range(NSQ):
            cs = slice(b * TD, (b + 1) * TD)
            pg.op("dve", lambda e, cs=cs: e.tensor_tensor_scan(out=q_B[:, cs], data0=one_c[0:4, 0:1].broadcast_to([4, TD]),
                                                                data1=q_lf[:, cs], initial=0.0, op0=ALU.mult, op1=ALU.add),
                  [gS, cB], [gS])
        pg.tt("dve", q_u[:, :], q_ig[:, :], q_B[:, :], ALU.subtract, [gS], [gS])
        for b in range(NSQ):
            cs = slice(b * TD, (b + 1) * TD)
            pg.op("dve", lambda e, cs=cs, b=b: e.tensor_tensor_scan(out=q_M[:, cs], data0=one_c[0:4, 0:1].broadcast_to([4, TD]),
                                                                     data1=q_u[:, cs], initial=smS[0:4, b:b + 1],
                                                                     op0=ALU.mult, op1=ALU.max), [gS, cB, sc], [gS])
        pg.cp("dve", q_R[:, :], q_M[:, :].rearrange("p (b t) -> p b t", b=NSQ)[:, :, TD - 1], [gS], [gS])
        RbS = q_R[:, :].unsqueeze(2).broadcast_to([4, NSQ, TD])
        pg.tt("dve", q_t1[:, :].rearrange("p (b t) -> p b t", b=NSQ), q_u[:, :].rearrange("p (b t) -> p b t", b=NSQ), RbS,
              ALU.subtract, [gS], [gS])
        pg.act(q_wk[:, :], q_t1[:, :], AF.Exp, [gS], [gS])
        pg.tt("dve", q_t2[:, :].rearrange("p (b t) -> p b t", b=NSQ), q_B[:, :].rearrange("p (b t) -> p b t", b=NSQ), RbS,
              ALU.add, [gS], [gS])
        pg.act(q_fl[:, :], q_t2[:, :], AF.Exp, [gS], [gS], scale=-1.0)
        pg.tt("dve", q_g[:, :], smS[:, :], q_R[:, :], ALU.subtract, [gS, sc], [gS])
        pg.act(q_g[:, :], q_g[:, :], AF.Exp, [gS], [gS])
        pg.tt("dve", q_m[:, :], q_R[:, :], q_B[:, :].rearrange("p (b t) -> p b t", b=NSQ)[:, :, TD - 1], ALU.add, [gS], [gS])
        pg.dma("sp", ms_o.rearrange("b h -> h b"), q_m[:, :], [gS], [], sb=semSo, allow_slow_non_contiguous=True)
        for h2 in range(4):
            pg.ts("dve", q_gd[:, h2 * 4:(h2 + 1) * 4], q_g[:, :], ident_f[0:4, h2:h2 + 1], None, ALU.mult, None, [gS, cB], [gS])
        tscS = sS("tscS", [8, NSQ, 8])
        gbS = sS("gbS", [128, 16])
        tscS_b = pg.buf("tscS")
        pgt, pgb = psS_g.next()
        for b in range(NSQ):
            pg.tr(pgt[0:TD, b * 8:b * 8 + 4], q_wk[0:4, b * TD:(b + 1) * TD], ident_f[0:4, 0:4], [gS, cB], [pgb])
            pg.tr(pgt[0:TD, b * 8 + 4:b * 8 + 8], q_fl[0:4, b * TD:(b + 1) * TD], ident_f[0:4, 0:4], [gS, cB], [pgb])
        pg.mm(pgt[:, 64:80], ones_f[0:4, 0:128], q_gd[0:4, 0:16], True, True, [gS, cB], [pgb])
        pg.cp("dve", tscS[:, :, :], pgt[0:TD, 0:32].rearrange("p (b k) -> p b k", b=NSQ), [pgb], [tscS_b])
        pg.cp("dve", gbS[:, :], pgt[:, 64:80], [pgb], [tscS_b])
        zcS = [sS(f"zcS{b}", [128, NSQ, 3 + TD]) for b in range(8)]
        zcS_b = [pg.buf("zcS") for b in range(8)]
        qkS = [sS(f"qkS{b}", [128, NT], BF16) for b in range(8)]
        qkS_b = [pg.buf("qkS") for b in range(8)]
        caccS = sS("caccS", [128, NSQ, TD]); csigS = sS("csigS", [128, NSQ, TD])
        caccS_b = pg.buf("caccS")
        semZ = pg.buf("semZ")
        pg.group_begin()
        for blk in range(8):
            for b in range(NSQ):
                pg.dma("sp", zcS[blk][:, b, 0:3], sconv_d[b, :, blk * 128:(blk + 1) * 128].rearrange("t f -> f t"), [], [zcS_b[blk]],
                       sb=semZ, allow_slow_non_contiguous=True)
        pg.group_end()
        for blk in range(8):
            pt, pb = psS_mm.next()
            for c in range(KC):
                pg.mm(pt[:, 0:NT], winS[:, c, MQK + blk * 128:MQK + (blk + 1) * 128], xsT[:, c, :], c == 0, c == KC - 1,
                      [xsT_b, winS_b], [pb])
            pg.cp("act", zcS[blk][:, :, 3:3 + TD], pt[:, 0:NT].rearrange("p (b t) -> p b t", b=NSQ), [pb], [zcS_b[blk]])
            pg.ts("dve", caccS[:, :, :], zcS[blk][:, :, 3:3 + TD], cwS[:, blk, 3:4], cbS[:, blk:blk + 1], ALU.mult, ALU.add,
                  [zcS_b[blk], sc], [caccS_b])
            for jj in (2, 1, 0):
                pg.stt("dve", caccS[:, :, :], zcS[blk][:, :, jj:jj + TD], cwS[:, blk, jj:jj + 1], caccS[:, :, :], ALU.mult, ALU.add,
                       [zcS_b[blk], sc, caccS_b], [caccS_b])
            pg.act(csigS[:, :, :], caccS[:, :, :], AF.Sigmoid, [caccS_b], [caccS_b])
            pg.stt("dve", qkS[blk][:, :].rearrange("p (b t) -> p b t", b=NSQ), caccS[:, :, :], (1.0 if blk < 4 else KSCALE),
                   csigS[:, :, :], ALU.mult, ALU.mult, [caccS_b], [qkS_b[blk]])
            for b in range(NSQ):
                pg.dma("sp", convs_o[b, :, blk * 128:(blk + 1) * 128].rearrange("t f -> f t"), zcS[blk][:, b, TD:TD + 3],
                       [zcS_b[blk]], [], sb=semSo, allow_slow_non_contiguous=True)

        knS = sS("knS", [8, NSQ, 512]); vnS = sS("vnS", [8, NSQ, 512]); vnB = sS("vnB", [8, NSQ, 512], BF16)
        lfnS = sS("lfnS", [8, NSQ, 8]); ltmp = sS("ltmp", [8, 32])
        vaS = sS("vaS", [8, NSQ, 4, 130], BF16); ogS = sS("ogS", [8, NSQ, 512]); ogtS = sS("ogtS", [8, 512])
        ktS = sS("ktS", [8, NSQ, 4, 128], BF16)
        tokS_b = [pg.buf("tokS") for b in range(NSQ)]
        ltmp_b = pg.buf("ltmp")
        pg.memset("pool", vaS[:, :, :, :], 0.0, tokS_b)
        lfn_bs = []
        for b in range(NSQ):
            cs = slice(b * TD, (b + 1) * TD)
            tb = tokS_b[b]
            for (c0, dst, od) in ((FK, knS, ksm_o), (FV, vnS, vsm_o)):
                pt, pb = psS_mm.next()
                for c in range(KC):
                    pg.mm(pt[0:TD, :], xsT[:, c, cs], winS[:, c, c0:c0 + 512], c == 0, c == KC - 1, [xsT_b, winS_b], [pb])
                ob_ = pg.buf("kvn")
                pg.cp("act", dst[:, b, :], pt[0:TD, :], [pb], [ob_])
                pg.dma("sp", od[b * TD:(b + 1) * TD, :], dst[:, b, :], [ob_], [], sb=semSo)
            pg.cp("dve", vnB[:, b, :], vnS[:, b, :], [ob_], [tb])
            pt, pb = psS_mm.next()
            for c in range(KC):
                pg.mm(pt[0:TD, 0:8], xsT[:, c, cs], winS[:, c, FF:FF + 8], c == 0, c == KC - 1, [xsT_b, winS_b], [pb])
            pg.tt("dve", ltmp[:, 0:8], pt[0:TD, 0:8], bffS[0:TD, :], ALU.add, [pb, sc], [ltmp_b])
            lfn_b = pg.buf("lfn")
            lfn_bs.append(lfn_b)
            log_sigmoid(lfnS[:, b, :], ltmp[:, 0:8], ltmp[:, 8:16], ltmp[:, 16:24], ltmp_b, [ltmp_b], [lfn_b])
            pg.dma("sp", lfs_o[b * TD:(b + 1) * TD, :], lfnS[:, b, :], [lfn_b], [], sb=semSo)
            pt, pb = psS_mm.next()
            for c in range(KC):
                pg.mm(pt[0:TD, :], xsT[:, c, cs], winS[:, c, MV:MV + 512], c == 0, c == KC - 1, [xsT_b, winS_b], [pb])
            for hm in range(4):
                pg.ts("dve", vaS[:, b, hm, 0:128], pt[0:TD, hm * 128:(hm + 1) * 128], tscS[:, b, hm:hm + 1], None, ALU.mult, None,
                      [pb, tscS_b], [tb])
            pg.cp("dve", vaS[:, b, :, 128:129], tscS[:, b, 0:4].unsqueeze(2), [tscS_b], [tb])
            pt, pb = psS_mm.next()
            for c in range(KC):
                pg.mm(pt[0:TD, :], xsT[:, c, cs], winS[:, c, MO:MO + 512], c == 0, c == KC - 1, [xsT_b, winS_b], [pb])
            pg.tt("dve", ogtS[:, :], pt[0:TD, :], bogS[0:TD, :], ALU.add, [pb, sc], [ltmp_b])
            pg.act(ogS[:, b, :], ogtS[:, :], AF.Sigmoid, [ltmp_b], [tb])
            pbt, pbb = psS_b.next()
            for hm in range(4):
                pg.tr(pbt[0:TD, hm * 128:(hm + 1) * 128], qkS[4 + hm][:, cs], ident_b[:, :], [qkS_b[4 + hm], cB], [pbb])
            pg.cp("act", ktS[:, b, :, :], pbt[0:TD, 0:512].rearrange("p (h d) -> p h d", h=4), [pbb], [tb])

        CfS = Rot(pg, [sS(f"CfS{i}", [128, 130]) for i in range(2)], "CfS")
        CbS = Rot(pg, [sS(f"CbS{i}", [128, 130], BF16) for i in range(2)], "CbS")
        CnS = Rot(pg, [sS(f"CnS{i}", [128, 130]) for i in range(2)], "CnS")
        ATS = Rot(pg, [sS(f"ATS{i}", [8, 8], BF16) for i in range(2)], "ATS")
        qgS = Rot(pg, [sS(f"qgS{i}", [128, 8], BF16) for i in range(2)], "qgS")
        nsS = Rot(pg, [sS(f"nsS{i}", [8, 130]) for i in range(2)], "nsS")
        hbS = Rot(pg, [sS(f"hbS{i}", [8, 512]) for i in range(2)], "hbS")
        hsmS = sS("hsmS", [8, 8]); statS = sS("statS", [8, 4, 6]); mvS = sS("mvS", [8, 4, 4]); mnfS = sS("mnfS", [8, 512])
        hsmS_b = pg.buf("hsmS"); mnfS_b = pg.buf("mnfS")
        mnbS = Rot(pg, [sS(f"mnbS{i}", [8, 512], BF16) for i in range(2)], "mnbS")
        mixS = sS("mixS", [128, 8, NT], BF16)
        mixS_b = pg.buf("mixS")
        for b in range(NSQ):
            cs = slice(b * TD, (b + 1) * TD)
            tb = tokS_b[b]
            hb, hbb = hbS.next()
            for hm in range(4):
                cf, cfb = CfS.next()
                pg.group_begin()
                pg.dma("sp", cf[:, 0:128], sC_d[b, hm, :, :], [], [cfb], sb=cfb)
                pg.dma("sp", cf[:, 128:129], sn_d[b, hm:hm + 1, :].rearrange("o d -> d o"), [], [cfb], sb=cfb,
                       allow_slow_non_contiguous=True)
                pg.group_end()
                cb_, cbb = CbS.next()
                pg.cp("act", cb_[:, 0:129], cf[:, 0:129], [cfb], [cbb])
                gcol = gbS[:, hm * 4 + b:hm * 4 + b + 1]
                pss, psb = psS_s.next()
                pg.mm(pss[0:TD, 0:TD], qkS[4 + hm][:, cs], qkS[hm][:, cs], True, True, [qkS_b[4 + hm], qkS_b[hm]], [psb])
                at, atb = ATS.next()
                pg.tt("dve", at[:, :], pss[0:TD, 0:TD], mask01[0:TD, 0:TD], ALU.mult, [psb, cB], [atb])
                qgt, qgb = qgS.next()
                pg.ts("dve", qgt[:, :], qkS[hm][:, cs], gcol, None, ALU.mult, None, [qkS_b[hm], tscS_b], [qgb])
                pg.mm(pss[0:TD, 256:385], at[:, :], vaS[:, b, hm, 0:129], True, False, [atb, tb], [psb])
                pg.mm(pss[0:TD, 256:385], qgt[:, :], cb_[:, 0:129], False, True, [qgb, cbb], [psb])
                ns_, nsb = nsS.next()
                pg.cp("act", ns_[:, 0:129], pss[0:TD, 256:385], [psb], [nsb])
                psu, pub = psS_s.next()
                pg.mm(psu[:, 0:129], ktS[:, b, hm, :], vaS[:, b, hm, 0:129], True, True, [tb], [pub])
                cn, cnb = CnS.next()
                pg.stt("dve", cn[:, 0:129], cf[:, 0:129], gcol, psu[:, 0:129], ALU.mult, ALU.add, [pub, tscS_b, cfb], [cnb])
                pg.group_begin()
                pg.dma("sp", Cs_o[b, hm, :, :], cn[:, 0:128], [cnb], [], sb=cnb)
                pg.dma("sp", ns_o[b, hm:hm + 1, :].rearrange("o d -> d o"), cn[:, 128:129], [cnb], [], sb=cnb,
                       allow_slow_non_contiguous=True)
                pg.group_end()
                pg.stt("dve", hsmS[:, 0:1], ns_[:, 128:129], -1.0, ns_[:, 128:129], ALU.mult, ALU.max, [nsb], [hsmS_b])
                pg.tt("dve", hsmS[:, 1:2], hsmS[:, 0:1], tscS[:, b, 4 + hm:5 + hm], ALU.max, [hsmS_b, tscS_b], [hsmS_b])
                pg.op("dve", lambda e: e.reciprocal(hsmS[:, 2:3], hsmS[:, 1:2]), [hsmS_b], [hsmS_b])
                pg.ts("dve", hb[:, hm * 128:(hm + 1) * 128], ns_[:, 0:128], hsmS[:, 2:3], None, ALU.mult, None, [nsb, hsmS_b], [hbb])
            for hm in range(4):
                pg.op("dve", lambda e, hm=hm, hb=hb: e.bn_stats(statS[:, hm, :], hb[:, hm * 128:(hm + 1) * 128]), [hbb], [hsmS_b])
                pg.op("dve", lambda e, hm=hm: e.bn_aggr(mvS[:, hm, 0:2], statS[:, hm, :]), [hsmS_b], [hsmS_b])
            pg.act(mvS[:, :, 2:3], mvS[:, :, 1:2], AF.Ln, [hsmS_b], [hsmS_b], bias=eps_c[0:TD, 0:1])
            pg.act(mvS[:, :, 2:3], mvS[:, :, 2:3], AF.Exp, [hsmS_b], [hsmS_b], scale=-0.5)
            for hm in range(4):
                pg.ts("dve", mnfS[:, hm * 128:(hm + 1) * 128], hb[:, hm * 128:(hm + 1) * 128], mvS[:, hm, 0:1], mvS[:, hm, 2:3],
                      ALU.subtract, ALU.mult, [hbb, hsmS_b], [mnfS_b])
            pg.tt("pool", mnfS[:, :], mnfS[:, :], mnwS[0:TD, :], ALU.mult, [mnfS_b, sc], [mnfS_b])
            mb, mbb = mnbS.next()
            pg.tt("pool", mb[:, :], mnfS[:, :], ogS[:, b, :], ALU.mult, [mnfS_b, tb], [mbb])
            pbt, pbb = psS_b.next()
            for hm in range(4):
                pg.tr(pbt[:, hm * 8:hm * 8 + TD], mb[:, hm * 128:(hm + 1) * 128], ident_b[0:TD, 0:TD], [mbb, cB], [pbb])
            pg.cp("act", mixS[:, 4:8, cs], pbt[:, 0:32].rearrange("p (h t) -> p h t", h=4), [pbb], [mixS_b])

        if with_cache:
            ptI = sS("ptI", [128, NPG], I32); ptF = sS("ptF", [128, NPG]); pcI = sS("pcI", [128, 1], I32); pcF = sS("pcF", [128, 1])
            idxF = sS("idxF", [128, NPG]); idxI = sS("idxI", [128, NPG], I32); pgI = sS("pgI", [128, 1], I32)
            idx_b = pg.buf("idx")
            lfp = sS("lfp", [128, 1024]); lfc = sS("lfc", [128, 8, 128]); ltot = sS("ltot", [128, 8]); llat = sS("llat", [128, 8])
            Rpg = sS("Rpg", [128, 8, 128]); RkT = sS("RkT", [128, 8, 128]); LnL = sS("LnL", [128, 8]); LnS = sS("LnS", [8, 8])
            bnew = sS("bnew", [8, 8])
            R_b = pg.buf("Rb")
            KVp = Rot(pg, [sS(f"KVp{i}", [128, 1024], BF16) for i in range(5)], "KVp")
            KTs = Rot(pg, [sS(f"KTs{i}", [128, 4, 128], BF16) for i in range(3)], "KTs")
            sTs = Rot(pg, [sS(f"sTs{i}", [128, 64]) for i in range(3)], "sTs")
            PTs = Rot(pg, [sS(f"PTs{i}", [128, 64], BF16) for i in range(3)], "PTs")
            oS = sS("oS", [8, 512]); lrow = sS("lrow", [1, 64]); lq = sS("lq", [8, 8]); rlq = sS("rlq", [8, 8])
            foS = sS("foS", [8, 512], BF16)
            fin_b = pg.buf("fin")
            pg.op("pool", lambda e: e.iota(pcI[:, :], pattern=[[0, 1]], base=0, channel_multiplier=1), [], [idx_b])
            pg.cp("dve", pcF[:, :], pcI[:, :], [idx_b], [idx_b])
            clfv = clf_d.rearrange("(pg t) h -> pg (t h)", t=128)
            for b in range(NSQ):
                cs = slice(b * TD, (b + 1) * TD)
                tb = tokS_b[b]
                pg.group_begin()
                pg.dma("sp", ptI[:, :], pt_d[b:b + 1, :].partition_broadcast(128), [], [idx_b], sb=idx_b)
                pg.dma("sp", pgI[:, :], pt_d[b:b + 1, :].rearrange("o p -> p o"), [], [idx_b], sb=idx_b, allow_slow_non_contiguous=True)
                pg.group_end()
                pg.cp("dve", ptF[:, :], ptI[:, :], [idx_b], [idx_b])
                pg.ts("dve", idxF[:, :], ptF[:, :], 128.0, pcF[:, 0:1], ALU.mult, ALU.add, [idx_b], [idx_b])
                pg.cp("dve", idxI[:, :], idxF[:, :], [idx_b], [idx_b])
                pg.dmaf("pool", lambda e: e.indirect_dma_start(out=lfp[:, :], out_offset=None, in_=clfv,
                                                              in_offset=bass.IndirectOffsetOnAxis(ap=pgI[:, 0:1], axis=0)),
                        [idx_b], [R_b], R_b)
                lf3 = lfp[:, :].rearrange("p (t h) -> p h t", h=8)
                for h in range(8):
                    pg.op("dve", lambda e, h=h: e.tensor_tensor_scan(out=lfc[:, h, :], data0=one_c[:, 0:1].broadcast_to([128, 128]),
                                                                      data1=lf3[:, h, :], initial=0.0, op0=ALU.mult, op1=ALU.add),
                          [R_b, cB], [R_b])
                pg.cp("dve", ltot[:, :], lfc[:, :, 127], [R_b], [R_b])
                pgt, pgb = psS_g.next()
                pg.mm(pgt[:, 0:8], maskU[:, :], ltot[:, :], True, True, [R_b, sc], [pgb])
                pg.mm(pgt[:, 8:16], ones_f[0:TD, 0:128], lfnS[:, b, :], True, True, [lfn_bs[b], cB], [pgb])
                pg.mm(pgt[0:TD, 16:24], mask01[0:TD, 0:TD], lfnS[:, b, :], True, True, [lfn_bs[b], cB], [pgb])
                pg.tt("dve", llat[:, :], pgt[:, 0:8], ltot[:, :], ALU.add, [pgb, R_b], [R_b])
                pg.cp("dve", LnL[:, :], pgt[:, 8:16], [pgb], [R_b])
                pg.cp("dve", LnS[:, :], pgt[0:TD, 16:24], [pgb], [R_b])
                pg.tt("dve", Rpg[:, :, :], llat[:, :].unsqueeze(2).broadcast_to([128, 8, 128]), lfc[:, :, :], ALU.subtract, [R_b], [R_b])
                for h in range(8):
                    ptt, ptb = psS_T.next()
                    pg.tr(ptt[:, 0:128], Rpg[:, h, :], ident_f[:, :], [R_b, cB], [ptb])
                    pg.ts("dve", RkT[:, h, :], ptt[:, 0:128], LnL[:, h:h + 1], None, ALU.add, None, [ptb, R_b], [R_b])
                pg.tt("dve", bnew[:, :], LnL[0:TD, :], LnS[:, :], ALU.subtract, [R_b], [R_b])
                po, pob = psS_o.next()
                pl, plb = psS_l.next()
                npg_ = NPG if 'p4' not in DBG else 4
                st1 = {}
                st2 = {}

                def stage1(p_, b=b):
                    kvp, kpb = KVp.next()
                    pg.dmaf("pool", lambda e, kvp=kvp, p_=p_: e.indirect_dma_start(
                        out=kvp[:, :], out_offset=None, in_=ckv_d, in_offset=bass.IndirectOffsetOnAxis(ap=idxI[:, p_:p_ + 1], axis=0)),
                        [idx_b], [kpb], kpb)
                    pbt, pbb = psS_b.next()
                    for pp in range(4):
                        pg.tr(pbt[:, pp * 128:(pp + 1) * 128], kvp[:, pp * 128:(pp + 1) * 128], ident_b[:, :], [kpb, cB], [pbb])
                    kt_, ktb_ = KTs.next()
                    pg.cp("dve", kt_[:, :, :], pbt[:, 0:512].rearrange("p (a t) -> p a t", a=4), [pbb], [ktb_])
                    st1[p_] = (kvp, kpb, kt_, ktb_)

                def stage2(p_, b=b):
                    kvp, kpb, kt_, ktb_ = st1.pop(p_)
                    pss, psb = psS_s.next()
                    for pp in range(4):
                        pg.mm(pss[:, pp * 16:(pp + 1) * 16], kt_[:, pp, :], bdq[:, b, pp, :], True, True, [ktb_, bdq_b], [psb])
                    st_, stb_ = sTs.next()
                    pg.tt("dve", st_[:, :].rearrange("p (h q) -> p h q", h=8), pss[:, 0:64].rearrange("p (h q) -> p h q", h=8),
                          RkT[:, :, p_].unsqueeze(2).broadcast_to([128, 8, 8]), ALU.add, [psb, R_b], [stb_])
                    pt_, ptb_ = PTs.next()
                    pg.act(pt_[:, :], st_[:, :], AF.Exp, [stb_], [ptb_])
                    st2[p_] = (kvp, kpb, pt_, ptb_)

                def stage3(p_, po=po, pob=pob, pl=pl, plb=plb):
                    kvp, kpb, pt_, ptb_ = st2.pop(p_)
                    for h in range(8):
                        pg.mm(po[0:TD, h * 64:(h + 1) * 64], pt_[:, h * 8:(h + 1) * 8], kvp[:, 512 + h * 64:512 + (h + 1) * 64],
                              p_ == 0, False, [ptb_, kpb], [pob])
                    pg.mm(pl[0:1, 0:64], ones_b[:, 0:1], pt_[:, :], p_ == 0, False, [ptb_, sc], [plb])

                for t_ in range(npg_ + 2):
                    if t_ < npg_:
                        stage1(t_)
                    if 0 <= t_ - 1 < npg_:
                        stage2(t_ - 1)
                    if 0 <= t_ - 2 < npg_:
                        stage3(t_ - 2)
                pss, psb = psS_s.next()
                for pp in range(4):
                    pg.mm(pss[0:TD, pp * 16:(pp + 1) * 16], kTn[:, pp, cs], bdq[:, b, pp, :], True, True, [kTn_b, bdq_b], [psb])
                st_, stb_ = sTs.next()
                pg.tt("dve", st_[0:TD, :].rearrange("p (h q) -> p h q", h=8), pss[0:TD, 0:64].rearrange("p (h q) -> p h q", h=8),
                      bnew[:, :].unsqueeze(2).broadcast_to([TD, 8, 8]), ALU.add, [psb, R_b], [stb_])
                pg.act(st_[0:TD, :], st_[0:TD, :], AF.Exp, [stb_], [stb_])
                pt_, ptb_ = PTs.next()
                pg.tt("dve", pt_[0:TD, :].rearrange("p (h q) -> p h q", h=8), st_[0:TD, :].rearrange("p (h q) -> p h q", h=8),
                      mask01[0:TD, 0:TD].unsqueeze(1).broadcast_to([TD, 8, TD]), ALU.mult, [stb_, cB], [ptb_])
                for h in range(8):
                    pg.mm(po[0:TD, h * 64:(h + 1) * 64], pt_[0:TD, h * 8:(h + 1) * 8], vnB[:, b, h * 64:(h + 1) * 64], False, True,
                          [ptb_, tb], [pob])
                pg.mm(pl[0:1, 0:64], ones_b[0:TD, 0:1], pt_[0:TD, :], False, True, [ptb_, sc], [plb])
                pg.cp("act", oS[:, :], po[0:TD, :], [pob], [fin_b])
                pg.cp("dve", lrow[:, :], pl[0:1, 0:64], [plb], [fin_b])
                lr_h = lrow.tensor if hasattr(lrow, "tensor") else lrow
                pg.group_begin()
                for q_ in range(TD):
                    pg.dma("sp", lq[q_:q_ + 1, :], bass.AP(lr_h, lrow[:, :].offset + q_, [[lrow[:, :].ap[0][0], 1], [8, 8]]),
                           [fin_b], [fin_b], sb=fin_b, allow_slow_non_contiguous=True)
                pg.group_end()
                pg.op("dve", lambda e: e.reciprocal(rlq[:, :], lq[:, :]), [fin_b], [fin_b])
                pg.tt("dve", foS[:, :].rearrange("p (h d) -> p h d", h=8), oS[:, :].rearrange("p (h d) -> p h d", h=8),
                      rlq[:, :].unsqueeze(2).broadcast_to([TD, 8, 64]), ALU.mult, [fin_b], [fin_b])
                pbt, pbb = psS_b.next()
                for c4 in range(4):
                    pg.tr(pbt[:, c4 * 8:c4 * 8 + TD], foS[:, c4 * 128:(c4 + 1) * 128], ident_b[0:TD, 0:TD], [fin_b, cB], [pbb])
                pg.cp("act", mixS[:, 0:4, cs], pbt[:, 0:32].rearrange("p (h t) -> p h t", h=4), [pbb], [mixS_b])
        pg.dma("sp", mix_d[:, :, S:S + NT], mixS[:, :, :], [mixS_b], [mix_db], sb=mixS_b)
        nS = pg.emit()
        esS.close()

    esC = ExitStack()

    def sC(name, shape, dt=F32):
        return esC.enter_context(nc.sbuf_tensor(name, list(shape), dt))

    def pC(name, dt=F32, n=512):
        return esC.enter_context(nc.psum_tensor(name, [128, n], dt))

    TB = 256
    w1b = sC("w1b", [128, KC, DFF // 2], BF16)
    w1b_b = pg.buf("w1b")
    for c in range(KC):
        pg.dma("pool", w1b[:, c, :], w1v[:, c, 2048:4096], [], [w1b_b], sb=w1b_b)
    w2S = sC("w2S", [128, 32, D], BF16)
    for c in range(32):
        pg.dma("pool", w2S[:, c, :], w2v[:, c, :], [], [w2_b], sb=w2_b)
    lnp = sC("lnp", [128, 4, D])
    for ii, dd in enumerate((l1g_d, l1b_d, l2g_d, l2b_d)):
        pg.dma("sp", lnp[:, ii, :], dd[0:1, :].partition_broadcast(128), [], [cB], sb=cB)
    mixC = sC("mixC", [128, KC, TB], BF16)
    mixC_b = pg.buf("mixC")
    xr = Rot(pg, [sC("xr0", [128, D])], "xr")
    x1s = [sC(f"x1s{i}", [128, D]) for i in range(4)]
    x1s_b = [pg.buf("x1s") for i in range(4)]
    x1T = sC("x1T", [128, KC, TB], BF16)
    x1T_b = pg.buf("x1T")
    hidT = sC("hidT", [128, 32, TB], BF16)
    hidT_b = [pg.buf("hidT") for f in range(32)]
    rtmp = Rot(pg, [sC(f"rtmp{i}", [128, TB]) for i in range(1)], "rtmp")
    lstat = sC("lstat", [128, 2, 6])
    lmv = sC("lmv", [128, 4])
    lst_b = pg.buf("lstat")
    psC_mm = PsumPool(pg, [pC(f"psC_mm{i}") for i in range(4)])
    psC_T = PsumPool(pg, [pC(f"psC_T{i}") for i in range(2)])

    def ln_inplace(xa, T, gi, xb):
        for h2 in range(2):
            pg.op("dve", lambda e, h2=h2: e.bn_stats(lstat[0:T, h2, :], xa[:, h2 * 512:(h2 + 1) * 512]), [xb], [lst_b])
        pg.op("dve", lambda e: e.bn_aggr(lmv[0:T, 0:2], lstat[0:T, :, :]), [lst_b], [lst_b])
        pg.act(lmv[0:T, 2:3], lmv[0:T, 1:2], AF.Ln, [lst_b], [lst_b], bias=eps_c[0:T, 0:1])
        pg.act(lmv[0:T, 2:3], lmv[0:T, 2:3], AF.Exp, [lst_b], [lst_b], scale=-0.5)
        pg.ts("dve", xa, xa, lmv[0:T, 0:1], lmv[0:T, 2:3], ALU.subtract, ALU.mult, [xb, lst_b], [xb])
        pg.tt("pool", xa, xa, lnp[0:T, gi, :], ALU.mult, [xb, cB], [xb])
        pg.tt("pool", xa, xa, lnp[0:T, gi + 1, :], ALU.add, [xb, cB], [xb])

    blocks = []
    nblk = (S // TB) if 'j1' not in DBG else 2
    if 'C' in SKIP:
        nblk = 0
    for bi in range(nblk):
        blocks.append((x_d, y_o, bi * TB, bi * TB, 2, 128))
    if 'nosample' not in DBG:
        blocks.append((xs_d, ys_o, 0, S, 1, NSTOK))
    def phase1(bi, blk):
        (src_d, out_d, row0, col0, ntl, T) = blk
        W = ntl * T if T == 128 else T
        pg.dma("sp", mixC[:, :, 0:W], mix_d[:, :, col0:col0 + W], [mix_db], [mixC_b], sb=mixC_b)
        for tl in range(ntl):
            sl = slice(tl * T, (tl + 1) * T)
            xi = (bi % 2) * 2 + tl
            xt, xb = xr.next()
            pg.dma("sp", xt[0:T, :], src_d[row0 + tl * T:row0 + (tl + 1) * T, :], [], [xb], sb=xb)
            xa = x1s[xi][0:T, :]
            for n2 in range(2):
                pt, pb = psC_mm.next()
                for c in range(KC):
                    pg.mm(pt[0:T, :], mixC[:, c, sl], woS[:, c, n2 * 512:(n2 + 1) * 512], c == 0, c == KC - 1,
                          [mixC_b, wo_b], [pb])
                pg.stt("dve", xa[:, n2 * 512:(n2 + 1) * 512], xt[0:T, n2 * 512:(n2 + 1) * 512], ALPHA, pt[0:T, :],
                       ALU.mult, ALU.add, [xb, pb], [x1s_b[xi]])
            ln_inplace(xa, T, 0, x1s_b[xi])

    def phase2(bi, blk):
        (src_d, out_d, row0, col0, ntl, T) = blk
        W = ntl * T if T == 128 else T
        for tl in range(ntl):
            xi = (bi % 2) * 2 + tl
            xa = x1s[xi][0:T, :]
            for g in range(2):
                ptt, ptb = psC_T.next()
                for cc in range(4):
                    c = g * 4 + cc
                    pg.tr(ptt[:, cc * 128:cc * 128 + T], xa[:, c * 128:(c + 1) * 128], ident_f[0:T, 0:T], [x1s_b[xi], cB], [ptb])
                pg.cp(pg.ev(), x1T[:, g * 4:g * 4 + 4, tl * T:(tl + 1) * T],
                      ptt[:, :].rearrange("p (c t) -> p c t", c=4)[:, :, 0:T], [ptb], [x1T_b])
        for f in range(32):
            pt, pb = psC_mm.next()
            for c in range(KC):
                w1t = w1a if f < 16 else w1b
                fo = (f % 16) * 128
                pg.mm(pt[:, 0:W], w1t[:, c, fo:fo + 128], x1T[:, c, 0:W], c == 0, c == KC - 1,
                      [x1T_b, w1_b if f < 16 else w1b_b], [pb])
            rt, rtb = rtmp.next()
            pg.act(rt[:, 0:W], pt[:, 0:W], AF.Relu, [pb], [rtb])
            pg.tt("dve" if f % 2 == 0 else "pool", hidT[:, f, 0:W], rt[:, 0:W], rt[:, 0:W], ALU.mult, [rtb], [hidT_b[f]])

    def phase3(bi, blk):
        (src_d, out_d, row0, col0, ntl, T) = blk
        for tl in range(ntl):
            sl = slice(tl * T, (tl + 1) * T)
            xi = (bi % 2) * 2 + tl
            xa = x1s[xi][0:T, :]
            for n2 in range(2):
                pt, pb = psC_mm.next()
                for f in range(32):
                    pg.mm(pt[0:T, :], hidT[:, f, sl], w2S[:, f, n2 * 512:(n2 + 1) * 512], f == 0, f == 31,
                          [hidT_b[f], w2_b], [pb])
                pg.stt("dve", xa[:, n2 * 512:(n2 + 1) * 512], xa[:, n2 * 512:(n2 + 1) * 512], ALPHA, pt[0:T, :],
                       ALU.mult, ALU.add, [x1s_b[xi], pb], [x1s_b[xi]])
            ln_inplace(xa, T, 2, x1s_b[xi])
            pg.dma("sp", out_d[row0 + tl * T:row0 + (tl + 1) * T, :], xa, [x1s_b[xi]], [], sb=x1s_b[xi])

    for bi, blk in enumerate(blocks):
        phase1(bi, blk)
        if bi > 0:
            phase3(bi - 1, blocks[bi - 1])
        phase2(bi, blk)
    if blocks:
        phase3(len(blocks) - 1, blocks[-1])
    nC = pg.emit()
    esC.close()
    return nc, es, pg, dict(nA=nA, nB=nB, nC=nC)


_CACHE = {}


def kernel(**inputs):
    n = 8
    nc, es, pg, info = build_program(debug=False, with_cache=True)
    I = {k: np.asarray(v) for k, v in inputs.items()}
    in_maps = []
    ckv = np.concatenate([I["cache_k"].reshape(5120 * 128, 512), I["cache_v"].reshape(5120 * 128, 512)], axis=1)
    clf = np.ascontiguousarray(I["cache_logf"]).reshape(5120 * 128, 8)
    for c in range(n):
        sl = slice(c * NSQ, (c + 1) * NSQ)
        in_maps.append({
            "x": np.ascontiguousarray(I["x_prompt"][c]),
            "xs": np.ascontiguousarray(I["x_sample"][sl].reshape(NSTOK, D)),
            "state_C": np.ascontiguousarray(I["state_C"][0, sl]),
            "state_n": np.ascontiguousarray(I["state_n"][0, sl]),
            "state_m": np.ascontiguousarray(I["state_m"][0, sl]),
            "state_conv": np.ascontiguousarray(I["state_conv"][0, sl]),
            "page_table": np.ascontiguousarray(I["page_table"][sl]),
            "cache_kv": ckv, "cache_logf": clf,
            "w_in": I["w_in"][0], "b_fox_f": I["b_fox_f"], "b_ig": I["b_ig"], "b_fg": I["b_fg"],
            "b_og": I["b_og"], "conv_w": I["conv_w"][0], "conv_b": I["conv_b"],
            "mlstm_norm_w": I["mlstm_norm_w"], "w_o": I["w_o"][0], "ln1_g": I["ln1_g"], "ln1_b": I["ln1_b"],
            "w1": I["w1"][0], "w2": I["w2"][0], "ln2_g": I["ln2_g"], "ln2_b": I["ln2_b"],
        })
    res = run_bass_kernel_spmd(nc, in_maps, core_ids=list(range(n)))
    R = res.results

    def cat(name, shape):
        return np.stack([R[c][name] for c in range(n)], 0).reshape(shape).astype(np.float32)

    y = cat("o_y", (8, S, D))
    ys = cat("o_ys", (32, TD, D))
    k = cat("o_k", (1, 8, S, 8, 64))
    v = cat("o_v", (1, 8, S, 8, 64))
    lf = cat("o_logf", (1, 8, S, 8))
    C = cat("o_C", (1, 8, 4, 128, 128))
    nn = cat("o_n", (1, 8, 4, 128))
    m = cat("o_m", (1, 8, 4))
    conv = cat("o_conv", (1, 8, 3, 1024))
    ks = cat("o_ks", (1, 32, TD, 8, 64))
    vs = cat("o_vs", (1, 32, TD, 8, 64))
    lfs = cat("o_logfs", (1, 32, TD, 8))
    Cs = cat("o_Cs", (1, 32, 4, 128, 128))
    ns = cat("o_ns", (1, 32, 4, 128))
    ms = cat("o_ms", (1, 32, 4))
    convs = cat("o_convs", (1, 32, 3, 1024))
    return (y, ys, k, v, lf, C, nn, m, conv, ks, vs, lfs, Cs, ns, ms, convs)
```

```python
import os
import numpy as np
DBG = os.environ.get('KDBG', '')
SKIP = os.environ.get('KSKIP', '')
SAME_ENG_FREE = tuple(x for x in os.environ.get('KSEF', '').split(',') if x)
from contextlib import ExitStack
import concourse.bass as bass
import concourse.mybir as mybir
from concourse.bass_utils import run_bass_kernel_spmd

F32 = mybir.dt.float32
BF16 = mybir.dt.bfloat16
I32 = mybir.dt.int32
AF = mybir.ActivationFunctionType
ALU = mybir.AluOpType
AX = mybir.AxisListType

D = 1024
S = 4096
TS = 512
NST = S // TS
KC = 8
FQ, FK, FV, FF, MQK, MV, MI, MF, MO, INC = 0, 512, 1024, 1536, 1544, 2568, 3080, 3084, 3088, 3600
DFF = 4096
ALPHA = 2.0 ** 0.25
EPS = 1e-5
NEG = -30000.0
NSQ = 4
TD = 8
NSTOK = NSQ * TD
NPG = 128
NTOT = S + NSTOK
KSCALE = 128.0 ** -0.5
QSCALE = 64.0 ** -0.5


class Buf:
    __slots__ = ("name", "w", "r", "dsem", "dcnt")

    def __init__(self, name):
        self.name = name
        self.w = None
        self.r = []
        self.dsem = None
        self.dcnt = 0


class Op:
    __slots__ = ("eng", "fn", "deps", "dma", "tok", "inc", "done")


class Prog:
    def __init__(self, nc, es):
        self.nc = nc
        self.es = es
        self.E = {"pe": nc.tensor, "act": nc.scalar, "dve": nc.vector, "pool": nc.gpsimd, "sp": nc.sync}
        self.ops = {k: [] for k in self.E}
        self.sem = {k: es.enter_context(nc.semaphore("sem_" + k)) for k in self.E}
        self.cnt = {k: 0 for k in self.E}
        self.dbufs = []
        self.nb = 0
        self.flip = 0
        self.grp = None

    def buf(self, name="b"):
        self.nb += 1
        return Buf(f"{name}_{self.nb}")

    def bufs(self, n, name="b"):
        return [self.buf(name) for _ in range(n)]

    def _add(self, o, r, w):
        deps = []

        def add(d, kind):
            if d is None or d.done:
                return
            if d.dma and o.dma and d.tok[0] is o.tok[0]:
                return
            if (not d.dma) and d.eng == o.eng and not o.dma:
                if o.eng == "pe":
                    return
                if o.eng in SAME_ENG_FREE:
                    return
            if d not in deps:
                deps.append(d)
                d.inc = True

        for b in r:
            add(b.w, "raw")
        for b in w:
            add(b.w, "waw")
            for x in b.r:
                add(x, "war")
        o.deps = deps
        o.done = False
        for b in r:
            b.r.append(o)
        for b in w:
            b.w = o
            b.r = []
        self.ops[o.eng].append(o)

    def op(self, eng, fn, r=(), w=()):
        o = Op()
        o.eng = eng
        o.dma = False
        o.inc = False
        o.fn = fn
        o.tok = None
        self._add(o, r, w)
        return o

    def dmaf(self, eng, mk, r, w, sb):
        o = Op()
        o.eng = eng
        o.dma = True
        o.inc = False
        if sb.dsem is None:
            sb.dsem = self.es.enter_context(self.nc.semaphore("d_" + sb.name))
            self.dbufs.append(sb)
        sb.dcnt += 16
        sem = sb.dsem
        o.tok = (sem, sb.dcnt)
        o.fn = lambda e: mk(e).then_inc(sem, 16)
        self._add(o, r, w)
        if self.grp is not None:
            self.grp.append((o, sb))
        return o

    def group_begin(self):
        self.grp = []

    def group_end(self):
        for (o, sb) in self.grp:
            o.tok = (sb.dsem, sb.dcnt)
        self.grp = None

    def dma(self, eng, out, in_, r=(), w=(), sb=None, **kw):
        return self.dmaf(eng, lambda e: e.dma_start(out=out, in_=in_, **kw), r, w, sb)

    def emit(self):
        lasts = []
        for k, lst in self.ops.items():
            for o in reversed(lst):
                if not o.dma and o.fn is not None:
                    o.inc = True
                    lasts.append(o)
                    break
        dtoks = []
        for b in self.dbufs:
            t = Op()
            t.dma = True
            t.tok = (b.dsem, b.dcnt)
            t.done = False
            dtoks.append(t)
        for k in self.E:
            o = Op()
            o.eng = k
            o.dma = False
            o.inc = False
            o.fn = None
            o.tok = None
            o.done = False
            o.deps = [x for x in lasts if x.eng != k] + dtoks
            self.ops[k].append(o)
        for k, lst in self.ops.items():
            for o in lst:
                if not o.dma and o.inc and o.fn is not None:
                    self.cnt[k] += 1
                    o.tok = (self.sem[k], self.cnt[k])
        prog = self
        with self.nc.Block() as block:
            def run(k):
                def f(e):
                    waited = {}
                    for o in prog.ops[k]:
                        for d in o.deps:
                            sem, val = d.tok
                            if waited.get(id(sem), 0) < val:
                                e.wait_ge(sem, val)
                                waited[id(sem)] = val
                        if o.fn is not None:
                            ins = o.fn(e)
                            if (not o.dma) and o.inc:
                                ins.then_inc(prog.sem[k], 1)
                return f
            block.tensor(run("pe"))
            block.scalar(run("act"))
            block.vector(run("dve"))
            block.gpsimd(run("pool"))
            block.sync(run("sp"))
        n = 0
        for k in self.ops:
            for o in self.ops[k]:
                o.done = True
                o.fn = None
                o.deps = None
            n += len(self.ops[k])
            self.ops[k] = []
        return n

    def mm(self, out, lhsT, rhs, start, stop, r, w):
        self.op("pe", lambda e: e.matmul(out, lhsT, rhs, start=start, stop=stop), r, w)

    def tr(self, out, in_, ident, r, w):
        self.op("pe", lambda e: e.transpose(out, in_, ident), r, w)

    def ev(self):
        self.flip ^= 1
        return "act" if self.flip else "dve"

    def cp(self, eng, out, in_, r, w):
        if eng == "act":
            self.op("act", lambda e: e.copy(out, in_), r, w)
        else:
            self.op(eng, lambda e: e.tensor_copy(out=out, in_=in_), r, w)

    def act(self, out, in_, func, r, w, bias=None, scale=None):
        kw = {}
        if bias is not None:
            kw["bias"] = bias
        if scale is not None:
            kw["scale"] = scale
        self.op("act", lambda e: e.activation(out, in_, func, **kw), r, w)

    def tt(self, eng, out, in0, in1, op, r, w):
        self.op(eng, lambda e: e.tensor_tensor(out=out, in0=in0, in1=in1, op=op), r, w)

    def ts(self, eng, out, in0, s1, s2, op0, op1, r, w):
        if s2 is None:
            self.op(eng, lambda e: e.tensor_scalar(out=out, in0=in0, scalar1=s1, scalar2=None, op0=op0), r, w)
        else:
            self.op(eng, lambda e: e.tensor_scalar(out=out, in0=in0, scalar1=s1, scalar2=s2, op0=op0, op1=op1), r, w)

    def stt(self, eng, out, in0, scalar, in1, op0, op1, r, w):
        self.op(eng, lambda e: e.scalar_tensor_tensor(out=out, in0=in0, scalar=scalar, in1=in1, op0=op0, op1=op1), r, w)

    def memset(self, eng, ap, val, w):
        self.op(eng, lambda e: e.memset(ap, val), (), w)


class PsumPool:
    def __init__(self, pg, tiles):
        self.t = tiles
        self.b = [pg.buf("ps") for _ in tiles]
        self.i = 0

    def next(self):
        i = self.i
        self.i = (i + 1) % len(self.t)
        return self.t[i], self.b[i]


class Rot:
    def __init__(self, pg, tiles, name="rot"):
        self.t = tiles
        self.b = [pg.buf(name) for _ in tiles]
        self.i = 0

    def next(self):
        i = self.i
        self.i = (i + 1) % len(self.t)
        return self.t[i], self.b[i]


def build_program(debug=False, with_cache=True):
    nc = bass.Bass("TRN2", target_bir_lowering=False)
    es = ExitStack()

    def din(name, shape, dt=F32):
        return nc.dram_tensor(name, list(shape), dt, kind="ExternalInput").ap()

    def dout(name, shape, dt=F32):
        return nc.dram_tensor(name, list(shape), dt, kind="ExternalOutput").ap()

    x_d = din("x", [S, D])
    xs_d = din("xs", [NSTOK, D])
    if with_cache:
        ckv_d = din("cache_kv", [5120 * 128, 1024])
        clf_d = din("cache_logf", [5120 * 128, 8])
    sC_d = din("state_C", [NSQ, 4, 128, 128])
    sn_d = din("state_n", [NSQ, 4, 128])
    sm_d = din("state_m", [NSQ, 4])
    sconv_d = din("state_conv", [NSQ, 3, 1024])
    pt_d = din("page_table", [NSQ, NPG], I32)
    win_d = din("w_in", [D, INC])
    bff_d = din("b_fox_f", [1, 8])
    big_d = din("b_ig", [1, 4])
    bfg_d = din("b_fg", [1, 4])
    bog_d = din("b_og", [1, 512])
    cw_d = din("conv_w", [4, 1024])
    cb_d = din("conv_b", [1, 1024])
    mnw_d = din("mlstm_norm_w", [1, 512])
    wo_d = din("w_o", [D, D])
    l1g_d = din("ln1_g", [1, D])
    l1b_d = din("ln1_b", [1, D])
    w1_d = din("w1", [D, DFF])
    w2_d = din("w2", [DFF, D])
    l2g_d = din("ln2_g", [1, D])
    l2b_d = din("ln2_b", [1, D])

    y_o = dout("o_y", [S, D])
    ys_o = dout("o_ys", [NSTOK, D])
    k_o = dout("o_k", [S, 512])
    v_o = dout("o_v", [S, 512])
    lf_o = dout("o_logf", [S, 8])
    C_o = dout("o_C", [4, 128, 128])
    n_o = dout("o_n", [4, 128])
    m_o = dout("o_m", [1, 4])
    conv_o = dout("o_conv", [3, 1024])
    ksm_o = dout("o_ks", [NSTOK, 512])
    vsm_o = dout("o_vs", [NSTOK, 512])
    lfs_o = dout("o_logfs", [NSTOK, 8])
    Cs_o = dout("o_Cs", [NSQ, 4, 128, 128])
    ns_o = dout("o_ns", [NSQ, 4, 128])
    ms_o = dout("o_ms", [NSQ, 4])
    convs_o = dout("o_convs", [NSQ, 3, 1024])

    mix_d = nc.dram_tensor("mix_scratch", [128, 8, NTOT], BF16, kind=("ExternalOutput" if debug else "Internal")).ap()
    mix_db = None

    pg = Prog(nc, es)
    mix_db = pg.buf("mixd")

    def sb(name, shape, dt=F32, stack=None):
        return (stack or es).enter_context(nc.sbuf_tensor(name, list(shape), dt))

    ident_f = sb("ident_f", [128, 128])
    ident_b = sb("ident_b", [128, 128], BF16)
    mask01 = sb("mask01", [128, 128])
    ones_f = sb("ones_f", [128, 128])
    cB = pg.buf("const")

    def mk_consts():
        pg.memset("pool", ident_f[:], 0.0, [cB])
        pg.op("pool", lambda e: e.affine_select(out=ident_f[:], in_=ident_f[:], pattern=[[-1, 128]],
                                                 compare_op=ALU.not_equal, fill=1.0, base=0, channel_multiplier=1),
              [cB], [cB])
        pg.cp("pool", ident_b[:], ident_f[:], [cB], [cB])
        pg.memset("pool", ones_f[:], 1.0, [cB])
        pg.memset("pool", mask01[:], 1.0, [cB])
        pg.op("pool", lambda e: e.affine_select(out=mask01[:], in_=mask01[:], pattern=[[1, 128]],
                                                 compare_op=ALU.is_ge, fill=0.0, base=0, channel_multiplier=-1),
              [cB], [cB])

    mk_consts()


    def load_xT(src_d, row0, T, col0, xs_rot, psT, xT, xTb):
        xt, xb = xs_rot.next()
        pg.dma("sp", xt[0:T, :], src_d[row0:row0 + T, :], [], [xb], sb=xb)
        for g in range(2):
            pt, pb = psT.next()
            for cc in range(4):
                c = g * 4 + cc
                pg.tr(pt[:, cc * 128:cc * 128 + T], xt[0:T, c * 128:(c + 1) * 128], ident_f[0:T, 0:T], [xb, cB], [pb])
            pg.cp(pg.ev(), xT[:, g * 4:g * 4 + 4, col0:col0 + T],
                  pt[:, :].rearrange("p (c t) -> p c t", c=4)[:, :, 0:T], [pb], [xTb])
        return xt, xb

    LSL = int(os.environ.get("LSL", "9"))

    def log_sigmoid(out, src, a, b, tb, r, w):
        if LSL >= 1:
            pg.stt("dve", a, src, -1.0, src, ALU.mult, ALU.max, r, [tb])
        if LSL >= 2:
            pg.act(a, a, AF.Exp, [tb], [tb], scale=-1.0)
        if LSL >= 3:
            pg.act(a, a, AF.Ln, [tb], [tb], bias=one_c[0:a.shape[0], 0:1])
        if LSL >= 4:
            pg.ts("dve", b, src, 0.0, None, ALU.min, None, r, [tb])
        if LSL >= 5:
            pg.tt("dve", out, b, a, ALU.subtract, [tb], w)

    one_c = sb("one_c", [128, 1])
    eps_c = sb("eps_c", [128, 1])
    pg.memset("pool", one_c[:], 1.0, [cB])
    pg.memset("pool", eps_c[:], EPS, [cB])

    esA = ExitStack()

    def sA(name, shape, dt=F32):
        return esA.enter_context(nc.sbuf_tensor(name, list(shape), dt))

    def pA(name, dt=F32, n=512):
        return esA.enter_context(nc.psum_tensor(name, [128, n], dt))

    winA = sA("winA", [128, KC, 1544], BF16)
    winA_b = pg.buf("winA")
    wv = win_d.rearrange("(c p) n -> p c n", p=128)
    for c in range(KC):
        pg.dma("pool", winA[:, c, :], wv[:, c, 0:1544], [], [winA_b], sb=winA_b)
    ka = [sA(f"ka{h}", [70, S], BF16) for h in range(8)]
    ka_b = [[pg.buf("ka") for j in range(NST)] for h in range(8)]
    qa = [sA(f"qa{h}", [70, TS], BF16) for h in range(8)]
    qa_b = [pg.buf("qa") for h in range(8)]
    Vst = sA("Vst", [128, 32, 8, 128], BF16)
    Vst_b = [pg.buf("V") for t in range(32)]
    maskb = sA("maskb", [128, 4, 512], BF16)
    xsA = Rot(pg, [sA(f"xsA{i}", [128, D]) for i in range(1)], "xsA")
    xT_t = [sA(f"xTA{i}", [128, KC, TS], BF16) for i in range(1)] * 2
    xT_bs = [pg.buf("xT")] * 2
    stg = Rot(pg, [sA(f"stgA{i}", [128, 512]) for i in range(2)], "stgA")
    stgf = Rot(pg, [sA(f"stgfA{i}", [128, 8]) for i in range(2)], "stgfA")
    tmpA = sA("tmpA", [128, 512])
    tmpB = sA("tmpB", [128, 512])
    maskf = tmpA
    tmp_b = pg.buf("tmpA")
    tmpC = tmpA[8:16, :] if False else sA("tmpC", [8, 512])
    Lt = sA("Lt", [8, TS])
    Lcar = sA("Lcar", [8, 1])
    LP = sA("LP", [8, 3, TS], BF16)
    LN = sA("LN", [8, 3, TS], BF16)
    Lres = sA("Lres", [8, TS])
    L_b = pg.buf("L")
    semQA = pg.buf("semQA")
    semKA = pg.buf("semKA")
    bff_t = sA("bff_t", [128, 8])
    bff_c = sA("bff_c", [8, 1])
    PT = Rot(pg, [sA(f"PT{i}", [128, TS], BF16) for i in range(2)], "PT")
    rl = sA("rl", [64, TS])
    rl_b = pg.buf("rl")
    foxT = Rot(pg, [sA(f"foxT{i}", [128, 4, TS], BF16) for i in range(1)], "foxT")
    ps_mm = PsumPool(pg, [pA(f"psA_mm{i}") for i in range(2)])
    ps_T = PsumPool(pg, [pA(f"psA_T{i}") for i in range(1)])
    ps_s = PsumPool(pg, [pA(f"psA_s{i}") for i in range(3)])
    ps_o = PsumPool(pg, [pA(f"psA_o{i}") for i in range(2)])

    pg.dma("sp", bff_t[:, :], bff_d[0:1, :].partition_broadcast(128), [], [cB], sb=cB)
    pg.dma("sp", bff_c[:, :], bff_d.rearrange("o h -> h o"), [], [cB], sb=cB, allow_slow_non_contiguous=True)
    for h in range(8):
        pg.memset("pool", ka[h][64:70, :], 1.0, [ka_b[h][j] for j in range(NST)])
        pg.memset("pool", qa[h][64:70, :], 1.0, [qa_b[h]])
    pg.memset("pool", Vst[:, :, :, 64:128], 1.0, Vst_b)
    pg.memset("pool", Lcar[:], 0.0, [L_b])
    for r_ in range(4):
        pg.memset("pool", maskf[:], 0.0, [tmp_b])
        pg.op("pool", lambda e, r_=r_: e.affine_select(out=maskf[:], in_=maskf[:], pattern=[[1, 512]],
                                                        compare_op=ALU.is_ge, fill=NEG, base=-128 * r_,
                                                        channel_multiplier=-1), [tmp_b], [tmp_b])
        pg.cp("pool", maskb[:, r_, :], maskf[:], [tmp_b], [cB])

    def vaug(t, h):
        return Vst[:, t, h, :]

    for j in range((NST if 'j1' not in DBG else (0 if 'noloop' in DBG else 1)) if 'A' not in SKIP else 0):
        xT = xT_t[j % 2]
        xTb = xT_bs[j % 2]
        for tt in range(4):
            load_xT(x_d, j * TS + tt * 128, 128, tt * 128, xsA, ps_T, xT, xTb)
        pt, pb = ps_mm.next()
        for c in range(KC):
            pg.mm(pt[0:8, :], winA[:, c, FF:FF + 8], xT[:, c, :], c == 0, c == KC - 1, [xTb, winA_b], [pb])
        pg.ts("dve", tmpB[0:8, :], pt[0:8, :], bff_c[0:8, 0:1], None, ALU.add, None, [pb, cB], [tmp_b])
        log_sigmoid(Lres[0:8, :], tmpB[0:8, :], tmpA[0:8, :], tmpC[0:8, :], tmp_b, [tmp_b], [L_b])
        pg.op("dve", lambda e: e.tensor_tensor_scan(out=Lt[0:8, :], data0=one_c[0:8, 0:1].broadcast_to([8, TS]),
                                                    data1=Lres[0:8, :], initial=Lcar[0:8, 0:1], op0=ALU.mult, op1=ALU.add),
              [L_b, cB], [L_b])
        pg.cp("dve", Lcar[0:8, 0:1], Lt[0:8, TS - 1:TS], [L_b], [L_b])
        pg.cp("dve", LP[0:8, 0, :], Lt[0:8, :], [L_b], [L_b])
        pg.tt("dve", Lres[0:8, :], Lt[0:8, :], LP[0:8, 0, :], ALU.subtract, [L_b], [L_b])
        pg.cp("dve", LP[0:8, 1, :], Lres[0:8, :], [L_b], [L_b])
        pg.tt("dve", Lres[0:8, :], Lres[0:8, :], LP[0:8, 1, :], ALU.subtract, [L_b], [L_b])
        pg.cp("dve", LP[0:8, 2, :], Lres[0:8, :], [L_b], [L_b])
        pg.ts("dve", LN[0:8, :, :], LP[0:8, :, :], -1.0, None, ALU.mult, None, [L_b], [L_b])
        pg.group_begin()
        for h in range(0 if 'noL' not in DBG else 99, 8):
            pg.dma("sp", qa[h][64:67, :], LP[h:h + 1, :, :], [L_b], [qa_b[h]], sb=semQA)
            pg.dma("sp", ka[h][67:70, j * TS:(j + 1) * TS], LN[h:h + 1, :, :], [L_b], [ka_b[h][j]], sb=semQA)
        pg.group_end()
        for tt in range(4 if 'notok' not in DBG else 0):
            t = j * 4 + tt
            for (c0, kind) in ((FK, "k"), (FV, "v")):
                pt, pb = ps_mm.next()
                for c in range(KC):
                    pg.mm(pt[:, :], xT[:, c, tt * 128:(tt + 1) * 128], winA[:, c, c0:c0 + 512], c == 0, c == KC - 1,
                          [xTb, winA_b], [pb])
                st, stb = stg.next()
                pg.cp("act", st[:, :], pt[:, :], [pb], [stb])
                od = k_o if kind == "k" else v_o
                if "nodma" not in DBG:
                    pg.dma("sp", od[t * 128:(t + 1) * 128, :], st[:, :], [stb], [], sb=stb)
                if kind == "v" and "nov" not in DBG:
                    if True:
                        pg.cp("dve", Vst[:, t, :, 0:64], st[:, :].rearrange("p (h d) -> p h d", h=8), [stb], [Vst_b[t]])
                    else:
                        pg.cp("act" if "vact" in DBG else "dve", Vst[:, t, :, 0:64], pt[:, :].rearrange("p (h d) -> p h d", h=8), [pb], [Vst_b[t]])
            if 'skipf' in DBG:
                continue
            pt, pb = ps_mm.next()
            for c in range(KC):
                pg.mm(pt[:, 0:8], xT[:, c, tt * 128:(tt + 1) * 128], winA[:, c, FF:FF + 8], c == 0, c == KC - 1,
                      [xTb, winA_b], [pb])
            sf, sfb = stgf.next()
            pg.tt("dve", tmpA[:, 0:8], pt[:, 0:8], bff_t[:, :], ALU.add, [pb, cB], [tmp_b])
            log_sigmoid(sf[:, :], tmpA[:, 0:8], tmpA[:, 8:16], tmpA[:, 16:24], tmp_b, [tmp_b], [sfb])
            if "nolfdma" not in DBG:
                pg.dma("sp", lf_o[t * 128:(t + 1) * 128, :], sf[:, :], [sfb], [], sb=sfb)
        if 'nofm' in DBG:
            continue
        for pp in range(4):
            pt, pb = ps_mm.next()
            for c in range(KC):
                pg.mm(pt[:, :], winA[:, c, FQ + pp * 128:FQ + (pp + 1) * 128], xT[:, c, :], c == 0, c == KC - 1,
                      [xTb, winA_b], [pb])
            pg.op("act", lambda e, pt=pt, pp=pp: e.mul(qa[2 * pp][0:64, :], pt[0:64, :], QSCALE), [pb], [qa_b[2 * pp]])
            pg.ts("dve", qa[2 * pp + 1][0:64, :], pt[64:128, :], QSCALE, None, ALU.mult, None, [pb], [qa_b[2 * pp + 1]])
            pt, pb = ps_mm.next()
            for c in range(KC):
                pg.mm(pt[:, :], winA[:, c, FK + pp * 128:FK + (pp + 1) * 128], xT[:, c, :], c == 0, c == KC - 1,
                      [xTb, winA_b], [pb])
            pg.cp("act", ka[2 * pp][0:64, j * TS:(j + 1) * TS], pt[0:64, :], [pb], [ka_b[2 * pp][j]])
            pg.cp("dve", ka[2 * pp + 1][0:64, j * TS:(j + 1) * TS], pt[64:128, :], [pb], [ka_b[2 * pp + 1][j]])
        fx, fxb = foxT.next()
        for h in range(0 if 'noattn' not in DBG else 99, 8):
            po, pob = ps_o.next()
            nk = 4 * j + 4
            pend = []

            def do_pv(i, ptile, ptb, po=po, pob=pob, h=h, nk=nk):
                pg.mm(po[:, :], vaug(i, h), ptile[:, :], i == 0, i == nk - 1, [Vst_b[i], ptb], [pob])

            for i in range(nk):
                r_ = i - 4 * j
                pss, psb = ps_s.next()
                pg.mm(pss[:, :], ka[h][0:70, i * 128:(i + 1) * 128], qa[h][0:70, :], True, r_ < 0,
                      [ka_b[h][i // 4], qa_b[h]], [psb])
                if r_ >= 0:
                    pg.mm(pss[:, :], ident_b[:, :], maskb[:, r_, :], False, True, [cB], [psb])
                ptile, ptb = PT.next()
                pg.act(ptile[:, :], pss[:, :], AF.Exp, [psb], [ptb])
                pend.append((i, ptile, ptb))
                if len(pend) > 1:
                    do_pv(*pend.pop(0))
            while pend:
                do_pv(*pend.pop(0))
            pg.op("dve", lambda e, po=po: e.reciprocal(rl[0:64, :], po[64:128, :]), [pob], [rl_b])
            pg.tt("dve", fx[(h % 2) * 64:(h % 2) * 64 + 64, h // 2, :], po[0:64, :], rl[0:64, :], ALU.mult,
                  [pob, rl_b], [fxb])
        pg.dma("sp", mix_d[:, 0:4, j * TS:(j + 1) * TS], fx[:, :, :], [fxb], [mix_db], sb=fxb)
    nA = pg.emit()
    esA.close()

    woS = sb("woS", [128, KC, D], BF16)
    w1a = sb("w1a", [128, KC, DFF // 2], BF16)
    wo_b, w1_b, w2_b = pg.buf("wo"), pg.buf("w1"), pg.buf("w2")
    wov = wo_d.rearrange("(c p) n -> p c n", p=128)
    w1v = w1_d.rearrange("(c p) n -> p c n", p=128)
    w2v = w2_d.rearrange("(c p) n -> p c n", p=128)
    esB = ExitStack()

    def sB(name, shape, dt=F32):
        return esB.enter_context(nc.sbuf_tensor(name, list(shape), dt))

    def pB(name, dt=F32, n=512):
        return esB.enter_context(nc.psum_tensor(name, [128, n], dt))

    NB = INC - MQK
    winB = sB("winB", [128, KC, NB], BF16)
    winB_b = pg.buf("winB")
    for c in range(KC):
        pg.dma("pool", winB[:, c, 0:1024], wv[:, c, MQK:MQK + 1024], [], [winB_b], sb=winB_b)
        pg.dma("pool", winB[:, c, 1024:NB], wv[:, c, MQK + 1024:INC], [], [winB_b], sb=winB_b)
    for c in range(KC):
        pg.dma("pool", woS[:, c, :], wov[:, c, :], [], [wo_b], sb=wo_b)
    for c in range(KC):
        pg.dma("pool", w1a[:, c, :], w1v[:, c, 0:2048], [], [w1_b], sb=w1_b)
    cQK, cV, cI, cF, cO = 0, MV - MQK, MI - MQK, MF - MQK, MO - MQK
    xsB = Rot(pg, [sB(f"xsB{i}", [128, D]) for i in range(1)], "xsB")
    xTB = sB("xTB", [128, KC, TS], BF16)
    xTB_b = pg.buf("xTB")
    zc = [sB(f"zc{b}", [128, 3 + TS]) for b in range(8)]
    zc_b = [pg.buf("zc") for b in range(8)]
    qkT = [sB(f"qkT{b}", [128, TS], BF16) for b in range(8)]
    qkT_b = [pg.buf("qkT") for b in range(8)]
    cacc = sB("cacc", [128, TS])
    csig = sB("csig", [128, TS])
    cacc_b = pg.buf("cacc")
    cwT = sB("cwT", [128, 8, 4])
    cbT = sB("cbT", [128, 8])
    big_c = sB("big_c", [4, 1])
    bfg_c = sB("bfg_c", [4, 1])
    bog_t = sB("bog_t", [128, 512])
    mnw_t = sB("mnw_t", [128, 512])
    gB = pg.buf("gates")
    semOB = pg.buf("semOB")
    g_ig = sB("g_ig", [4, TS])
    g_lf = sB("g_lf", [4, TS])
    g_B = sB("g_B", [4, TS])
    g_u = sB("g_u", [4, TS])
    g_M = sB("g_M", [4, TS])
    g_t1 = sB("g_t1", [4, TS])
    g_t2 = sB("g_t2", [4, TS])
    g_wk = sB("g_wk", [4, TS])
    g_fl = sB("g_fl", [4, TS])
    g_car = sB("g_car", [4, 4])
    g_Rc = sB("g_Rc", [4, 4])
    g_Rp = sB("g_Rp", [4, 4])
    g_g = sB("g_g", [4, 4])
    g_gd = sB("g_gd", [4, 16])
    g_m = sB("g_m", [4, 1])
    tsc = sB("tsc", [128, 4, 8])
    tsc_b = pg.buf("tsc")
    gb = sB("gb", [128, 16])
    gb_b = pg.buf("gb")
    vaugs = Rot(pg, [sB(f"vaug{i}", [128, 4, 130], BF16) for i in range(2)], "vaug")
    ogs = Rot(pg, [sB(f"og{i}", [128, 512]) for i in range(1)], "og")
    ogtmp = sB("ogtmp", [128, 512])
    ogtmp_b = pg.buf("ogtmp")
    ktoks = Rot(pg, [sB(f"ktok{i}", [128, 4, 128], BF16) for i in range(2)], "ktok")
    ATs = Rot(pg, [sB(f"AT{i}", [128, 128], BF16) for i in range(2)], "AT")
    qgs = Rot(pg, [sB(f"qg{i}", [128, 128], BF16) for i in range(2)], "qg")
    numS = Rot(pg, [sB(f"numS{i}", [128, 130]) for i in range(2)], "numS")
    Cf = [sB(f"Cf{h}", [128, 130]) for h in range(4)]
    Cb = [sB(f"Cb{h}", [128, 130], BF16) for h in range(4)]
    Cf_b = [pg.buf("Cf") for h in range(4)]
    Cb_b = [pg.buf("Cb") for h in range(4)]
    hbufs = Rot(pg, [sB(f"hbuf{i}", [128, 512]) for i in range(2)], "hbuf")
    hsm = sB("hsm", [128, 8])
    hsm_b = pg.buf("hsm")
    stats = sB("stats", [128, 4, 6])
    mvs = sB("mvs", [128, 4, 4])
    mnb = Rot(pg, [sB(f"mnb{i}", [128, 512], BF16) for i in range(2)], "mnb")
    mnf = sB("mnf", [128, 512])
    mnf_b = pg.buf("mnf")
    mnT = Rot(pg, [sB(f"mnT{i}", [128, 4, TS], BF16) for i in range(2)], "mnT")
    psB_mm = PsumPool(pg, [pB(f"psB_mm{i}") for i in range(2)])
    psB_T = PsumPool(pg, [pB(f"psB_T{i}") for i in range(1)])
    psB_g = PsumPool(pg, [pB("psB_g")])
    psB_b = PsumPool(pg, [pB("psB_b", BF16, 1024)])
    psB_s = PsumPool(pg, [pB(f"psB_s{i}") for i in range(2)])
    psB_u = PsumPool(pg, [pB("psB_u")])

    for jj in range(4):
        pg.dma("sp", cwT[:, :, jj], cw_d[jj:jj + 1, :].rearrange("o (b p) -> p (o b)", p=128), [], [cB], sb=cB,
               allow_slow_non_contiguous=True)
    pg.dma("sp", cbT[:, :], cb_d.rearrange("o (b p) -> p (o b)", p=128), [], [cB], sb=cB, allow_slow_non_contiguous=True)
    pg.dma("sp", big_c[:, :], big_d.rearrange("o h -> h o"), [], [cB], sb=cB, allow_slow_non_contiguous=True)
    pg.dma("sp", bfg_c[:, :], bfg_d.rearrange("o h -> h o"), [], [cB], sb=cB, allow_slow_non_contiguous=True)
    pg.dma("sp", bog_t[:, :], bog_d[0:1, :].partition_broadcast(128), [], [cB], sb=cB)
    pg.dma("sp", mnw_t[:, :], mnw_d[0:1, :].partition_broadcast(128), [], [cB], sb=cB)
    for b in range(8):
        pg.memset("pool", zc[b][:, 0:3], 0.0, [zc_b[b]])
    pg.memset("pool", g_car[:, :], 0.0, [gB])
    for hh in range(2):
        vt = vaugs.t[hh]
        pg.memset("pool", vt[:, :, :], 0.0, [vaugs.b[hh]])

    for j in range((NST if 'j1' not in DBG else 1) if 'B' not in SKIP else 0):
        for tt in range(4):
            load_xT(x_d, j * TS + tt * 128, 128, tt * 128, xsB, psB_T, xTB, xTB_b)
        pgt, pgb = psB_g.next()
        for c in range(KC):
            pg.mm(pgt[0:4, :], winB[:, c, cI:cI + 4], xTB[:, c, :], c == 0, c == KC - 1, [xTB_b, winB_b], [pgb])
        pg.ts("dve", g_ig[:, :], pgt[0:4, :], big_c[0:4, 0:1], None, ALU.add, None, [pgb, cB], [gB])
        pgt, pgb = psB_g.next()
        for c in range(KC):
            pg.mm(pgt[0:4, :], winB[:, c, cF:cF + 4], xTB[:, c, :], c == 0, c == KC - 1, [xTB_b, winB_b], [pgb])
        pg.ts("dve", g_t1[:, :], pgt[0:4, :], bfg_c[0:4, 0:1], None, ALU.add, None, [pgb, cB], [gB])
        log_sigmoid(g_lf[:, :], g_t1[:, :], g_t2[:, :], g_M[:, :], gB, [gB], [gB])
        pg.op("dve", lambda e: e.tensor_tensor_scan(out=g_B[:, :], data0=one_c[0:4, 0:1].broadcast_to([4, TS]), data1=g_lf[:, :],
                                                    initial=g_car[0:4, 0:1], op0=ALU.mult, op1=ALU.add), [gB, cB], [gB])
        pg.tt("dve", g_u[:, :], g_ig[:, :], g_B[:, :], ALU.subtract, [gB], [gB])
        pg.op("dve", lambda e: e.tensor_tensor_scan(out=g_M[:, :], data0=one_c[0:4, 0:1].broadcast_to([4, TS]), data1=g_u[:, :],
                                                    initial=g_car[0:4, 1:2], op0=ALU.mult, op1=ALU.max), [gB, cB], [gB])
        pg.cp("dve", g_Rp[:, 0:1], g_car[:, 1:2], [gB], [gB])
        pg.cp("dve", g_Rc[:, :], g_M[:, :].rearrange("p (c t) -> p c t", c=4)[:, :, 127], [gB], [gB])
        pg.cp("dve", g_Rp[:, 1:4], g_Rc[:, 0:3], [gB], [gB])
        pg.cp("dve", g_car[:, 0:1], g_B[:, TS - 1:TS], [gB], [gB])
        pg.cp("dve", g_car[:, 1:2], g_M[:, TS - 1:TS], [gB], [gB])
        Rb = g_Rc[:, :].unsqueeze(2).broadcast_to([4, 4, 128])
        pg.tt("dve", g_t1[:, :].rearrange("p (c t) -> p c t", c=4), g_u[:, :].rearrange("p (c t) -> p c t", c=4), Rb,
              ALU.subtract, [gB], [gB])
        pg.act(g_wk[:, :], g_t1[:, :], AF.Exp, [gB], [gB])
        pg.tt("dve", g_t2[:, :].rearrange("p (c t) -> p c t", c=4), g_B[:, :].rearrange("p (c t) -> p c t", c=4), Rb,
              ALU.add, [gB], [gB])
        pg.act(g_fl[:, :], g_t2[:, :], AF.Exp, [gB], [gB], scale=-1.0)
        pg.tt("dve", g_g[:, :], g_Rp[:, :], g_Rc[:, :], ALU.subtract, [gB], [gB])
        pg.act(g_g[:, :], g_g[:, :], AF.Exp, [gB], [gB])
        for h2 in range(4):
            pg.ts("dve", g_gd[:, h2 * 4:(h2 + 1) * 4], g_g[:, :], ident_f[0:4, h2:h2 + 1], None, ALU.mult, None, [gB, cB], [gB])
        pgt, pgb = psB_g.next()
        for cc in range(4):
            pg.tr(pgt[:, cc * 8:cc * 8 + 4], g_wk[0:4, cc * 128:(cc + 1) * 128], ident_f[0:4, 0:4], [gB, cB], [pgb])
            pg.tr(pgt[:, cc * 8 + 4:cc * 8 + 8], g_fl[0:4, cc * 128:(cc + 1) * 128], ident_f[0:4, 0:4], [gB, cB], [pgb])
        pg.mm(pgt[:, 64:80], ones_f[0:4, 0:128], g_gd[0:4, 0:16], True, True, [gB, cB], [pgb])
        pg.cp("dve", tsc[:, :, :], pgt[:, 0:32].rearrange("p (c k) -> p c k", c=4), [pgb], [tsc_b])
        pg.cp("dve", gb[:, :], pgt[:, 64:80], [pgb], [gb_b])
        for blk in range(8):
            pt, pb = psB_mm.next()
            for c in range(KC):
                pg.mm(pt[:, :], winB[:, c, cQK + blk * 128:cQK + (blk + 1) * 128], xTB[:, c, :], c == 0, c == KC - 1,
                      [xTB_b, winB_b], [pb])
            pg.cp("act", zc[blk][:, 3:3 + TS], pt[:, :], [pb], [zc_b[blk]])
            pg.ts("dve", cacc[:, :], zc[blk][:, 3:3 + TS], cwT[:, blk, 3:4], cbT[:, blk:blk + 1], ALU.mult, ALU.add,
                  [zc_b[blk], cB], [cacc_b])
            for jj in (2, 1, 0):
                pg.stt("dve", cacc[:, :], zc[blk][:, jj:jj + TS], cwT[:, blk, jj:jj + 1], cacc[:, :], ALU.mult, ALU.add,
                       [zc_b[blk], cB, cacc_b], [cacc_b])
            pg.act(csig[:, :], cacc[:, :], AF.Sigmoid, [cacc_b], [cacc_b])
            pg.stt("dve", qkT[blk][:, :], cacc[:, :], (1.0 if blk < 4 else KSCALE), csig[:, :], ALU.mult, ALU.mult,
                   [cacc_b], [qkT_b[blk]])
            pg.cp("act", zc[blk][:, 0:3], zc[blk][:, TS:TS + 3], [zc_b[blk]], [zc_b[blk]])
            if j == NST - 1:
                pg.dma("sp", conv_o[:, blk * 128:(blk + 1) * 128].rearrange("t f -> f t"), zc[blk][:, 0:3], [zc_b[blk]], [],
                       sb=semOB, allow_slow_non_contiguous=True)
        mt, mtb = mnT.next()
        for tt in range(4):
            t = j * 4 + tt
            sl = slice(tt * 128, (tt + 1) * 128)
            first = (t == 0)
            va, vab = vaugs.next()
            pt, pb = psB_mm.next()
            for c in range(KC):
                pg.mm(pt[:, :], xTB[:, c, sl], winB[:, c, cV:cV + 512], c == 0, c == KC - 1, [xTB_b, winB_b], [pb])
            for hm in range(4):
                pg.ts("dve", va[:, hm, 0:128], pt[:, hm * 128:(hm + 1) * 128], tsc[:, tt, hm:hm + 1], None, ALU.mult, None,
                      [pb, tsc_b], [vab])
            pg.cp("dve", va[:, :, 128:129], tsc[:, tt, 0:4].unsqueeze(2), [tsc_b], [vab])
            ogt, ogb = ogs.next()
            pt, pb = psB_mm.next()
            for c in range(KC):
                pg.mm(pt[:, :], xTB[:, c, sl], winB[:, c, cO:cO + 512], c == 0, c == KC - 1, [xTB_b, winB_b], [pb])
            pg.tt("dve", ogtmp[:, :], pt[:, :], bog_t[:, :], ALU.add, [pb, cB], [ogtmp_b])
            pg.act(ogt[:, :], ogtmp[:, :], AF.Sigmoid, [ogtmp_b], [ogb])
            kt, ktb = ktoks.next()
            pbt, pbb = psB_b.next()
            for hm in range(4):
                pg.tr(pbt[:, hm * 128:(hm + 1) * 128], qkT[4 + hm][:, sl], ident_b[:, :], [qkT_b[4 + hm], cB], [pbb])
            pg.cp("act", kt[:, :, :], pbt[:, 0:512].rearrange("p (h d) -> p h d", h=4), [pbb], [ktb])
            hb, hbb = hbufs.next()
            stS = {}

            def partS(hm, tt=tt, sl=sl, first=first):
                gcol = gb[:, hm * 4 + tt:hm * 4 + tt + 1]
                pss, psb = psB_s.next()
                pg.mm(pss[:, 0:128], qkT[4 + hm][:, sl], qkT[hm][:, sl], True, True, [qkT_b[4 + hm], qkT_b[hm]], [psb])
                at, atb = ATs.next()
                pg.tt("dve", at[:, :], pss[:, 0:128], mask01[:, :], ALU.mult, [psb, cB], [atb])
                qgt = qgb = None
                if not first:
                    qgt, qgb = qgs.next()
                    pg.ts("dve", qgt[:, :], qkT[hm][:, sl], gcol, None, ALU.mult, None, [qkT_b[hm], gb_b], [qgb])
                stS[hm] = (pss, psb, at, atb, qgt, qgb, gcol)

            def partN(hm, tt=tt, first=first, va=va, vab=vab, kt=kt, ktb=ktb, hb=hb, hbb=hbb):
                pss, psb, at, atb, qgt, qgb, gcol = stS.pop(hm)
                pg.mm(pss[:, 256:385], at[:, :], va[:, hm, 0:129], True, first, [atb, vab], [psb])
                if not first:
                    pg.mm(pss[:, 256:385], qgt[:, :], Cb[hm][:, 0:129], False, True, [qgb, Cb_b[hm]], [psb])
                ns_, nsb = numS.next()
                pg.cp("act", ns_[:, 0:129], pss[:, 256:385], [psb], [nsb])
                psu, pub = psB_u.next()
                pg.mm(psu[:, 0:129], kt[:, hm, :], va[:, hm, 0:129], True, True, [ktb, vab], [pub])
                if first:
                    pg.cp("dve", Cf[hm][:, 0:129], psu[:, 0:129], [pub], [Cf_b[hm]])
                else:
                    pg.stt("dve", Cf[hm][:, 0:129], Cf[hm][:, 0:129], gcol, psu[:, 0:129], ALU.mult, ALU.add,
                           [pub, gb_b, Cf_b[hm]], [Cf_b[hm]])
                pg.cp("act", Cb[hm][:, 0:129], Cf[hm][:, 0:129], [Cf_b[hm]], [Cb_b[hm]])
                pg.stt("dve", hsm[:, 0:1], ns_[:, 128:129], -1.0, ns_[:, 128:129], ALU.mult, ALU.max, [nsb], [hsm_b])
                pg.tt("dve", hsm[:, 1:2], hsm[:, 0:1], tsc[:, tt, 4 + hm:5 + hm], ALU.max, [hsm_b, tsc_b], [hsm_b])
                pg.op("dve", lambda e: e.reciprocal(hsm[:, 2:3], hsm[:, 1:2]), [hsm_b], [hsm_b])
                pg.ts("dve", hb[:, hm * 128:(hm + 1) * 128], ns_[:, 0:128], hsm[:, 2:3], None, ALU.mult, None,
                      [nsb, hsm_b], [hbb])

            partS(0)
            for hm in range(4):
                if hm + 1 < 4:
                    partS(hm + 1)
                partN(hm)
            for hm in range(4):
                pg.op("dve", lambda e, hm=hm, hb=hb: e.bn_stats(stats[:, hm, :], hb[:, hm * 128:(hm + 1) * 128]), [hbb], [hsm_b])
                pg.op("dve", lambda e, hm=hm: e.bn_aggr(mvs[:, hm, 0:2], stats[:, hm, :]), [hsm_b], [hsm_b])
            pg.act(mvs[:, :, 2:3], mvs[:, :, 1:2], AF.Ln, [hsm_b], [hsm_b], bias=eps_c[:, 0:1])
            pg.act(mvs[:, :, 2:3], mvs[:, :, 2:3], AF.Exp, [hsm_b], [hsm_b], scale=-0.5)
            for hm in range(4):
                pg.ts("dve", mnf[:, hm * 128:(hm + 1) * 128], hb[:, hm * 128:(hm + 1) * 128], mvs[:, hm, 0:1], mvs[:, hm, 2:3],
                      ALU.subtract, ALU.mult, [hbb, hsm_b], [mnf_b])
            pg.tt("pool", mnf[:, :], mnf[:, :], mnw_t[:, :], ALU.mult, [mnf_b, cB], [mnf_b])
            mb, mbb = mnb.next()
            pg.tt("pool", mb[:, :], mnf[:, :], ogt[:, :], ALU.mult, [mnf_b, ogb], [mbb])
            pbt, pbb = psB_b.next()
            for hm in range(4):
                pg.tr(pbt[:, hm * 128:(hm + 1) * 128], mb[:, hm * 128:(hm + 1) * 128], ident_b[:, :], [mbb, cB], [pbb])
            pg.cp("act", mt[:, :, sl], pbt[:, 0:512].rearrange("p (h d) -> p h d", h=4), [pbb], [mtb])
        pg.dma("sp", mix_d[:, 4:8, j * TS:(j + 1) * TS], mt[:, :, :], [mtb], [mix_db], sb=mtb)
    for hm in range(4):
        pg.dma("sp", C_o[hm, :, :], Cf[hm][:, 0:128], [Cf_b[hm]], [], sb=semOB)
        pg.dma("sp", n_o[hm:hm + 1, :].rearrange("o d -> d o"), Cf[hm][:, 128:129], [Cf_b[hm]], [], sb=semOB,
               allow_slow_non_contiguous=True)
    pg.tt("dve", g_m[:, :], g_car[:, 0:1], g_car[:, 1:2], ALU.add, [gB], [gB])
    pg.dma("sp", m_o.rearrange("o h -> h o"), g_m[:, :], [gB], [], sb=semOB, allow_slow_non_contiguous=True)
    nB = pg.emit()
    esB.close()

    if 'nosample' not in DBG:
        esS = ExitStack()

        def sS(name, shape, dt=F32):
            return esS.enter_context(nc.sbuf_tensor(name, list(shape), dt))

        def pS(name, dt=F32, n=512):
            return esS.enter_context(nc.psum_tensor(name, [128, n], dt))

        NT = NSTOK
        winS = sS("winS", [128, KC, INC], BF16)
        winS_b = pg.buf("winS")
        for c in range(KC):
            for (a0, a1) in ((0, 1544), (1544, 2568), (2568, INC)):
                pg.dma("pool", winS[:, c, a0:a1], wv[:, c, a0:a1], [], [winS_b], sb=winS_b)
        semS = pg.buf("semS")
        semSo = pg.buf("semSo")
        xsS = Rot(pg, [sS("xsS0", [128, D])], "xsS")
        xsT = sS("xsT", [128, KC, NT], BF16)
        xsT_b = pg.buf("xsT")
        psS_mm = PsumPool(pg, [pS("psS_mm0")])
        psS_T = PsumPool(pg, [pS("psS_T0")])
        psS_g = psS_T
        psS_b = PsumPool(pg, [pS(f"psS_b{i}", BF16, 1024) for i in range(2)])
        psS_s = PsumPool(pg, [pS(f"psS_s{i}") for i in range(2)])
        psS_o = PsumPool(pg, [pS("psS_o")])
        psS_l = PsumPool(pg, [pS("psS_l")])
        sc = pg.buf("sconst")
        bffS = sS("bffS", [128, 8])
        bogS = sS("bogS", [128, 512])
        mnwS = sS("mnwS", [128, 512])
        cwS = sS("cwS", [128, 8, 4])
        cbS = sS("cbS", [128, 8])
        bigS = sS("bigS", [4, 1])
        bfgS = sS("bfgS", [4, 1])
        smS = sS("smS", [4, NSQ])
        maskU = sS("maskU", [128, 128])
        ones_b = sS("ones_b", [128, 1], BF16)
        pg.group_begin()
        pg.dma("sp", bffS[:, :], bff_d[0:1, :].partition_broadcast(128), [], [sc], sb=semS)
        pg.dma("sp", bogS[:, :], bog_d[0:1, :].partition_broadcast(128), [], [sc], sb=semS)
        pg.dma("sp", mnwS[:, :], mnw_d[0:1, :].partition_broadcast(128), [], [sc], sb=semS)
        for jj in range(4):
            pg.dma("sp", cwS[:, :, jj], cw_d[jj:jj + 1, :].rearrange("o (b p) -> p (o b)", p=128), [], [sc], sb=semS,
                   allow_slow_non_contiguous=True)
        pg.dma("sp", cbS[:, :], cb_d.rearrange("o (b p) -> p (o b)", p=128), [], [sc], sb=semS, allow_slow_non_contiguous=True)
        pg.dma("sp", bigS[:, :], big_d.rearrange("o h -> h o"), [], [sc], sb=semS, allow_slow_non_contiguous=True)
        pg.dma("sp", bfgS[:, :], bfg_d.rearrange("o h -> h o"), [], [sc], sb=semS, allow_slow_non_contiguous=True)
        pg.dma("sp", smS[:, :], sm_d.rearrange("b h -> h b"), [], [sc], sb=semS, allow_slow_non_contiguous=True)
        pg.group_end()
        pg.ts("pool", maskU[:, :], mask01[:, :], -1.0, 1.0, ALU.mult, ALU.add, [cB], [sc])
        pg.memset("pool", ones_b[:, :], 1.0, [sc])

        load_xT(xs_d, 0, NT, 0, xsS, psS_T, xsT, xsT_b)

        bdq = sS("bdq", [128, NSQ, 4, 16], BF16)
        bdq_b = pg.buf("bdq")
        pg.memset("pool", bdq[:, :, :, :], 0.0, [bdq_b])
        kTn = sS("kTn", [128, 4, NT], BF16)
        kTn_b = pg.buf("kTn")
        for pp in range(4):
            pt, pb = psS_mm.next()
            for c in range(KC):
                pg.mm(pt[:, 0:NT], winS[:, c, FQ + pp * 128:FQ + (pp + 1) * 128], xsT[:, c, :], c == 0, c == KC - 1,
                      [xsT_b, winS_b], [pb])
            pg.ts("dve", bdq[0:64, :, pp, 0:8], pt[0:64, 0:NT].rearrange("p (b t) -> p b t", b=NSQ), QSCALE, None, ALU.mult, None,
                  [pb], [bdq_b])
            pg.ts("dve", bdq[64:128, :, pp, 8:16], pt[64:128, 0:NT].rearrange("p (b t) -> p b t", b=NSQ), QSCALE, None, ALU.mult,
                  None, [pb], [bdq_b])
            pt, pb = psS_mm.next()
            for c in range(KC):
                pg.mm(pt[:, 0:NT], winS[:, c, FK + pp * 128:FK + (pp + 1) * 128], xsT[:, c, :], c == 0, c == KC - 1,
                      [xsT_b, winS_b], [pb])
            pg.cp("act", kTn[:, pp, :], pt[:, 0:NT], [pb], [kTn_b])
        gS = pg.buf("gS")
        q_ig = sS("q_ig", [4, NT]); q_lf = sS("q_lf", [4, NT]); q_B = sS("q_B", [4, NT]); q_u = sS("q_u", [4, NT])
        q_M = sS("q_M", [4, NT]); q_t1 = sS("q_t1", [4, NT]); q_t2 = sS("q_t2", [4, NT]); q_wk = sS("q_wk", [4, NT])
        q_fl = sS("q_fl", [4, NT]); q_R = sS("q_R", [4, NSQ]); q_g = sS("q_g", [4, NSQ]); q_gd = sS("q_gd", [4, 16])
        q_m = sS("q_m", [4, NSQ])
        pgt, pgb = psS_g.next()
        for c in range(KC):
            pg.mm(pgt[0:4, 0:NT], winS[:, c, MI:MI + 4], xsT[:, c, :], c == 0, c == KC - 1, [xsT_b, winS_b], [pgb])
        pg.ts("dve", q_ig[:, :], pgt[0:4, 0:NT], bigS[0:4, 0:1], None, ALU.add, None, [pgb, sc], [gS])
        pgt, pgb = psS_g.next()
        for c in range(KC):
            pg.mm(pgt[0:4, 0:NT], winS[:, c, MF:MF + 4], xsT[:, c, :], c == 0, c == KC - 1, [xsT_b, winS_b], [pgb])
        pg.ts("dve", q_t1[:, :], pgt[0:4, 0:NT], bfgS[0:4, 0:1], None, ALU.add, None, [pgb, sc], [gS])
        log_sigmoid(q_lf[:, :], q_t1[:, :], q_t2[:, :], q_M[:, :], gS, [gS], [gS])
        for b in range(NSQ):
            cs = slice(b * TD, (b + 1) * TD)
            pg.op("dve", lambda e, cs=cs: e.tensor_tensor_scan(out=q_B[:, cs], data0=one_c[0:4, 0:1].broadcast_to([4, TD]),
                                                                data1=q_lf[:, cs], initial=0.0, op0=ALU.mult, op1=ALU.add),
                  [gS, cB], [gS])
        pg.tt("dve", q_u[:, :], q_ig[:, :], q_B[:, :], ALU.subtract, [gS], [gS])
        for b in range(NSQ):
            cs = slice(b * TD, (b + 1) * TD)
            pg.op("dve", lambda e, cs=cs, b=b: e.tensor_tensor_scan(out=q_M[:, cs], data0=one_c[0:4, 0:1].broadcast_to([4, TD]),
                                                                     data1=q_u[:, cs], initial=smS[0:4, b:b + 1],
                                                                     op0=ALU.mult, op1=ALU.max), [gS, cB, sc], [gS])
        pg.cp("dve", q_R[:, :], q_M[:, :].rearrange("p (b t) -> p b t", b=NSQ)[:, :, TD - 1], [gS], [gS])
        RbS = q_R[:, :].unsqueeze(2).broadcast_to([4, NSQ, TD])
        pg.tt("dve", q_t1[:, :].rearrange("p (b t) -> p b t", b=NSQ), q_u[:, :].rearrange("p (b t) -> p b t", b=NSQ), RbS,
              ALU.subtract, [gS], [gS])
        pg.act(q_wk[:, :], q_t1[:, :], AF.Exp, [gS], [gS])
        pg.tt("dve", q_t2[:, :].rearrange("p (b t) -> p b t", b=NSQ), q_B[:, :].rearrange("p (b t) -> p b t", b=NSQ), RbS,
              ALU.add, [gS], [gS])
        pg.act(q_fl[:, :], q_t2[:, :], AF.Exp, [gS], [gS], scale=-1.0)
        pg.tt("dve", q_g[:, :], smS[:, :], q_R[:, :], ALU.subtract, [gS, sc], [gS])
        pg.act(q_g[:, :], q_g[:, :], AF.Exp, [gS], [gS])
        pg.tt("dve", q_m[:, :], q_R[:, :], q_B[:, :].rearrange("p (b t) -> p b t", b=NSQ)[:, :, TD - 1], ALU.add, [gS], [gS])
        pg.dma("sp", ms_o.rearrange("b h -> h b"), q_m[:, :], [gS], [], sb=semSo, allow_slow_non_contiguous=True)
        for h2 in range(4):
            pg.ts("dve", q_gd[:, h2 * 4:(h2 + 1) * 4], q_g[:, :], ident_f[0:4, h2:h2 + 1], None, ALU.mult, None, [gS, cB], [gS])
        tscS = sS("tscS", [8, NSQ, 8])
        gbS = sS("gbS", [128, 16])
        tscS_b = pg.buf("tscS")
        pgt, pgb = psS_g.next()
        for b in range(NSQ):
            pg.tr(pgt[0:TD, b * 8:b * 8 + 4], q_wk[0:4, b * TD:(b + 1) * TD], ident_f[0:4, 0:4], [gS, cB], [pgb])
            pg.tr(pgt[0:TD, b * 8 + 4:b * 8 + 8], q_fl[0:4, b * TD:(b + 1) * TD], ident_f[0:4, 0:4], [gS, cB], [pgb])
        pg.mm(pgt[:, 64:80], ones_f[0:4, 0:128], q_gd[0:4, 0:16], True, True, [gS, cB], [pgb])
        pg.cp("dve", tscS[:, :, :], pgt[0:TD, 0:32].rearrange("p (b k) -> p b k", b=NSQ), [pgb], [tscS_b])
        pg.cp("dve", gbS[:, :], pgt[:, 64:80], [pgb], [tscS_b])
        zcS = [sS(f"zcS{b}", [128, NSQ, 3 + TD]) for b in range(8)]
        zcS_b = [pg.buf("zcS") for b in range(8)]
        qkS = [sS(f"qkS{b}", [128, NT], BF16) for b in range(8)]
        qkS_b = [pg.buf("qkS") for b in range(8)]
        caccS = sS("caccS", [128, NSQ, TD]); csigS = sS("csigS", [128, NSQ, TD])
        caccS_b = pg.buf("caccS")
        semZ = pg.buf("semZ")
        pg.group_begin()
        for blk in range(8):
            for b in range(NSQ):
                pg.dma("sp", zcS[blk][:, b, 0:3], sconv_d[b, :, blk * 128:(blk + 1) * 128].rearrange("t f -> f t"), [], [zcS_b[blk]],
                       sb=semZ, allow_slow_non_contiguous=True)
        pg.group_end()
        for blk in range(8):
            pt, pb = psS_mm.next()
            for c in range(KC):
                pg.mm(pt[:, 0:NT], winS[:, c, MQK + blk * 128:MQK + (blk + 1) * 128], xsT[:, c, :], c == 0, c == KC - 1,
                      [xsT_b, winS_b], [pb])
            pg.cp("act", zcS[blk][:, :, 3:3 + TD], pt[:, 0:NT].rearrange("p (b t) -> p b t", b=NSQ), [pb], [zcS_b[blk]])
            pg.ts("dve", caccS[:, :, :], zcS[blk][:, :, 3:3 + TD], cwS[:, blk, 3:4], cbS[:, blk:blk + 1], ALU.mult, ALU.add,
                  [zcS_b[blk], sc], [caccS_b])
            for jj in (2, 1, 0):
                pg.stt("dve", caccS[:, :, :], zcS[blk][:, :, jj:jj + TD], cwS[:, blk, jj:jj + 1], caccS[:, :, :], ALU.mult, ALU.add,
                       [zcS_b[blk], sc, caccS_b], [caccS_b])
            pg.act(csigS[:, :, :], caccS[:, :, :], AF.Sigmoid, [caccS_b], [caccS_b])
            pg.stt("dve", qkS[blk][:, :].rearrange("p (b t) -> p b t", b=NSQ), caccS[:, :, :], (1.0 if blk < 4 else KSCALE),
                   csigS[:, :, :], ALU.mult, ALU.mult, [caccS_b], [qkS_b[blk]])
            for b in range(NSQ):
                pg.dma("sp", convs_o[b, :, blk * 128:(blk + 1) * 128].rearrange("t f -> f t"), zcS[blk][:, b, TD:TD + 3],
                       [zcS_b[blk]], [], sb=semSo, allow_slow_non_contiguous=True)

        knS = sS("knS", [8, NSQ, 512]); vnS = sS("vnS", [8, NSQ, 512]); vnB = sS("vnB", [8, NSQ, 512], BF16)
        lfnS = sS("lfnS", [8, NSQ, 8]); ltmp = sS("ltmp", [8, 32])
        vaS = sS("vaS", [8, NSQ, 4, 130], BF16); ogS = sS("ogS", [8, NSQ, 512]); ogtS = sS("ogtS", [8, 512])
        ktS = sS("ktS", [8, NSQ, 4, 128], BF16)
        tokS_b = [pg.buf("tokS") for b in range(NSQ)]
        ltmp_b = pg.buf("ltmp")
        pg.memset("pool", vaS[:, :, :, :], 0.0, tokS_b)
        lfn_bs = []
        for b in range(NSQ):
            cs = slice(b * TD, (b + 1) * TD)
            tb = tokS_b[b]
            for (c0, dst, od) in ((FK, knS, ksm_o), (FV, vnS, vsm_o)):
                pt, pb = psS_mm.next()
                for c in range(KC):
                    pg.mm(pt[0:TD, :], xsT[:, c, cs], winS[:, c, c0:c0 + 512], c == 0, c == KC - 1, [xsT_b, winS_b], [pb])
                ob_ = pg.buf("kvn")
                pg.cp("act", dst[:, b, :], pt[0:TD, :], [pb], [ob_])
                pg.dma("sp", od[b * TD:(b + 1) * TD, :], dst[:, b, :], [ob_], [], sb=semSo)
            pg.cp("dve", vnB[:, b, :], vnS[:, b, :], [ob_], [tb])
            pt, pb = psS_mm.next()
            for c in range(KC):
                pg.mm(pt[0:TD, 0:8], xsT[:, c, cs], winS[:, c, FF:FF + 8], c == 0, c == KC - 1, [xsT_b, winS_b], [pb])
            pg.tt("dve", ltmp[:, 0:8], pt[0:TD, 0:8], bffS[0:TD, :], ALU.add, [pb, sc], [ltmp_b])
            lfn_b = pg.buf("lfn")
            lfn_bs.append(lfn_b)
            log_sigmoid(lfnS[:, b, :], ltmp[:, 0:8], ltmp[:, 8:16], ltmp[:, 16:24], ltmp_b, [ltmp_b], [lfn_b])
            pg.dma("sp", lfs_o[b * TD:(b + 1) * TD, :], lfnS[:, b, :], [lfn_b], [], sb=semSo)
            pt, pb = psS_mm.next()
            for c in range(KC):
                pg.mm(pt[0:TD, :], xsT[:, c, cs], winS[:, c, MV:MV + 512], c == 0, c == KC - 1, [xsT_b, winS_b], [pb])
            for hm in range(4):
                pg.ts("dve", vaS[:, b, hm, 0:128], pt[0:TD, hm * 128:(hm + 1) * 128], tscS[:, b, hm:hm + 1], None, ALU.mult, None,
                      [pb, tscS_b], [tb])
            pg.cp("dve", vaS[:, b, :, 128:129], tscS[:, b, 0:4].unsqueeze(2), [tscS_b], [tb])
            pt, pb = psS_mm.next()
            for c in range(KC):
                pg.mm(pt[0:TD, :], xsT[:, c, cs], winS[:, c, MO:MO + 512], c == 0, c == KC - 1, [xsT_b, winS_b], [pb])
            pg.tt("dve", ogtS[:, :], pt[0:TD, :], bogS[0:TD, :], ALU.add, [pb, sc], [ltmp_b])
            pg.act(ogS[:, b, :], ogtS[:, :], AF.Sigmoid, [ltmp_b], [tb])
            pbt, pbb = psS_b.next()
            for hm in range(4):
                pg.tr(pbt[0:TD, hm * 128:(hm + 1) * 128], qkS[4 + hm][:, cs], ident_b[:, :], [qkS_b[4 + hm], cB], [pbb])
            pg.cp("act", ktS[:, b, :, :], pbt[0:TD, 0:512].rearrange("p (h d) -> p h d", h=4), [pbb], [tb])

        CfS = Rot(pg, [sS(f"CfS{i}", [128, 130]) for i in range(2)], "CfS")
        CbS = Rot(pg, [sS(f"CbS{i}", [128, 130], BF16) for i in range(2)], "CbS")
        CnS = Rot(pg, [sS(f"CnS{i}", [128, 130]) for i in range(2)], "CnS")
        ATS = Rot(pg, [sS(f"ATS{i}", [8, 8], BF16) for i in range(2)], "ATS")
        qgS = Rot(pg, [sS(f"qgS{i}", [128, 8], BF16) for i in range(2)], "qgS")
        nsS = Rot(pg, [sS(f"nsS{i}", [8, 130]) for i in range(2)], "nsS")
        hbS = Rot(pg, [sS(f"hbS{i}", [8, 512]) for i in range(2)], "hbS")
        hsmS = sS("hsmS", [8, 8]); statS = sS("statS", [8, 4, 6]); mvS = sS("mvS", [8, 4, 4]); mnfS = sS("mnfS", [8, 512])
        hsmS_b = pg.buf("hsmS"); mnfS_b = pg.buf("mnfS")
        mnbS = Rot(pg, [sS(f"mnbS{i}", [8, 512], BF16) for i in range(2)], "mnbS")
        mixS = sS("mixS", [128, 8, NT], BF16)
        mixS_b = pg.buf("mixS")
        for b in range(NSQ):
            cs = slice(b * TD, (b + 1) * TD)
            tb = tokS_b[b]
            hb, hbb = hbS.next()
            for hm in range(4):
                cf, cfb = CfS.next()
                pg.group_begin()
                pg.dma("sp", cf[:, 0:128], sC_d[b, hm, :, :], [], [cfb], sb=cfb)
                pg.dma("sp", cf[:, 128:129], sn_d[b, hm:hm + 1, :].rearrange("o d -> d o"), [], [cfb], sb=cfb,
                       allow_slow_non_contiguous=True)
                pg.group_end()
                cb_, cbb = CbS.next()
                pg.cp("act", cb_[:, 0:129], cf[:, 0:129], [cfb], [cbb])
                gcol = gbS[:, hm * 4 + b:hm * 4 + b + 1]
                pss, psb = psS_s.next()
                pg.mm(pss[0:TD, 0:TD], qkS[4 + hm][:, cs], qkS[hm][:, cs], True, True, [qkS_b[4 + hm], qkS_b[hm]], [psb])
                at, atb = ATS.next()
                pg.tt("dve", at[:, :], pss[0:TD, 0:TD], mask01[0:TD, 0:TD], ALU.mult, [psb, cB], [atb])
                qgt, qgb = qgS.next()
                pg.ts("dve", qgt[:, :], qkS[hm][:, cs], gcol, None, ALU.mult, None, [qkS_b[hm], tscS_b], [qgb])
                pg.mm(pss[0:TD, 256:385], at[:, :], vaS[:, b, hm, 0:129], True, False, [atb, tb], [psb])
                pg.mm(pss[0:TD, 256:385], qgt[:, :], cb_[:, 0:129], False, True, [qgb, cbb], [psb])
                ns_, nsb = nsS.next()
                pg.cp("act", ns_[:, 0:129], pss[0:TD, 256:385], [psb], [nsb])
                psu, pub = psS_s.next()
                pg.mm(psu[:, 0:129], ktS[:, b, hm, :], vaS[:, b, hm, 0:129], True, True, [tb], [pub])
                cn, cnb = CnS.next()
                pg.stt("dve", cn[:, 0:129], cf[:, 0:129], gcol, psu[:, 0:129], ALU.mult, ALU.add, [pub, tscS_b, cfb], [cnb])
                pg.group_begin()
                pg.dma("sp", Cs_o[b, hm, :, :], cn[:, 0:128], [cnb], [], sb=cnb)
                pg.dma("sp", ns_o[b, hm:hm + 1, :].rearrange("o d -> d o"), cn[:, 128:129], [cnb], [], sb=cnb,
                       allow_slow_non_contiguous=True)
                pg.group_end()
                pg.stt("dve", hsmS[:, 0:1], ns_[:, 128:129], -1.0, ns_[:, 128:129], ALU.mult, ALU.max, [nsb], [hsmS_b])
                pg.tt("dve", hsmS[:, 1:2], hsmS[:, 0:1], tscS[:, b, 4 + hm:5 + hm], ALU.max, [hsmS_b, tscS_b], [hsmS_b])
                pg.op("dve", lambda e: e.reciprocal(hsmS[:, 2:3], hsmS[:, 1:2]), [hsmS_b], [hsmS_b])
                pg.ts("dve", hb[:, hm * 128:(hm + 1) * 128], ns_[:, 0:128], hsmS[:, 2:3], None, ALU.mult, None, [nsb, hsmS_b], [hbb])
            for hm in range(4):
                pg.op("dve", lambda e, hm=hm, hb=hb: e.bn_stats(statS[:, hm, :], hb[:, hm * 128:(hm + 1) * 128]), [hbb], [hsmS_b])
                pg.op("dve", lambda e, hm=hm: e.bn_aggr(mvS[:, hm, 0:2], statS[:, hm, :]), [hsmS_b], [hsmS_b])
            pg.act(mvS[:, :, 2:3], mvS[:, :, 1:2], AF.Ln, [hsmS_b], [hsmS_b], bias=eps_c[0:TD, 0:1])
            pg.act(mvS[:, :, 2:3], mvS[:, :, 2:3], AF.Exp, [hsmS_b], [hsmS_b], scale=-0.5)
            for hm in range(4):
                pg.ts("dve", mnfS[:, hm * 128:(hm + 1) * 128], hb[:, hm * 128:(hm + 1) * 128], mvS[:, hm, 0:1], mvS[:, hm, 2:3],
                      ALU.subtract, ALU.mult, [hbb, hsmS_b], [mnfS_b])
            pg.tt("pool", mnfS[:, :], mnfS[:, :], mnwS[0:TD, :], ALU.mult, [mnfS_b, sc], [mnfS_b])
            mb, mbb = mnbS.next()
            pg.tt("pool", mb[:, :], mnfS[:, :], ogS[:, b, :], ALU.mult, [mnfS_b, tb], [mbb])
            pbt, pbb = psS_b.next()
            for hm in range(4):
                pg.tr(pbt[:, hm * 8:hm * 8 + TD], mb[:, hm * 128:(hm + 1) * 128], ident_b[0:TD, 0:TD], [mbb, cB], [pbb])
            pg.cp("act", mixS[:, 4:8, cs], pbt[:, 0:32].rearrange("p (h t) -> p h t", h=4), [pbb], [mixS_b])

        if with_cache:
            ptI = sS("ptI", [128, NPG], I32); ptF = sS("ptF", [128, NPG]); pcI = sS("pcI", [128, 1], I32); pcF = sS("pcF", [128, 1])
            idxF = sS("idxF", [128, NPG]); idxI = sS("idxI", [128, NPG], I32); pgI = sS("pgI", [128, 1], I32)
            idx_b = pg.buf("idx")
            lfp = sS("lfp", [128, 1024]); lfc = sS("lfc", [128, 8, 128]); ltot = sS("ltot", [128, 8]); llat = sS("llat", [128, 8])
            Rpg = sS("Rpg", [128, 8, 128]); RkT = sS("RkT", [128, 8, 128]); LnL = sS("LnL", [128, 8]); LnS = sS("LnS", [8, 8])
            bnew = sS("bnew", [8, 8])
            R_b = pg.buf("Rb")
            KVp = Rot(pg, [sS(f"KVp{i}", [128, 1024], BF16) for i in range(5)], "KVp")
            KTs = Rot(pg, [sS(f"KTs{i}", [128, 4, 128], BF16) for i in range(3)], "KTs")
            sTs = Rot(pg, [sS(f"sTs{i}", [128, 64]) for i in range(3)], "sTs")
            PTs = Rot(pg, [sS(f"PTs{i}", [128, 64], BF16) for i in range(3)], "PTs")
            oS = sS("oS", [8, 512]); lrow = sS("lrow", [1, 64]); lq = sS("lq", [8, 8]); rlq = sS("rlq", [8, 8])
            foS = sS("foS", [8, 512], BF16)
            fin_b = pg.buf("fin")
            pg.op("pool", lambda e: e.iota(pcI[:, :], pattern=[[0, 1]], base=0, channel_multiplier=1), [], [idx_b])
            pg.cp("dve", pcF[:, :], pcI[:, :], [idx_b], [idx_b])
            clfv = clf_d.rearrange("(pg t) h -> pg (t h)", t=128)
            for b in range(NSQ):
                cs = slice(b * TD, (b + 1) * TD)
                tb = tokS_b[b]
                pg.group_begin()
                pg.dma("sp", ptI[:, :], pt_d[b:b + 1, :].partition_broadcast(128), [], [idx_b], sb=idx_b)
                pg.dma("sp", pgI[:, :], pt_d[b:b + 1, :].rearrange("o p -> p o"), [], [idx_b], sb=idx_b, allow_slow_non_contiguous=True)
                pg.group_end()
                pg.cp("dve", ptF[:, :], ptI[:, :], [idx_b], [idx_b])
                pg.ts("dve", idxF[:, :], ptF[:, :], 128.0, pcF[:, 0:1], ALU.mult, ALU.add, [idx_b], [idx_b])
                pg.cp("dve", idxI[:, :], idxF[:, :], [idx_b], [idx_b])
                pg.dmaf("pool", lambda e: e.indirect_dma_start(out=lfp[:, :], out_offset=None, in_=clfv,
                                                              in_offset=bass.IndirectOffsetOnAxis(ap=pgI[:, 0:1], axis=0)),
                        [idx_b], [R_b], R_b)
                lf3 = lfp[:, :].rearrange("p (t h) -> p h t", h=8)
                for h in range(8):
                    pg.op("dve", lambda e, h=h: e.tensor_tensor_scan(out=lfc[:, h, :], data0=one_c[:, 0:1].broadcast_to([128, 128]),
                                                                      data1=lf3[:, h, :], initial=0.0, op0=ALU.mult, op1=ALU.add),
                          [R_b, cB], [R_b])
                pg.cp("dve", ltot[:, :], lfc[:, :, 127], [R_b], [R_b])
                pgt, pgb = psS_g.next()
                pg.mm(pgt[:, 0:8], maskU[:, :], ltot[:, :], True, True, [R_b, sc], [pgb])
                pg.mm(pgt[:, 8:16], ones_f[0:TD, 0:128], lfnS[:, b, :], True, True, [lfn_bs[b], cB], [pgb])
                pg.mm(pgt[0:TD, 16:24], mask01[0:TD, 0:TD], lfnS[:, b, :], True, True, [lfn_bs[b], cB], [pgb])
                pg.tt("dve", llat[:, :], pgt[:, 0:8], ltot[:, :], ALU.add, [pgb, R_b], [R_b])
                pg.cp("dve", LnL[:, :], pgt[:, 8:16], [pgb], [R_b])
                pg.cp("dve", LnS[:, :], pgt[0:TD, 16:24], [pgb], [R_b])
                pg.tt("dve", Rpg[:, :, :], llat[:, :].unsqueeze(2).broadcast_to([128, 8, 128]), lfc[:, :, :], ALU.subtract, [R_b], [R_b])
                for h in range(8):
                    ptt, ptb = psS_T.next()
                    pg.tr(ptt[:, 0:128], Rpg[:, h, :], ident_f[:, :], [R_b, cB], [ptb])
                    pg.ts("dve", RkT[:, h, :], ptt[:, 0:128], LnL[:, h:h + 1], None, ALU.add, None, [ptb, R_b], [R_b])
                pg.tt("dve", bnew[:, :], LnL[0:TD, :], LnS[:, :], ALU.subtract, [R_b], [R_b])
                po, pob = psS_o.next()
                pl, plb = psS_l.next()
                npg_ = NPG if 'p4' not in DBG else 4
                st1 = {}
                st2 = {}

                def stage1(p_, b=b):
                    kvp, kpb = KVp.next()
                    pg.dmaf("pool", lambda e, kvp=kvp, p_=p_: e.indirect_dma_start(
                        out=kvp[:, :], out_offset=None, in_=ckv_d, in_offset=bass.IndirectOffsetOnAxis(ap=idxI[:, p_:p_ + 1], axis=0)),
                        [idx_b], [kpb], kpb)
                    pbt, pbb = psS_b.next()
                    for pp in range(4):
                        pg.tr(pbt[:, pp * 128:(pp + 1) * 128], kvp[:, pp * 128:(pp + 1) * 128], ident_b[:, :], [kpb, cB], [pbb])
                    kt_, ktb_ = KTs.next()
                    pg.cp("dve", kt_[:, :, :], pbt[:, 0:512].rearrange("p (a t) -> p a t", a=4), [pbb], [ktb_])
                    st1[p_] = (kvp, kpb, kt_, ktb_)

                def stage2(p_, b=b):
                    kvp, kpb, kt_, ktb_ = st1.pop(p_)
                    pss, psb = psS_s.next()
                    for pp in range(4):
                        pg.mm(pss[:, pp * 16:(pp + 1) * 16], kt_[:, pp, :], bdq[:, b, pp, :], True, True, [ktb_, bdq_b], [psb])
                    st_, stb_ = sTs.next()
                    pg.tt("dve", st_[:, :].rearrange("p (h q) -> p h q", h=8), pss[:, 0:64].rearrange("p (h q) -> p h q", h=8),
                          RkT[:, :, p_].unsqueeze(2).broadcast_to([128, 8, 8]), ALU.add, [psb, R_b], [stb_])
                    pt_, ptb_ = PTs.next()
                    pg.act(pt_[:, :], st_[:, :], AF.Exp, [stb_], [ptb_])
                    st2[p_] = (kvp, kpb, pt_, ptb_)

                def stage3(p_, po=po, pob=pob, pl=pl, plb=plb):
                    kvp, kpb, pt_, ptb_ = st2.pop(p_)
                    for h in range(8):
                        pg.mm(po[0:TD, h * 64:(h + 1) * 64], pt_[:, h * 8:(h + 1) * 8], kvp[:, 512 + h * 64:512 + (h + 1) * 64],
                              p_ == 0, False, [ptb_, kpb], [pob])
                    pg.mm(pl[0:1, 0:64], ones_b[:, 0:1], pt_[:, :], p_ == 0, False, [ptb_, sc], [plb])

                for t_ in range(npg_ + 2):
                    if t_ < npg_:
                        stage1(t_)
                    if 0 <= t_ - 1 < npg_:
                        stage2(t_ - 1)
                    if 0 <= t_ - 2 < npg_:
                        stage3(t_ - 2)
                pss, psb = psS_s.next()
                for pp in range(4):
                    pg.mm(pss[0:TD, pp * 16:(pp + 1) * 16], kTn[:, pp, cs], bdq[:, b, pp, :], True, True, [kTn_b, bdq_b], [psb])
                st_, stb_ = sTs.next()
                pg.tt("dve", st_[0:TD, :].rearrange("p (h q) -> p h q", h=8), pss[0:TD, 0:64].rearrange("p (h q) -> p h q", h=8),
                      bnew[:, :].unsqueeze(2).broadcast_to([TD, 8, 8]), ALU.add, [psb, R_b], [stb_])
                pg.act(st_[0:TD, :], st_[0:TD, :], AF.Exp, [stb_], [stb_])
                pt_, ptb_ = PTs.next()
                pg.tt("dve", pt_[0:TD, :].rearrange("p (h q) -> p h q", h=8), st_[0:TD, :].rearrange("p (h q) -> p h q", h=8),
                      mask01[0:TD, 0:TD].unsqueeze(1).broadcast_to([TD, 8, TD]), ALU.mult, [stb_, cB], [ptb_])
                for h in range(8):
                    pg.mm(po[0:TD, h * 64:(h + 1) * 64], pt_[0:TD, h * 8:(h + 1) * 8], vnB[:, b, h * 64:(h + 1) * 64], False, True,
                          [ptb_, tb], [pob])
                pg.mm(pl[0:1, 0:64], ones_b[0:TD, 0:1], pt_[0:TD, :], False, True, [ptb_, sc], [plb])
                pg.cp("act", oS[:, :], po[0:TD, :], [pob], [fin_b])
                pg.cp("dve", lrow[:, :], pl[0:1, 0:64], [plb], [fin_b])
                lr_h = lrow.tensor if hasattr(lrow, "tensor") else lrow
                pg.group_begin()
                for q_ in range(TD):
                    pg.dma("sp", lq[q_:q_ + 1, :], bass.AP(lr_h, lrow[:, :].offset + q_, [[lrow[:, :].ap[0][0], 1], [8, 8]]),
                           [fin_b], [fin_b], sb=fin_b, allow_slow_non_contiguous=True)
                pg.group_end()
                pg.op("dve", lambda e: e.reciprocal(rlq[:, :], lq[:, :]), [fin_b], [fin_b])
                pg.tt("dve", foS[:, :].rearrange("p (h d) -> p h d", h=8), oS[:, :].rearrange("p (h d) -> p h d", h=8),
                      rlq[:, :].unsqueeze(2).broadcast_to([TD, 8, 64]), ALU.mult, [fin_b], [fin_b])
                pbt, pbb = psS_b.next()
                for c4 in range(4):
                    pg.tr(pbt[:, c4 * 8:c4 * 8 + TD], foS[:, c4 * 128:(c4 + 1) * 128], ident_b[0:TD, 0:TD], [fin_b, cB], [pbb])
                pg.cp("act", mixS[:, 0:4, cs], pbt[:, 0:32].rearrange("p (h t) -> p h t", h=4), [pbb], [mixS_b])
        pg.dma("sp", mix_d[:, :, S:S + NT], mixS[:, :, :], [mixS_b], [mix_db], sb=mixS_b)
        nS = pg.emit()
        esS.close()

    esC = ExitStack()

    def sC(name, shape, dt=F32):
        return esC.enter_context(nc.sbuf_tensor(name, list(shape), dt))

    def pC(name, dt=F32, n=512):
        return esC.enter_context(nc.psum_tensor(name, [128, n], dt))

    TB = 256
    w1b = sC("w1b", [128, KC, DFF // 2], BF16)
    w1b_b = pg.buf("w1b")
    for c in range(KC):
        pg.dma("pool", w1b[:, c, :], w1v[:, c, 2048:4096], [], [w1b_b], sb=w1b_b)
    w2S = sC("w2S", [128, 32, D], BF16)
    for c in range(32):
        pg.dma("pool", w2S[:, c, :], w2v[:, c, :], [], [w2_b], sb=w2_b)
    lnp = sC("lnp", [128, 4, D])
    for ii, dd in enumerate((l1g_d, l1b_d, l2g_d, l2b_d)):
        pg.dma("sp", lnp[:, ii, :], dd[0:1, :].partition_broadcast(128), [], [cB], sb=cB)
    mixC = sC("mixC", [128, KC, TB], BF16)
    mixC_b = pg.buf("mixC")
    xr = Rot(pg, [sC("xr0", [128, D])], "xr")
    x1s = [sC(f"x1s{i}", [128, D]) for i in range(4)]
    x1s_b = [pg.buf("x1s") for i in range(4)]
    x1T = sC("x1T", [128, KC, TB], BF16)
    x1T_b = pg.buf("x1T")
    hidT = sC("hidT", [128, 32, TB], BF16)
    hidT_b = [pg.buf("hidT") for f in range(32)]
    rtmp = Rot(pg, [sC(f"rtmp{i}", [128, TB]) for i in range(1)], "rtmp")
    lstat = sC("lstat", [128, 2, 6])
    lmv = sC("lmv", [128, 4])
    lst_b = pg.buf("lstat")
    psC_mm = PsumPool(pg, [pC(f"psC_mm{i}") for i in range(4)])
    psC_T = PsumPool(pg, [pC(f"psC_T{i}") for i in range(2)])

    def ln_inplace(xa, T, gi, xb):
        for h2 in range(2):
            pg.op("dve", lambda e, h2=h2: e.bn_stats(lstat[0:T, h2, :], xa[:, h2 * 512:(h2 + 1) * 512]), [xb], [lst_b])
        pg.op("dve", lambda e: e.bn_aggr(lmv[0:T, 0:2], lstat[0:T, :, :]), [lst_b], [lst_b])
        pg.act(lmv[0:T, 2:3], lmv[0:T, 1:2], AF.Ln, [lst_b], [lst_b], bias=eps_c[0:T, 0:1])
        pg.act(lmv[0:T, 2:3], lmv[0:T, 2:3], AF.Exp, [lst_b], [lst_b], scale=-0.5)
        pg.ts("dve", xa, xa, lmv[0:T, 0:1], lmv[0:T, 2:3], ALU.subtract, ALU.mult, [xb, lst_b], [xb])
        pg.tt("pool", xa, xa, lnp[0:T, gi, :], ALU.mult, [xb, cB], [xb])
        pg.tt("pool", xa, xa, lnp[0:T, gi + 1, :], ALU.add, [xb, cB], [xb])

    blocks = []
    nblk = (S // TB) if 'j1' not in DBG else 2
    if 'C' in SKIP:
        nblk = 0
    for bi in range(nblk):
        blocks.append((x_d, y_o, bi * TB, bi * TB, 2, 128))
    if 'nosample' not in DBG:
        blocks.append((xs_d, ys_o, 0, S, 1, NSTOK))
    def phase1(bi, blk):
        (src_d, out_d, row0, col0, ntl, T) = blk
        W = ntl * T if T == 128 else T
        pg.dma("sp", mixC[:, :, 0:W], mix_d[:, :, col0:col0 + W], [mix_db], [mixC_b], sb=mixC_b)
        for tl in range(ntl):
            sl = slice(tl * T, (tl + 1) * T)
            xi = (bi % 2) * 2 + tl
            xt, xb = xr.next()
            pg.dma("sp", xt[0:T, :], src_d[row0 + tl * T:row0 + (tl + 1) * T, :], [], [xb], sb=xb)
            xa = x1s[xi][0:T, :]
            for n2 in range(2):
                pt, pb = psC_mm.next()
                for c in range(KC):
                    pg.mm(pt[0:T, :], mixC[:, c, sl], woS[:, c, n2 * 512:(n2 + 1) * 512], c == 0, c == KC - 1,
                          [mixC_b, wo_b], [pb])
                pg.stt("dve", xa[:, n2 * 512:(n2 + 1) * 512], xt[0:T, n2 * 512:(n2 + 1) * 512], ALPHA, pt[0:T, :],
                       ALU.mult, ALU.add, [xb, pb], [x1s_b[xi]])
            ln_inplace(xa, T, 0, x1s_b[xi])

    def phase2(bi, blk):
        (src_d, out_d, row0, col0, ntl, T) = blk
        W = ntl * T if T == 128 else T
        for tl in range(ntl):
            xi = (bi % 2) * 2 + tl
            xa = x1s[xi][0:T, :]
            for g in range(2):
                ptt, ptb = psC_T.next()
                for cc in range(4):
                    c = g * 4 + cc
                    pg.tr(ptt[:, cc * 128:cc * 128 + T], xa[:, c * 128:(c + 1) * 128], ident_f[0:T, 0:T], [x1s_b[xi], cB], [ptb])
                pg.cp(pg.ev(), x1T[:, g * 4:g * 4 + 4, tl * T:(tl + 1) * T],
                      ptt[:, :].rearrange("p (c t) -> p c t", c=4)[:, :, 0:T], [ptb], [x1T_b])
        for f in range(32):
            pt, pb = psC_mm.next()
            for c in range(KC):
                w1t = w1a if f < 16 else w1b
                fo = (f % 16) * 128
                pg.mm(pt[:, 0:W], w1t[:, c, fo:fo + 128], x1T[:, c, 0:W], c == 0, c == KC - 1,
                      [x1T_b, w1_b if f < 16 else w1b_b], [pb])
            rt, rtb = rtmp.next()
            pg.act(rt[:, 0:W], pt[:, 0:W], AF.Relu, [pb], [rtb])
            pg.tt("dve" if f % 2 == 0 else "pool", hidT[:, f, 0:W], rt[:, 0:W], rt[:, 0:W], ALU.mult, [rtb], [hidT_b[f]])

    def phase3(bi, blk):
        (src_d, out_d, row0, col0, ntl, T) = blk
        for tl in range(ntl):
            sl = slice(tl * T, (tl + 1) * T)
            xi = (bi % 2) * 2 + tl
            xa = x1s[xi][0:T, :]
            for n2 in range(2):
                pt, pb = psC_mm.next()
                for f in range(32):
                    pg.mm(pt[0:T, :], hidT[:, f, sl], w2S[:, f, n2 * 512:(n2 + 1) * 512], f == 0, f == 31,
                          [hidT_b[f], w2_b], [pb])
                pg.stt("dve", xa[:, n2 * 512:(n2 + 1) * 512], xa[:, n2 * 512:(n2 + 1) * 512], ALPHA, pt[0:T, :],
                       ALU.mult, ALU.add, [x1s_b[xi], pb], [x1s_b[xi]])
            ln_inplace(xa, T, 2, x1s_b[xi])
            pg.dma("sp", out_d[row0 + tl * T:row0 + (tl + 1) * T, :], xa, [x1s_b[xi]], [], sb=x1s_b[xi])

    for bi, blk in enumerate(blocks):
        phase1(bi, blk)
        if bi > 0:
            phase3(bi - 1, blocks[bi - 1])
        phase2(bi, blk)
    if blocks:
        phase3(len(blocks) - 1, blocks[-1])
    nC = pg.emit()
    esC.close()
    return nc, es, pg, dict(nA=nA, nB=nB, nC=nC)


_CACHE = {}


def kernel(**inputs):
    n = 8
    nc, es, pg, info = build_program(debug=False, with_cache=True)
    I = {k: np.asarray(v) for k, v in inputs.items()}
    in_maps = []
    ckv = np.concatenate([I["cache_k"].reshape(5120 * 128, 512), I["cache_v"].reshape(5120 * 128, 512)], axis=1)
    clf = np.ascontiguousarray(I["cache_logf"]).reshape(5120 * 128, 8)
    for c in range(n):
        sl = slice(c * NSQ, (c + 1) * NSQ)
        in_maps.append({
            "x": np.ascontiguousarray(I["x_prompt"][c]),
            "xs": np.ascontiguousarray(I["x_sample"][sl].reshape(NSTOK, D)),
            "state_C": np.ascontiguousarray(I["state_C"][0, sl]),
            "state_n": np.ascontiguousarray(I["state_n"][0, sl]),
            "state_m": np.ascontiguousarray(I["state_m"][0, sl]),
            "state_conv": np.ascontiguousarray(I["state_conv"][0, sl]),
            "page_table": np.ascontiguousarray(I["page_table"][sl]),
            "cache_kv": ckv, "cache_logf": clf,
            "w_in": I["w_in"][0], "b_fox_f": I["b_fox_f"], "b_ig": I["b_ig"], "b_fg": I["b_fg"],
            "b_og": I["b_og"], "conv_w": I["conv_w"][0], "conv_b": I["conv_b"],
            "mlstm_norm_w": I["mlstm_norm_w"], "w_o": I["w_o"][0], "ln1_g": I["ln1_g"], "ln1_b": I["ln1_b"],
            "w1": I["w1"][0], "w2": I["w2"][0], "ln2_g": I["ln2_g"], "ln2_b": I["ln2_b"],
        })
    res = run_bass_kernel_spmd(nc, in_maps, core_ids=list(range(n)))
    R = res.results

    def cat(name, shape):
        return np.stack([R[c][name] for c in range(n)], 0).reshape(shape).astype(np.float32)

    y = cat("o_y", (8, S, D))
    ys = cat("o_ys", (32, TD, D))
    k = cat("o_k", (1, 8, S, 8, 64))
    v = cat("o_v", (1, 8, S, 8, 64))
    lf = cat("o_logf", (1, 8, S, 8))
    C = cat("o_C", (1, 8, 4, 128, 128))
    nn = cat("o_n", (1, 8, 4, 128))
    m = cat("o_m", (1, 8, 4))
    conv = cat("o_conv", (1, 8, 3, 1024))
    ks = cat("o_ks", (1, 32, TD, 8, 64))
    vs = cat("o_vs", (1, 32, TD, 8, 64))
    lfs = cat("o_logfs", (1, 32, TD, 8))
    Cs = cat("o_Cs", (1, 32, 4, 128, 128))
    ns = cat("o_ns", (1, 32, 4, 128))
    ms = cat("o_ms", (1, 32, 4))
    convs = cat("o_convs", (1, 32, 3, 1024))
    return (y, ys, k, v, lf, C, nn, m, conv, ks, vs, lfs, Cs, ns, ms, convs)
```

```python
import os
import numpy as np
DBG = os.environ.get('KDBG', '')
SKIP = os.environ.get('KSKIP', '')
SAME_ENG_FREE = tuple(x for x in os.environ.get('KSEF', '').split(',') if x)
from contextlib import ExitStack
import concourse.bass as bass
import concourse.mybir as mybir
from concourse.bass_utils import run_bass_kernel_spmd

F32 = mybir.dt.float32
BF16 = mybir.dt.bfloat16
I32 = mybir.dt.int32
AF = mybir.ActivationFunctionType
ALU = mybir.AluOpType
AX = mybir.AxisListType

D = 1024
S = 4096
TS = 512
NST = S // TS
KC = 8
FQ, FK, FV, FF, MQK, MV, MI, MF, MO, INC = 0, 512, 1024, 1536, 1544, 2568, 3080, 3084, 3088, 3600
DFF = 4096
ALPHA = 2.0 ** 0.25
EPS = 1e-5
NEG = -30000.0
NSQ = 4
TD = 8
NSTOK = NSQ * TD
NPG = 128
NTOT = S + NSTOK
KSCALE = 128.0 ** -0.5
QSCALE = 64.0 ** -0.5


class Buf:
    __slots__ = ("name", "w", "r", "dsem", "dcnt")

    def __init__(self, name):
        self.name = name
        self.w = None
        self.r = []
        self.dsem = None
        self.dcnt = 0


class Op:
    __slots__ = ("eng", "fn", "deps", "dma", "tok", "inc", "done")


class Prog:
    def __init__(self, nc, es):
        self.nc = nc
        self.es = es
        self.E = {"pe": nc.tensor, "act": nc.scalar, "dve": nc.vector, "pool": nc.gpsimd, "sp": nc.sync}
        self.ops = {k: [] for k in self.E}
        self.sem = {k: es.enter_context(nc.semaphore("sem_" + k)) for k in self.E}
        self.cnt = {k: 0 for k in self.E}
        self.dbufs = []
        self.nb = 0
        self.flip = 0
        self.grp = None

    def buf(self, name="b"):
        self.nb += 1
        return Buf(f"{name}_{self.nb}")

    def bufs(self, n, name="b"):
        return [self.buf(name) for _ in range(n)]

    def _add(self, o, r, w):
        deps = []

        def add(d, kind):
            if d is None or d.done:
                return
            if d.dma and o.dma and d.tok[0] is o.tok[0]:
                return
            if (not d.dma) and d.eng == o.eng and not o.dma:
                if o.eng == "pe":
                    return
                if o.eng in SAME_ENG_FREE:
                    return
            if d not in deps:
                deps.append(d)
                d.inc = True

        for b in r:
            add(b.w, "raw")
        for b in w:
            add(b.w, "waw")
            for x in b.r:
                add(x, "war")
        o.deps = deps
        o.done = False
        for b in r:
            b.r.append(o)
        for b in w:
            b.w = o
            b.r = []
        self.ops[o.eng].append(o)

    def op(self, eng, fn, r=(), w=()):
        o = Op()
        o.eng = eng
        o.dma = False
        o.inc = False
        o.fn = fn
        o.tok = None
        self._add(o, r, w)
        return o

    def dmaf(self, eng, mk, r, w, sb):
        o = Op()
        o.eng = eng
        o.dma = True
        o.inc = False
        if sb.dsem is None:
            sb.dsem = self.es.enter_context(self.nc.semaphore("d_" + sb.name))
            self.dbufs.append(sb)
        sb.dcnt += 16
        sem = sb.dsem
        o.tok = (sem, sb.dcnt)
        o.fn = lambda e: mk(e).then_inc(sem, 16)
        self._add(o, r, w)
        if self.grp is not None:
            self.grp.append((o, sb))
        return o

    def group_begin(self):
        self.grp = []

    def group_end(self):
        for (o, sb) in self.grp:
            o.tok = (sb.dsem, sb.dcnt)
        self.grp = None

    def dma(self, eng, out, in_, r=(), w=(), sb=None, **kw):
        return self.dmaf(eng, lambda e: e.dma_start(out=out, in_=in_, **kw), r, w, sb)

    def emit(self):
        lasts = []
        for k, lst in self.ops.items():
            for o in reversed(lst):
                if not o.dma and o.fn is not None:
                    o.inc = True
                    lasts.append(o)
                    break
        dtoks = []
        for b in self.dbufs:
            t = Op()
            t.dma = True
            t.tok = (b.dsem, b.dcnt)
            t.done = False
            dtoks.append(t)
        for k in self.E:
            o = Op()
            o.eng = k
            o.dma = False
            o.inc = False
            o.fn = None
            o.tok = None
            o.done = False
            o.deps = [x for x in lasts if x.eng != k] + dtoks
            self.ops[k].append(o)
        for k, lst in self.ops.items():
            for o in lst:
                if not o.dma and o.inc and o.fn is not None:
                    self.cnt[k] += 1
                    o.tok = (self.sem[k], self.cnt[k])
        prog = self
        with self.nc.Block() as block:
            def run(k):
                def f(e):
                    waited = {}
                    for o in prog.ops[k]:
                        for d in o.deps:
                            sem, val = d.tok
                            if waited.get(id(sem), 0) < val:
                                e.wait_ge(sem, val)
                                waited[id(sem)] = val
                        if o.fn is not None:
                            ins = o.fn(e)
                            if (not o.dma) and o.inc:
                                ins.then_inc(prog.sem[k], 1)
                return f
            block.tensor(run("pe"))
            block.scalar(run("act"))
            block.vector(run("dve"))
            block.gpsimd(run("pool"))
            block.sync(run("sp"))
        n = 0
        for k in self.ops:
            for o in self.ops[k]:
                o.done = True
                o.fn = None
                o.deps = None
            n += len(self.ops[k])
            self.ops[k] = []
        return n

    def mm(self, out, lhsT, rhs, start, stop, r, w):
        self.op("pe", lambda e: e.matmul(out, lhsT, rhs, start=start, stop=stop), r, w)

    def tr(self, out, in_, ident, r, w):
        self.op("pe", lambda e: e.transpose(out, in_, ident), r, w)

    def ev(self):
        self.flip ^= 1
        return "act" if self.flip else "dve"

    def cp(self, eng, out, in_, r, w):
        if eng == "act":
            self.op("act", lambda e: e.copy(out, in_), r, w)
        else:
            self.op(eng, lambda e: e.tensor_copy(out=out, in_=in_), r, w)

    def act(self, out, in_, func, r, w, bias=None, scale=None):
        kw = {}
        if bias is not None:
            kw["bias"] = bias
        if scale is not None:
            kw["scale"] = scale
        self.op("act", lambda e: e.activation(out, in_, func, **kw), r, w)

    def tt(self, eng, out, in0, in1, op, r, w):
        self.op(eng, lambda e: e.tensor_tensor(out=out, in0=in0, in1=in1, op=op), r, w)

    def ts(self, eng, out, in0, s1, s2, op0, op1, r, w):
        if s2 is None:
            self.op(eng, lambda e: e.tensor_scalar(out=out, in0=in0, scalar1=s1, scalar2=None, op0=op0), r, w)
        else:
            self.op(eng, lambda e: e.tensor_scalar(out=out, in0=in0, scalar1=s1, scalar2=s2, op0=op0, op1=op1), r, w)

    def stt(self, eng, out, in0, scalar, in1, op0, op1, r, w):
        self.op(eng, lambda e: e.scalar_tensor_tensor(out=out, in0=in0, scalar=scalar, in1=in1, op0=op0, op1=op1), r, w)

    def memset(self, eng, ap, val, w):
        self.op(eng, lambda e: e.memset(ap, val), (), w)


class PsumPool:
    def __init__(self, pg, tiles):
        self.t = tiles
        self.b = [pg.buf("ps") for _ in tiles]
        self.i = 0

    def next(self):
        i = self.i
        self.i = (i + 1) % len(self.t)
        return self.t[i], self.b[i]


class Rot:
    def __init__(self, pg, tiles, name="rot"):
        self.t = tiles
        self.b = [pg.buf(name) for _ in tiles]
        self.i = 0

    def next(self):
        i = self.i
        self.i = (i + 1) % len(self.t)
        return self.t[i], self.b[i]


def build_program(debug=False, with_cache=True):
    nc = bass.Bass("TRN2", target_bir_lowering=False)
    es = ExitStack()

    def din(name, shape, dt=F32):
        return nc.dram_tensor(name, list(shape), dt, kind="ExternalInput").ap()

    def dout(name, shape, dt=F32):
        return nc.dram_tensor(name, list(shape), dt, kind="ExternalOutput").ap()

    x_d = din("x", [S, D])
    xs_d = din("xs", [NSTOK, D])
    if with_cache:
        ckv_d = din("cache_kv", [5120 * 128, 1024])
        clf_d = din("cache_logf", [5120 * 128, 8])
    sC_d = din("state_C", [NSQ, 4, 128, 128])
    sn_d = din("state_n", [NSQ, 4, 128])
    sm_d = din("state_m", [NSQ, 4])
    sconv_d = din("state_conv", [NSQ, 3, 1024])
    pt_d = din("page_table", [NSQ, NPG], I32)
    win_d = din("w_in", [D, INC])
    bff_d = din("b_fox_f", [1, 8])
    big_d = din("b_ig", [1, 4])
    bfg_d = din("b_fg", [1, 4])
    bog_d = din("b_og", [1, 512])
    cw_d = din("conv_w", [4, 1024])
    cb_d = din("conv_b", [1, 1024])
    mnw_d = din("mlstm_norm_w", [1, 512])
    wo_d = din("w_o", [D, D])
    l1g_d = din("ln1_g", [1, D])
    l1b_d = din("ln1_b", [1, D])
    w1_d = din("w1", [D, DFF])
    w2_d = din("w2", [DFF, D])
    l2g_d = din("ln2_g", [1, D])
    l2b_d = din("ln2_b", [1, D])

    y_o = dout("o_y", [S, D])
    ys_o = dout("o_ys", [NSTOK, D])
    k_o = dout("o_k", [S, 512])
    v_o = dout("o_v", [S, 512])
    lf_o = dout("o_logf", [S, 8])
    C_o = dout("o_C", [4, 128, 128])
    n_o = dout("o_n", [4, 128])
    m_o = dout("o_m", [1, 4])
    conv_o = dout("o_conv", [3, 1024])
    ksm_o = dout("o_ks", [NSTOK, 512])
    vsm_o = dout("o_vs", [NSTOK, 512])
    lfs_o = dout("o_logfs", [NSTOK, 8])
    Cs_o = dout("o_Cs", [NSQ, 4, 128, 128])
    ns_o = dout("o_ns", [NSQ, 4, 128])
    ms_o = dout("o_ms", [NSQ, 4])
    convs_o = dout("o_convs", [NSQ, 3, 1024])

    mix_d = nc.dram_tensor("mix_scratch", [128, 8, NTOT], BF16, kind=("ExternalOutput" if debug else "Internal")).ap()
    mix_db = None

    pg = Prog(nc, es)
    mix_db = pg.buf("mixd")

    def sb(name, shape, dt=F32, stack=None):
        return (stack or es).enter_context(nc.sbuf_tensor(name, list(shape), dt))

    ident_f = sb("ident_f", [128, 128])
    ident_b = sb("ident_b", [128, 128], BF16)
    mask01 = sb("mask01", [128, 128])
    ones_f = sb("ones_f", [128, 128])
    cB = pg.buf("const")

    def mk_consts():
        pg.memset("pool", ident_f[:], 0.0, [cB])
        pg.op("pool", lambda e: e.affine_select(out=ident_f[:], in_=ident_f[:], pattern=[[-1, 128]],
                                                 compare_op=ALU.not_equal, fill=1.0, base=0, channel_multiplier=1),
              [cB], [cB])
        pg.cp("pool", ident_b[:], ident_f[:], [cB], [cB])
        pg.memset("pool", ones_f[:], 1.0, [cB])
        pg.memset("pool", mask01[:], 1.0, [cB])
        pg.op("pool", lambda e: e.affine_select(out=mask01[:], in_=mask01[:], pattern=[[1, 128]],
                                                 compare_op=ALU.is_ge, fill=0.0, base=0, channel_multiplier=-1),
              [cB], [cB])

    mk_consts()


    def load_xT(src_d, row0, T, col0, xs_rot, psT, xT, xTb):
        xt, xb = xs_rot.next()
        pg.dma("sp", xt[0:T, :], src_d[row0:row0 + T, :], [], [xb], sb=xb)
        for g in range(2):
            pt, pb = psT.next()
            for cc in range(4):
                c = g * 4 + cc
                pg.tr(pt[:, cc * 128:cc * 128 + T], xt[0:T, c * 128:(c + 1) * 128], ident_f[0:T, 0:T], [xb, cB], [pb])
            pg.cp(pg.ev(), xT[:, g * 4:g * 4 + 4, col0:col0 + T],
                  pt[:, :].rearrange("p (c t) -> p c t", c=4)[:, :, 0:T], [pb], [xTb])
        return xt, xb

    LSL = int(os.environ.get("LSL", "9"))

    def log_sigmoid(out, src, a, b, tb, r, w):
        if LSL >= 1:
            pg.stt("dve", a, src, -1.0, src, ALU.mult, ALU.max, r, [tb])
        if LSL >= 2:
            pg.act(a, a, AF.Exp, [tb], [tb], scale=-1.0)
        if LSL >= 3:
            pg.act(a, a, AF.Ln, [tb], [tb], bias=one_c[0:a.shape[0], 0:1])
        if LSL >= 4:
            pg.ts("dve", b, src, 0.0, None, ALU.min, None, r, [tb])
        if LSL >= 5:
            pg.tt("dve", out, b, a, ALU.subtract, [tb], w)

    one_c = sb("one_c", [128, 1])
    eps_c = sb("eps_c", [128, 1])
    pg.memset("pool", one_c[:], 1.0, [cB])
    pg.memset("pool", eps_c[:], EPS, [cB])

    esA = ExitStack()

    def sA(name, shape, dt=F32):
        return esA.enter_context(nc.sbuf_tensor(name, list(shape), dt))

    def pA(name, dt=F32, n=512):
        return esA.enter_context(nc.psum_tensor(name, [128, n], dt))

    winA = sA("winA", [128, KC, 1544], BF16)
    winA_b = pg.buf("winA")
    wv = win_d.rearrange("(c p) n -> p c n", p=128)
    for c in range(KC):
        pg.dma("pool", winA[:, c, :], wv[:, c, 0:1544], [], [winA_b], sb=winA_b)
    ka = [sA(f"ka{h}", [70, S], BF16) for h in range(8)]
    ka_b = [[pg.buf("ka") for j in range(NST)] for h in range(8)]
    qa = [sA(f"qa{h}", [70, TS], BF16) for h in range(8)]
    qa_b = [pg.buf("qa") for h in range(8)]
    Vst = sA("Vst", [128, 32, 8, 128], BF16)
    Vst_b = [pg.buf("V") for t in range(32)]
    maskb = sA("maskb", [128, 4, 512], BF16)
    xsA = Rot(pg, [sA(f"xsA{i}", [128, D]) for i in range(1)], "xsA")
    xT_t = [sA(f"xTA{i}", [128, KC, TS], BF16) for i in range(1)] * 2
    xT_bs = [pg.buf("xT")] * 2
    stg = Rot(pg, [sA(f"stgA{i}", [128, 512]) for i in range(2)], "stgA")
    stgf = Rot(pg, [sA(f"stgfA{i}", [128, 8]) for i in range(2)], "stgfA")
    tmpA = sA("tmpA", [128, 512])
    tmpB = sA("tmpB", [128, 512])
    maskf = tmpA
    tmp_b = pg.buf("tmpA")
    tmpC = tmpA[8:16, :] if False else sA("tmpC", [8, 512])
    Lt = sA("Lt", [8, TS])
    Lcar = sA("Lcar", [8, 1])
    LP = sA("LP", [8, 3, TS], BF16)
    LN = sA("LN", [8, 3, TS], BF16)
    Lres = sA("Lres", [8, TS])
    L_b = pg.buf("L")
    semQA = pg.buf("semQA")
    semKA = pg.buf("semKA")
    bff_t = sA("bff_t", [128, 8])
    bff_c = sA("bff_c", [8, 1])
    PT = Rot(pg, [sA(f"PT{i}", [128, TS], BF16) for i in range(2)], "PT")
    rl = sA("rl", [64, TS])
    rl_b = pg.buf("rl")
    foxT = Rot(pg, [sA(f"foxT{i}", [128, 4, TS], BF16) for i in range(1)], "foxT")
    ps_mm = PsumPool(pg, [pA(f"psA_mm{i}") for i in range(2)])
    ps_T = PsumPool(pg, [pA(f"psA_T{i}") for i in range(1)])
    ps_s = PsumPool(pg, [pA(f"psA_s{i}") for i in range(3)])
    ps_o = PsumPool(pg, [pA(f"psA_o{i}") for i in range(2)])

    pg.dma("sp", bff_t[:, :], bff_d[0:1, :].partition_broadcast(128), [], [cB], sb=cB)
    pg.dma("sp", bff_c[:, :], bff_d.rearrange("o h -> h o"), [], [cB], sb=cB, allow_slow_non_contiguous=True)
    for h in range(8):
        pg.memset("pool", ka[h][64:70, :], 1.0, [ka_b[h][j] for j in range(NST)])
        pg.memset("pool", qa[h][64:70, :], 1.0, [qa_b[h]])
    pg.memset("pool", Vst[:, :, :, 64:128], 1.0, Vst_b)
    pg.memset("pool", Lcar[:], 0.0, [L_b])
    for r_ in range(4):
        pg.memset("pool", maskf[:], 0.0, [tmp_b])
        pg.op("pool", lambda e, r_=r_: e.affine_select(out=maskf[:], in_=maskf[:], pattern=[[1, 512]],
                                                        compare_op=ALU.is_ge, fill=NEG, base=-128 * r_,
                                                        channel_multiplier=-1), [tmp_b], [tmp_b])
        pg.cp("pool", maskb[:, r_, :], maskf[:], [tmp_b], [cB])

    def vaug(t, h):
        return Vst[:, t, h, :]

    for j in range((NST if 'j1' not in DBG else (0 if 'noloop' in DBG else 1)) if 'A' not in SKIP else 0):
        xT = xT_t[j % 2]
        xTb = xT_bs[j % 2]
        for tt in range(4):
            load_xT(x_d, j * TS + tt * 128, 128, tt * 128, xsA, ps_T, xT, xTb)
        pt, pb = ps_mm.next()
        for c in range(KC):
            pg.mm(pt[0:8, :], winA[:, c, FF:FF + 8], xT[:, c, :], c == 0, c == KC - 1, [xTb, winA_b], [pb])
        pg.ts("dve", tmpB[0:8, :], pt[0:8, :], bff_c[0:8, 0:1], None, ALU.add, None, [pb, cB], [tmp_b])
        log_sigmoid(Lres[0:8, :], tmpB[0:8, :], tmpA[0:8, :], tmpC[0:8, :], tmp_b, [tmp_b], [L_b])
        pg.op("dve", lambda e: e.tensor_tensor_scan(out=Lt[0:8, :], data0=one_c[0:8, 0:1].broadcast_to([8, TS]),
                                                    data1=Lres[0:8, :], initial=Lcar[0:8, 0:1], op0=ALU.mult, op1=ALU.add),
              [L_b, cB], [L_b])
        pg.cp("dve", Lcar[0:8, 0:1], Lt[0:8, TS - 1:TS], [L_b], [L_b])
        pg.cp("dve", LP[0:8, 0, :], Lt[0:8, :], [L_b], [L_b])
        pg.tt("dve", Lres[0:8, :], Lt[0:8, :], LP[0:8, 0, :], ALU.subtract, [L_b], [L_b])
        pg.cp("dve", LP[0:8, 1, :], Lres[0:8, :], [L_b], [L_b])
        pg.tt("dve", Lres[0:8, :], Lres[0:8, :], LP[0:8, 1, :], ALU.subtract, [L_b], [L_b])
        pg.cp("dve", LP[0:8, 2, :], Lres[0:8, :], [L_b], [L_b])
        pg.ts("dve", LN[0:8, :, :], LP[0:8, :, :], -1.0, None, ALU.mult, None, [L_b], [L_b])
        pg.group_begin()
        for h in range(0 if 'noL' not in DBG else 99, 8):
            pg.dma("sp", qa[h][64:67, :], LP[h:h + 1, :, :], [L_b], [qa_b[h]], sb=semQA)
            pg.dma("sp", ka[h][67:70, j * TS:(j + 1) * TS], LN[h:h + 1, :, :], [L_b], [ka_b[h][j]], sb=semQA)
        pg.group_end()
        for tt in range(4 if 'notok' not in DBG else 0):
            t = j * 4 + tt
            for (c0, kind) in ((FK, "k"), (FV, "v")):
                pt, pb = ps_mm.next()
                for c in range(KC):
                    pg.mm(pt[:, :], xT[:, c, tt * 128:(tt + 1) * 128], winA[:, c, c0:c0 + 512], c == 0, c == KC - 1,
                          [xTb, winA_b], [pb])
                st, stb = stg.next()
                pg.cp("dve" if kind == "k" else "act", st[:, :], pt[:, :], [pb], [stb])
                od = k_o if kind == "k" else v_o
                if "nodma" not in DBG:
                    pg.dma("sp", od[t * 128:(t + 1) * 128, :], st[:, :], [stb], [], sb=stb)
                if kind == "v" and "nov" not in DBG:
                    if True:
                        pg.cp("dve", Vst[:, t, :, 0:64], st[:, :].rearrange("p (h d) -> p h d", h=8), [stb], [Vst_b[t]])
                    else:
                        pg.cp("act" if "vact" in DBG else "dve", Vst[:, t, :, 0:64], pt[:, :].rearrange("p (h d) -> p h d", h=8), [pb], [Vst_b[t]])
            if 'skipf' in DBG:
                continue
            pt, pb = ps_mm.next()
            for c in range(KC):
                pg.mm(pt[:, 0:8], xT[:, c, tt * 128:(tt + 1) * 128], winA[:, c, FF:FF + 8], c == 0, c == KC - 1,
                      [xTb, winA_b], [pb])
            sf, sfb = stgf.next()
            pg.tt("dve", tmpA[:, 0:8], pt[:, 0:8], bff_t[:, :], ALU.add, [pb, cB], [tmp_b])
            log_sigmoid(sf[:, :], tmpA[:, 0:8], tmpA[:, 8:16], tmpA[:, 16:24], tmp_b, [tmp_b], [sfb])
            if "nolfdma" not in DBG:
                pg.dma("sp", lf_o[t * 128:(t + 1) * 128, :], sf[:, :], [sfb], [], sb=sfb)
        if 'nofm' in DBG:
            continue
        for pp in range(4):
            pt, pb = ps_mm.next()
            for c in range(KC):
                pg.mm(pt[:, :], winA[:, c, FQ + pp * 128:FQ + (pp + 1) * 128], xT[:, c, :], c == 0, c == KC - 1,
                      [xTb, winA_b], [pb])
            pg.op("act", lambda e, pt=pt, pp=pp: e.mul(qa[2 * pp][0:64, :], pt[0:64, :], QSCALE), [pb], [qa_b[2 * pp]])
            pg.ts("dve", qa[2 * pp + 1][0:64, :], pt[64:128, :], QSCALE, None, ALU.mult, None, [pb], [qa_b[2 * pp + 1]])
            pt, pb = ps_mm.next()
            for c in range(KC):
                pg.mm(pt[:, :], winA[:, c, FK + pp * 128:FK + (pp + 1) * 128], xT[:, c, :], c == 0, c == KC - 1,
                      [xTb, winA_b], [pb])
            pg.cp("dve", ka[2 * pp][0:64, j * TS:(j + 1) * TS], pt[0:64, :], [pb], [ka_b[2 * pp][j]])
            pg.cp("dve", ka[2 * pp + 1][0:64, j * TS:(j + 1) * TS], pt[64:128, :], [pb], [ka_b[2 * pp + 1][j]])
        fx, fxb = foxT.next()
        for h in range(0 if 'noattn' not in DBG else 99, 8):
            po, pob = ps_o.next()
            nk = 4 * j + 4
            pend = []

            def do_pv(i, ptile, ptb, po=po, pob=pob, h=h, nk=nk):
                pg.mm(po[:, :], vaug(i, h), ptile[:, :], i == 0, i == nk - 1, [Vst_b[i], ptb], [pob])

            for i in range(nk):
                r_ = i - 4 * j
                pss, psb = ps_s.next()
                pg.mm(pss[:, :], ka[h][0:70, i * 128:(i + 1) * 128], qa[h][0:70, :], True, r_ < 0,
                      [ka_b[h][i // 4], qa_b[h]], [psb])
                if r_ >= 0:
                    pg.mm(pss[:, :], ident_b[:, :], maskb[:, r_, :], False, True, [cB], [psb])
                ptile, ptb = PT.next()
                pg.act(ptile[:, :], pss[:, :], AF.Exp, [psb], [ptb])
                pend.append((i, ptile, ptb))
                if len(pend) > 1:
                    do_pv(*pend.pop(0))
            while pend:
                do_pv(*pend.pop(0))
            pg.op("dve", lambda e, po=po: e.reciprocal(rl[0:64, :], po[64:128, :]), [pob], [rl_b])
            pg.tt("dve", fx[(h % 2) * 64:(h % 2) * 64 + 64, h // 2, :], po[0:64, :], rl[0:64, :], ALU.mult,
                  [pob, rl_b], [fxb])
        pg.dma("sp", mix_d[:, 0:4, j * TS:(j + 1) * TS], fx[:, :, :], [fxb], [mix_db], sb=fxb)
    nA = pg.emit()
    esA.close()

    woS = sb("woS", [128, KC, D], BF16)
    w1a = sb("w1a", [128, KC, DFF // 2], BF16)
    wo_b, w1_b, w2_b = pg.buf("wo"), pg.buf("w1"), pg.buf("w2")
    wov = wo_d.rearrange("(c p) n -> p c n", p=128)
    w1v = w1_d.rearrange("(c p) n -> p c n", p=128)
    w2v = w2_d.rearrange("(c p) n -> p c n", p=128)
    esB = ExitStack()

    def sB(name, shape, dt=F32):
        return esB.enter_context(nc.sbuf_tensor(name, list(shape), dt))

    def pB(name, dt=F32, n=512):
        return esB.enter_context(nc.psum_tensor(name, [128, n], dt))

    NB = INC - MQK
    winB = sB("winB", [128, KC, NB], BF16)
    winB_b = pg.buf("winB")
    for c in range(KC):
        pg.dma("pool", winB[:, c, 0:1024], wv[:, c, MQK:MQK + 1024], [], [winB_b], sb=winB_b)
        pg.dma("pool", winB[:, c, 1024:NB], wv[:, c, MQK + 1024:INC], [], [winB_b], sb=winB_b)
    for c in range(KC):
        pg.dma("pool", woS[:, c, :], wov[:, c, :], [], [wo_b], sb=wo_b)
    for c in range(KC):
        pg.dma("pool", w1a[:, c, :], w1v[:, c, 0:2048], [], [w1_b], sb=w1_b)
    cQK, cV, cI, cF, cO = 0, MV - MQK, MI - MQK, MF - MQK, MO - MQK
    xsB = Rot(pg, [sB(f"xsB{i}", [128, D]) for i in range(1)], "xsB")
    xTB = sB("xTB", [128, KC, TS], BF16)
    xTB_b = pg.buf("xTB")
    zc = [sB(f"zc{b}", [128, 3 + TS]) for b in range(8)]
    zc_b = [pg.buf("zc") for b in range(8)]
    qkT = [sB(f"qkT{b}", [128, TS], BF16) for b in range(8)]
    qkT_b = [pg.buf("qkT") for b in range(8)]
    cacc = sB("cacc", [128, TS])
    csig = sB("csig", [128, TS])
    cacc_b = pg.buf("cacc")
    cwT = sB("cwT", [128, 8, 4])
    cbT = sB("cbT", [128, 8])
    big_c = sB("big_c", [4, 1])
    bfg_c = sB("bfg_c", [4, 1])
    bog_t = sB("bog_t", [128, 512])
    mnw_t = sB("mnw_t", [128, 512])
    gB = pg.buf("gates")
    semOB = pg.buf("semOB")
    g_ig = sB("g_ig", [4, TS])
    g_lf = sB("g_lf", [4, TS])
    g_B = sB("g_B", [4, TS])
    g_u = sB("g_u", [4, TS])
    g_M = sB("g_M", [4, TS])
    g_t1 = sB("g_t1", [4, TS])
    g_t2 = sB("g_t2", [4, TS])
    g_wk = sB("g_wk", [4, TS])
    g_fl = sB("g_fl", [4, TS])
    g_car = sB("g_car", [4, 4])
    g_Rc = sB("g_Rc", [4, 4])
    g_Rp = sB("g_Rp", [4, 4])
    g_g = sB("g_g", [4, 4])
    g_gd = sB("g_gd", [4, 16])
    g_m = sB("g_m", [4, 1])
    tsc = sB("tsc", [128, 4, 8])
    tsc_b = pg.buf("tsc")
    gb = sB("gb", [128, 16])
    gb_b = pg.buf("gb")
    vaugs = Rot(pg, [sB(f"vaug{i}", [128, 4, 130], BF16) for i in range(2)], "vaug")
    ogs = Rot(pg, [sB(f"og{i}", [128, 512]) for i in range(1)], "og")
    ogtmp = sB("ogtmp", [128, 512])
    ogtmp_b = pg.buf("ogtmp")
    ktoks = Rot(pg, [sB(f"ktok{i}", [128, 4, 128], BF16) for i in range(2)], "ktok")
    ATs = Rot(pg, [sB(f"AT{i}", [128, 128], BF16) for i in range(2)], "AT")
    qgs = Rot(pg, [sB(f"qg{i}", [128, 128], BF16) for i in range(2)], "qg")
    numS = Rot(pg, [sB(f"numS{i}", [128, 130]) for i in range(2)], "numS")
    Cf = [sB(f"Cf{h}", [128, 130]) for h in range(4)]
    Cb = [sB(f"Cb{h}", [128, 130], BF16) for h in range(4)]
    Cf_b = [pg.buf("Cf") for h in range(4)]
    Cb_b = [pg.buf("Cb") for h in range(4)]
    hbufs = Rot(pg, [sB(f"hbuf{i}", [128, 512]) for i in range(2)], "hbuf")
    hsm = sB("hsm", [128, 8])
    hsm_b = pg.buf("hsm")
    stats = sB("stats", [128, 4, 6])
    mvs = sB("mvs", [128, 4, 4])
    mnb = Rot(pg, [sB(f"mnb{i}", [128, 512], BF16) for i in range(2)], "mnb")
    mnf = sB("mnf", [128, 512])
    mnf_b = pg.buf("mnf")
    mnT = Rot(pg, [sB(f"mnT{i}", [128, 4, TS], BF16) for i in range(2)], "mnT")
    psB_mm = PsumPool(pg, [pB(f"psB_mm{i}") for i in range(2)])
    psB_T = PsumPool(pg, [pB(f"psB_T{i}") for i in range(1)])
    psB_g = PsumPool(pg, [pB("psB_g")])
    psB_b = PsumPool(pg, [pB("psB_b", BF16, 1024)])
    psB_s = PsumPool(pg, [pB(f"psB_s{i}") for i in range(2)])
    psB_u = PsumPool(pg, [pB("psB_u")])

    for jj in range(4):
        pg.dma("sp", cwT[:, :, jj], cw_d[jj:jj + 1, :].rearrange("o (b p) -> p (o b)", p=128), [], [cB], sb=cB,
               allow_slow_non_contiguous=True)
    pg.dma("sp", cbT[:, :], cb_d.rearrange("o (b p) -> p (o b)", p=128), [], [cB], sb=cB, allow_slow_non_contiguous=True)
    pg.dma("sp", big_c[:, :], big_d.rearrange("o h -> h o"), [], [cB], sb=cB, allow_slow_non_contiguous=True)
    pg.dma("sp", bfg_c[:, :], bfg_d.rearrange("o h -> h o"), [], [cB], sb=cB, allow_slow_non_contiguous=True)
    pg.dma("sp", bog_t[:, :], bog_d[0:1, :].partition_broadcast(128), [], [cB], sb=cB)
    pg.dma("sp", mnw_t[:, :], mnw_d[0:1, :].partition_broadcast(128), [], [cB], sb=cB)
    for b in range(8):
        pg.memset("pool", zc[b][:, 0:3], 0.0, [zc_b[b]])
    pg.memset("pool", g_car[:, :], 0.0, [gB])
    for hh in range(2):
        vt = vaugs.t[hh]
        pg.memset("pool", vt[:, :, :], 0.0, [vaugs.b[hh]])

    for j in range((NST if 'j1' not in DBG else 1) if 'B' not in SKIP else 0):
        for tt in range(4):
            load_xT(x_d, j * TS + tt * 128, 128, tt * 128, xsB, psB_T, xTB, xTB_b)
        pgt, pgb = psB_g.next()
        for c in range(KC):
            pg.mm(pgt[0:4, :], winB[:, c, cI:cI + 4], xTB[:, c, :], c == 0, c == KC - 1, [xTB_b, winB_b], [pgb])
        pg.ts("dve", g_ig[:, :], pgt[0:4, :], big_c[0:4, 0:1], None, ALU.add, None, [pgb, cB], [gB])
        pgt, pgb = psB_g.next()
        for c in range(KC):
            pg.mm(pgt[0:4, :], winB[:, c, cF:cF + 4], xTB[:, c, :], c == 0, c == KC - 1, [xTB_b, winB_b], [pgb])
        pg.ts("dve", g_t1[:, :], pgt[0:4, :], bfg_c[0:4, 0:1], None, ALU.add, None, [pgb, cB], [gB])
        log_sigmoid(g_lf[:, :], g_t1[:, :], g_t2[:, :], g_M[:, :], gB, [gB], [gB])
        pg.op("dve", lambda e: e.tensor_tensor_scan(out=g_B[:, :], data0=one_c[0:4, 0:1].broadcast_to([4, TS]), data1=g_lf[:, :],
                                                    initial=g_car[0:4, 0:1], op0=ALU.mult, op1=ALU.add), [gB, cB], [gB])
        pg.tt("dve", g_u[:, :], g_ig[:, :], g_B[:, :], ALU.subtract, [gB], [gB])
        pg.op("dve", lambda e: e.tensor_tensor_scan(out=g_M[:, :], data0=one_c[0:4, 0:1].broadcast_to([4, TS]), data1=g_u[:, :],
                                                    initial=g_car[0:4, 1:2], op0=ALU.mult, op1=ALU.max), [gB, cB], [gB])
        pg.cp("dve", g_Rp[:, 0:1], g_car[:, 1:2], [gB], [gB])
        pg.cp("dve", g_Rc[:, :], g_M[:, :].rearrange("p (c t) -> p c t", c=4)[:, :, 127], [gB], [gB])
        pg.cp("dve", g_Rp[:, 1:4], g_Rc[:, 0:3], [gB], [gB])
        pg.cp("dve", g_car[:, 0:1], g_B[:, TS - 1:TS], [gB], [gB])
        pg.cp("dve", g_car[:, 1:2], g_M[:, TS - 1:TS], [gB], [gB])
        Rb = g_Rc[:, :].unsqueeze(2).broadcast_to([4, 4, 128])
        pg.tt("dve", g_t1[:, :].rearrange("p (c t) -> p c t", c=4), g_u[:, :].rearrange("p (c t) -> p c t", c=4), Rb,
              ALU.subtract, [gB], [gB])
        pg.act(g_wk[:, :], g_t1[:, :], AF.Exp, [gB], [gB])
        pg.tt("dve", g_t2[:, :].rearrange("p (c t) -> p c t", c=4), g_B[:, :].rearrange("p (c t) -> p c t", c=4), Rb,
              ALU.add, [gB], [gB])
        pg.act(g_fl[:, :], g_t2[:, :], AF.Exp, [gB], [gB], scale=-1.0)
        pg.tt("dve", g_g[:, :], g_Rp[:, :], g_Rc[:, :], ALU.subtract, [gB], [gB])
        pg.act(g_g[:, :], g_g[:, :], AF.Exp, [gB], [gB])
        for h2 in range(4):
            pg.ts("dve", g_gd[:, h2 * 4:(h2 + 1) * 4], g_g[:, :], ident_f[0:4, h2:h2 + 1], None, ALU.mult, None, [gB, cB], [gB])
        pgt, pgb = psB_g.next()
        for cc in range(4):
            pg.tr(pgt[:, cc * 8:cc * 8 + 4], g_wk[0:4, cc * 128:(cc + 1) * 128], ident_f[0:4, 0:4], [gB, cB], [pgb])
            pg.tr(pgt[:, cc * 8 + 4:cc * 8 + 8], g_fl[0:4, cc * 128:(cc + 1) * 128], ident_f[0:4, 0:4], [gB, cB], [pgb])
        pg.mm(pgt[:, 64:80], ones_f[0:4, 0:128], g_gd[0:4, 0:16], True, True, [gB, cB], [pgb])
        pg.cp("dve", tsc[:, :, :], pgt[:, 0:32].rearrange("p (c k) -> p c k", c=4), [pgb], [tsc_b])
        pg.cp("dve", gb[:, :], pgt[:, 64:80], [pgb], [gb_b])
        for blk in range(8):
            pt, pb = psB_mm.next()
            for c in range(KC):
                pg.mm(pt[:, :], winB[:, c, cQK + blk * 128:cQK + (blk + 1) * 128], xTB[:, c, :], c == 0, c == KC - 1,
                      [xTB_b, winB_b], [pb])
            pg.cp("act", zc[blk][:, 3:3 + TS], pt[:, :], [pb], [zc_b[blk]])
            pg.ts("dve", cacc[:, :], zc[blk][:, 3:3 + TS], cwT[:, blk, 3:4], cbT[:, blk:blk + 1], ALU.mult, ALU.add,
                  [zc_b[blk], cB], [cacc_b])
            for jj in (2, 1, 0):
                pg.stt("dve", cacc[:, :], zc[blk][:, jj:jj + TS], cwT[:, blk, jj:jj + 1], cacc[:, :], ALU.mult, ALU.add,
                       [zc_b[blk], cB, cacc_b], [cacc_b])
            pg.act(csig[:, :], cacc[:, :], AF.Sigmoid, [cacc_b], [cacc_b])
            pg.stt("dve", qkT[blk][:, :], cacc[:, :], (1.0 if blk < 4 else KSCALE), csig[:, :], ALU.mult, ALU.mult,
                   [cacc_b], [qkT_b[blk]])
            pg.cp("act", zc[blk][:, 0:3], zc[blk][:, TS:TS + 3], [zc_b[blk]], [zc_b[blk]])
            if j == NST - 1:
                pg.dma("sp", conv_o[:, blk * 128:(blk + 1) * 128].rearrange("t f -> f t"), zc[blk][:, 0:3], [zc_b[blk]], [],
                       sb=semOB, allow_slow_non_contiguous=True)
        mt, mtb = mnT.next()
        for tt in range(4):
            t = j * 4 + tt
            sl = slice(tt * 128, (tt + 1) * 128)
            first = (t == 0)
            va, vab = vaugs.next()
            pt, pb = psB_mm.next()
            for c in range(KC):
                pg.mm(pt[:, :], xTB[:, c, sl], winB[:, c, cV:cV + 512], c == 0, c == KC - 1, [xTB_b, winB_b], [pb])
            for hm in range(4):
                pg.ts("dve", va[:, hm, 0:128], pt[:, hm * 128:(hm + 1) * 128], tsc[:, tt, hm:hm + 1], None, ALU.mult, None,
                      [pb, tsc_b], [vab])
            pg.cp("dve", va[:, :, 128:129], tsc[:, tt, 0:4].unsqueeze(2), [tsc_b], [vab])
            ogt, ogb = ogs.next()
            pt, pb = psB_mm.next()
            for c in range(KC):
                pg.mm(pt[:, :], xTB[:, c, sl], winB[:, c, cO:cO + 512], c == 0, c == KC - 1, [xTB_b, winB_b], [pb])
            pg.tt("dve", ogtmp[:, :], pt[:, :], bog_t[:, :], ALU.add, [pb, cB], [ogtmp_b])
            pg.act(ogt[:, :], ogtmp[:, :], AF.Sigmoid, [ogtmp_b], [ogb])
            kt, ktb = ktoks.next()
            pbt, pbb = psB_b.next()
            for hm in range(4):
                pg.tr(pbt[:, hm * 128:(hm + 1) * 128], qkT[4 + hm][:, sl], ident_b[:, :], [qkT_b[4 + hm], cB], [pbb])
            pg.cp("act", kt[:, :, :], pbt[:, 0:512].rearrange("p (h d) -> p h d", h=4), [pbb], [ktb])
            hb, hbb = hbufs.next()
            stS = {}

            def partS(hm, tt=tt, sl=sl, first=first):
                gcol = gb[:, hm * 4 + tt:hm * 4 + tt + 1]
                pss, psb = psB_s.next()
                pg.mm(pss[:, 0:128], qkT[4 + hm][:, sl], qkT[hm][:, sl], True, True, [qkT_b[4 + hm], qkT_b[hm]], [psb])
                at, atb = ATs.next()
                pg.tt("dve", at[:, :], pss[:, 0:128], mask01[:, :], ALU.mult, [psb, cB], [atb])
                qgt = qgb = None
                if not first:
                    qgt, qgb = qgs.next()
                    pg.ts("dve", qgt[:, :], qkT[hm][:, sl], gcol, None, ALU.mult, None, [qkT_b[hm], gb_b], [qgb])
                stS[hm] = (pss, psb, at, atb, qgt, qgb, gcol)

            def partN(hm, tt=tt, first=first, va=va, vab=vab, kt=kt, ktb=ktb, hb=hb, hbb=hbb):
                pss, psb, at, atb, qgt, qgb, gcol = stS.pop(hm)
                pg.mm(pss[:, 256:385], at[:, :], va[:, hm, 0:129], True, first, [atb, vab], [psb])
                if not first:
                    pg.mm(pss[:, 256:385], qgt[:, :], Cb[hm][:, 0:129], False, True, [qgb, Cb_b[hm]], [psb])
                ns_, nsb = numS.next()
                pg.cp("act", ns_[:, 0:129], pss[:, 256:385], [psb], [nsb])
                psu, pub = psB_u.next()
                pg.mm(psu[:, 0:129], kt[:, hm, :], va[:, hm, 0:129], True, True, [ktb, vab], [pub])
                if first:
                    pg.cp("dve", Cf[hm][:, 0:129], psu[:, 0:129], [pub], [Cf_b[hm]])
                else:
                    pg.stt("dve", Cf[hm][:, 0:129], Cf[hm][:, 0:129], gcol, psu[:, 0:129], ALU.mult, ALU.add,
                           [pub, gb_b, Cf_b[hm]], [Cf_b[hm]])
                pg.cp("act", Cb[hm][:, 0:129], Cf[hm][:, 0:129], [Cf_b[hm]], [Cb_b[hm]])
                pg.stt("dve", hsm[:, 0:1], ns_[:, 128:129], -1.0, ns_[:, 128:129], ALU.mult, ALU.max, [nsb], [hsm_b])
                pg.tt("dve", hsm[:, 1:2], hsm[:, 0:1], tsc[:, tt, 4 + hm:5 + hm], ALU.max, [hsm_b, tsc_b], [hsm_b])
                pg.op("dve", lambda e: e.reciprocal(hsm[:, 2:3], hsm[:, 1:2]), [hsm_b], [hsm_b])
                pg.ts("dve", hb[:, hm * 128:(hm + 1) * 128], ns_[:, 0:128], hsm[:, 2:3], None, ALU.mult, None,
                      [nsb, hsm_b], [hbb])

            partS(0)
            for hm in range(4):
                if hm + 1 < 4:
                    partS(hm + 1)
                partN(hm)
            for hm in range(4):
                pg.op("dve", lambda e, hm=hm, hb=hb: e.bn_stats(stats[:, hm, :], hb[:, hm * 128:(hm + 1) * 128]), [hbb], [hsm_b])
                pg.op("dve", lambda e, hm=hm: e.bn_aggr(mvs[:, hm, 0:2], stats[:, hm, :]), [hsm_b], [hsm_b])
            pg.act(mvs[:, :, 2:3], mvs[:, :, 1:2], AF.Ln, [hsm_b], [hsm_b], bias=eps_c[:, 0:1])
            pg.act(mvs[:, :, 2:3], mvs[:, :, 2:3], AF.Exp, [hsm_b], [hsm_b], scale=-0.5)
            for hm in range(4):
                pg.ts("dve", mnf[:, hm * 128:(hm + 1) * 128], hb[:, hm * 128:(hm + 1) * 128], mvs[:, hm, 0:1], mvs[:, hm, 2:3],
                      ALU.subtract, ALU.mult, [hbb, hsm_b], [mnf_b])
            pg.tt("pool", mnf[:, :], mnf[:, :], mnw_t[:, :], ALU.mult, [mnf_b, cB], [mnf_b])
            mb, mbb = mnb.next()
            pg.tt("pool", mb[:, :], mnf[:, :], ogt[:, :], ALU.mult, [mnf_b, ogb], [mbb])
            pbt, pbb = psB_b.next()
            for hm in range(4):
                pg.tr(pbt[:, hm * 128:(hm + 1) * 128], mb[:, hm * 128:(hm + 1) * 128], ident_b[:, :], [mbb, cB], [pbb])
            pg.cp("act", mt[:, :, sl], pbt[:, 0:512].rearrange("p (h d) -> p h d", h=4), [pbb], [mtb])
        pg.dma("sp", mix_d[:, 4:8, j * TS:(j + 1) * TS], mt[:, :, :], [mtb], [mix_db], sb=mtb)
    for hm in range(4):
        pg.dma("sp", C_o[hm, :, :], Cf[hm][:, 0:128], [Cf_b[hm]], [], sb=semOB)
        pg.dma("sp", n_o[hm:hm + 1, :].rearrange("o d -> d o"), Cf[hm][:, 128:129], [Cf_b[hm]], [], sb=semOB,
               allow_slow_non_contiguous=True)
    pg.tt("dve", g_m[:, :], g_car[:, 0:1], g_car[:, 1:2], ALU.add, [gB], [gB])
    pg.dma("sp", m_o.rearrange("o h -> h o"), g_m[:, :], [gB], [], sb=semOB, allow_slow_non_contiguous=True)
    nB = pg.emit()
    esB.close()

    if 'nosample' not in DBG:
        esS = ExitStack()

        def sS(name, shape, dt=F32):
            return esS.enter_context(nc.sbuf_tensor(name, list(shape), dt))

        def pS(name, dt=F32, n=512):
            return esS.enter_context(nc.psum_tensor(name, [128, n], dt))

        NT = NSTOK
        winS = sS("winS", [128, KC, INC], BF16)
        winS_b = pg.buf("winS")
        for c in range(KC):
            for (a0, a1) in ((0, 1544), (1544, 2568), (2568, INC)):
                pg.dma("pool", winS[:, c, a0:a1], wv[:, c, a0:a1], [], [winS_b], sb=winS_b)
        semS = pg.buf("semS")
        semSo = pg.buf("semSo")
        xsS = Rot(pg, [sS("xsS0", [128, D])], "xsS")
        xsT = sS("xsT", [128, KC, NT], BF16)
        xsT_b = pg.buf("xsT")
        psS_mm = PsumPool(pg, [pS("psS_mm0")])
        psS_T = PsumPool(pg, [pS("psS_T0")])
        psS_g = psS_T
        psS_b = PsumPool(pg, [pS(f"psS_b{i}", BF16, 1024) for i in range(2)])
        psS_s = PsumPool(pg, [pS(f"psS_s{i}") for i in range(2)])
        psS_o = PsumPool(pg, [pS("psS_o")])
        psS_l = PsumPool(pg, [pS("psS_l")])
        sc = pg.buf("sconst")
        bffS = sS("bffS", [128, 8])
        bogS = sS("bogS", [128, 512])
        mnwS = sS("mnwS", [128, 512])
        cwS = sS("cwS", [128, 8, 4])
        cbS = sS("cbS", [128, 8])
        bigS = sS("bigS", [4, 1])
        bfgS = sS("bfgS", [4, 1])
        smS = sS("smS", [4, NSQ])
        maskU = sS("maskU", [128, 128])
        ones_b = sS("ones_b", [128, 1], BF16)
        pg.group_begin()
        pg.dma("sp", bffS[:, :], bff_d[0:1, :].partition_broadcast(128), [], [sc], sb=semS)
        pg.dma("sp", bogS[:, :], bog_d[0:1, :].partition_broadcast(128), [], [sc], sb=semS)
        pg.dma("sp", mnwS[:, :], mnw_d[0:1, :].partition_broadcast(128), [], [sc], sb=semS)
        for jj in range(4):
            pg.dma("sp", cwS[:, :, jj], cw_d[jj:jj + 1, :].rearrange("o (b p) -> p (o b)", p=128), [], [sc], sb=semS,
                   allow_slow_non_contiguous=True)
        pg.dma("sp", cbS[:, :], cb_d.rearrange("o (b p) -> p (o b)", p=128), [], [sc], sb=semS, allow_slow_non_contiguous=True)
        pg.dma("sp", bigS[:, :], big_d.rearrange("o h -> h o"), [], [sc], sb=semS, allow_slow_non_contiguous=True)
        pg.dma("sp", bfgS[:, :], bfg_d.rearrange("o h -> h o"), [], [sc], sb=semS, allow_slow_non_contiguous=True)
        pg.dma("sp", smS[:, :], sm_d.rearrange("b h -> h b"), [], [sc], sb=semS, allow_slow_non_contiguous=True)
        pg.group_end()
        pg.ts("pool", maskU[:, :], mask01[:, :], -1.0, 1.0, ALU.mult, ALU.add, [cB], [sc])
        pg.memset("pool", ones_b[:, :], 1.0, [sc])

        load_xT(xs_d, 0, NT, 0, xsS, psS_T, xsT, xsT_b)

        bdq = sS("bdq", [128, NSQ, 4, 16], BF16)
        bdq_b = pg.buf("bdq")
        pg.memset("pool", bdq[:, :, :, :], 0.0, [bdq_b])
        kTn = sS("kTn", [128, 4, NT], BF16)
        kTn_b = pg.buf("kTn")
        for pp in range(4):
            pt, pb = psS_mm.next()
            for c in range(KC):
                pg.mm(pt[:, 0:NT], winS[:, c, FQ + pp * 128:FQ + (pp + 1) * 128], xsT[:, c, :], c == 0, c == KC - 1,
                      [xsT_b, winS_b], [pb])
            pg.ts("dve", bdq[0:64, :, pp, 0:8], pt[0:64, 0:NT].rearrange("p (b t) -> p b t", b=NSQ), QSCALE, None, ALU.mult, None,
                  [pb], [bdq_b])
            pg.ts("dve", bdq[64:128, :, pp, 8:16], pt[64:128, 0:NT].rearrange("p (b t) -> p b t", b=NSQ), QSCALE, None, ALU.mult,
                  None, [pb], [bdq_b])
            pt, pb = psS_mm.next()
            for c in range(KC):
                pg.mm(pt[:, 0:NT], winS[:, c, FK + pp * 128:FK + (pp + 1) * 128], xsT[:, c, :], c == 0, c == KC - 1,
                      [xsT_b, winS_b], [pb])
            pg.cp("act", kTn[:, pp, :], pt[:, 0:NT], [pb], [kTn_b])
        gS = pg.buf("gS")
        q_ig = sS("q_ig", [4, NT]); q_lf = sS("q_lf", [4, NT]); q_B = sS("q_B", [4, NT]); q_u = sS("q_u", [4, NT])
        q_M = sS("q_M", [4, NT]); q_t1 = sS("q_t1", [4, NT]); q_t2 = sS("q_t2", [4, NT]); q_wk = sS("q_wk", [4, NT])
        q_fl = sS("q_fl", [4, NT]); q_R = sS("q_R", [4, NSQ]); q_g = sS("q_g", [4, NSQ]); q_gd = sS("q_gd", [4, 16])
        q_m = sS("q_m", [4, NSQ])
        pgt, pgb = psS_g.next()
        for c in range(KC):
            pg.mm(pgt[0:4, 0:NT], winS[:, c, MI:MI + 4], xsT[:, c, :], c == 0, c == KC - 1, [xsT_b, winS_b], [pgb])
        pg.ts("dve", q_ig[:, :], pgt[0:4, 0:NT], bigS[0:4, 0:1], None, ALU.add, None, [pgb, sc], [gS])
        pgt, pgb = psS_g.next()
        for c in range(KC):
            pg.mm(pgt[0:4, 0:NT], winS[:, c, MF:MF + 4], xsT[:, c, :], c == 0, c == KC - 1, [xsT_b, winS_b], [pgb])
        pg.ts("dve", q_t1[:, :], pgt[0:4, 0:NT], bfgS[0:4, 0:1], None, ALU.add, None, [pgb, sc], [gS])
        log_sigmoid(q_lf[:, :], q_t1[:, :], q_t2[:, :], q_M[:, :], gS, [gS], [gS])
        for b in range(NSQ):
            cs = slice(b * TD, (b + 1) * TD)
            pg.op("dve", lambda e, cs=cs: e.tensor_tensor_scan(out=q_B[:, cs], data0=one_c[0:4, 0:1].broadcast_to([4, TD]),
                                                                data1=q_lf[:, cs], initial=0.0, op0=ALU.mult, op1=ALU.add),
                  [gS, cB], [gS])
        pg.tt("dve", q_u[:, :], q_ig[:, :], q_B[:, :], ALU.subtract, [gS], [gS])
        for b in range(NSQ):
            cs = slice(b * TD, (b + 1) * TD)
            pg.op("dve", lambda e, cs=cs, b=b: e.tensor_tensor_scan(out=q_M[:, cs], data0=one_c[0:4, 0:1].broadcast_to([4, TD]),
                                                                     data1=q_u[:, cs], initial=smS[0:4, b:b + 1],
                                                                     op0=ALU.mult, op1=ALU.max), [gS, cB, sc], [gS])
        pg.cp("dve", q_R[:, :], q_M[:, :].rearrange("p (b t) -> p b t", b=NSQ)[:, :, TD - 1], [gS], [gS])
        RbS = q_R[:, :].unsqueeze(2).broadcast_to([4, NSQ, TD])
        pg.tt("dve", q_t1[:, :].rearrange("p (b t) -> p b t", b=NSQ), q_u[:, :].rearrange("p (b t) -> p b t", b=NSQ), RbS,
              ALU.subtract, [gS], [gS])
        pg.act(q_wk[:, :], q_t1[:, :], AF.Exp, [gS], [gS])
        pg.tt("dve", q_t2[:, :].rearrange("p (b t) -> p b t", b=NSQ), q_B[:, :].rearrange("p (b t) -> p b t", b=NSQ), RbS,
              ALU.add, [gS], [gS])
        pg.act(q_fl[:, :], q_t2[:, :], AF.Exp, [gS], [gS], scale=-1.0)
        pg.tt("dve", q_g[:, :], smS[:, :], q_R[:, :], ALU.subtract, [gS, sc], [gS])
        pg.act(q_g[:, :], q_g[:, :], AF.Exp, [gS], [gS])
        pg.tt("dve", q_m[:, :], q_R[:, :], q_B[:, :].rearrange("p (b t) -> p b t", b=NSQ)[:, :, TD - 1], ALU.add, [gS], [gS])
        pg.dma("sp", ms_o.rearrange("b h -> h b"), q_m[:, :], [gS], [], sb=semSo, allow_slow_non_contiguous=True)
        for h2 in range(4):
            pg.ts("dve", q_gd[:, h2 * 4:(h2 + 1) * 4], q_g[:, :], ident_f[0:4, h2:h2 + 1], None, ALU.mult, None, [gS, cB], [gS])
        tscS = sS("tscS", [8, NSQ, 8])
        gbS = sS("gbS", [128, 16])
        tscS_b = pg.buf("tscS")
        pgt, pgb = psS_g.next()
        for b in range(NSQ):
            pg.tr(pgt[0:TD, b * 8:b * 8 + 4], q_wk[0:4, b * TD:(b + 1) * TD], ident_f[0:4, 0:4], [gS, cB], [pgb])
            pg.tr(pgt[0:TD, b * 8 + 4:b * 8 + 8], q_fl[0:4, b * TD:(b + 1) * TD], ident_f[0:4, 0:4], [gS, cB], [pgb])
        pg.mm(pgt[:, 64:80], ones_f[0:4, 0:128], q_gd[0:4, 0:16], True, True, [gS, cB], [pgb])
        pg.cp("dve", tscS[:, :, :], pgt[0:TD, 0:32].rearrange("p (b k) -> p b k", b=NSQ), [pgb], [tscS_b])
        pg.cp("dve", gbS[:, :], pgt[:, 64:80], [pgb], [tscS_b])
        zcS = [sS(f"zcS{b}", [128, NSQ, 3 + TD]) for b in range(8)]
        zcS_b = [pg.buf("zcS") for b in range(8)]
        qkS = [sS(f"qkS{b}", [128, NT], BF16) for b in range(8)]
        qkS_b = [pg.buf("qkS") for b in range(8)]
        caccS = sS("caccS", [128, NSQ, TD]); csigS = sS("csigS", [128, NSQ, TD])
        caccS_b = pg.buf("caccS")
        semZ = pg.buf("semZ")
        pg.group_begin()
        for blk in range(8):
            for b in range(NSQ):
                pg.dma("sp", zcS[blk][:, b, 0:3], sconv_d[b, :, blk * 128:(blk + 1) * 128].rearrange("t f -> f t"), [], [zcS_b[blk]],
                       sb=semZ, allow_slow_non_contiguous=True)
        pg.group_end()
        for blk in range(8):
            pt, pb = psS_mm.next()
            for c in range(KC):
                pg.mm(pt[:, 0:NT], winS[:, c, MQK + blk * 128:MQK + (blk + 1) * 128], xsT[:, c, :], c == 0, c == KC - 1,
                      [xsT_b, winS_b], [pb])
            pg.cp("act", zcS[blk][:, :, 3:3 + TD], pt[:, 0:NT].rearrange("p (b t) -> p b t", b=NSQ), [pb], [zcS_b[blk]])
            pg.ts("dve", caccS[:, :, :], zcS[blk][:, :, 3:3 + TD], cwS[:, blk, 3:4], cbS[:, blk:blk + 1], ALU.mult, ALU.add,
                  [zcS_b[blk], sc], [caccS_b])
            for jj in (2, 1, 0):
                pg.stt("dve", caccS[:, :, :], zcS[blk][:, :, jj:jj + TD], cwS[:, blk, jj:jj + 1], caccS[:, :, :], ALU.mult, ALU.add,
                       [zcS_b[blk], sc, caccS_b], [caccS_b])
            pg.act(csigS[:, :, :], caccS[:, :, :], AF.Sigmoid, [caccS_b], [caccS_b])
            pg.stt("dve", qkS[blk][:, :].rearrange("p (b t) -> p b t", b=NSQ), caccS[:, :, :], (1.0 if blk < 4 else KSCALE),
                   csigS[:, :, :], ALU.mult, ALU.mult, [caccS_b], [qkS_b[blk]])
            for b in range(NSQ):
                pg.dma("sp", convs_o[b, :, blk * 128:(blk + 1) * 128].rearrange("t f -> f t"), zcS[blk][:, b, TD:TD + 3],
                       [zcS_b[blk]], [], sb=semSo, allow_slow_non_contiguous=True)

        knS = sS("knS", [8, NSQ, 512]); vnS = sS("vnS", [8, NSQ, 512]); vnB = sS("vnB", [8, NSQ, 512], BF16)
        lfnS = sS("lfnS", [8, NSQ, 8]); ltmp = sS("ltmp", [8, 32])
        vaS = sS("vaS", [8, NSQ, 4, 130], BF16); ogS = sS("ogS", [8, NSQ, 512]); ogtS = sS("ogtS", [8, 512])
        ktS = sS("ktS", [8, NSQ, 4, 128], BF16)
        tokS_b = [pg.buf("tokS") for b in range(NSQ)]
        ltmp_b = pg.buf("ltmp")
        pg.memset("pool", vaS[:, :, :, :], 0.0, tokS_b)
        lfn_bs = []
        for b in range(NSQ):
            cs = slice(b * TD, (b + 1) * TD)
            tb = tokS_b[b]
            for (c0, dst, od) in ((FK, knS, ksm_o), (FV, vnS, vsm_o)):
                pt, pb = psS_mm.next()
                for c in range(KC):
                    pg.mm(pt[0:TD, :], xsT[:, c, cs], winS[:, c, c0:c0 + 512], c == 0, c == KC - 1, [xsT_b, winS_b], [pb])
                ob_ = pg.buf("kvn")
                pg.cp("act", dst[:, b, :], pt[0:TD, :], [pb], [ob_])
                pg.dma("sp", od[b * TD:(b + 1) * TD, :], dst[:, b, :], [ob_], [], sb=semSo)
            pg.cp("dve", vnB[:, b, :], vnS[:, b, :], [ob_], [tb])
            pt, pb = psS_mm.next()
            for c in range(KC):
                pg.mm(pt[0:TD, 0:8], xsT[:, c, cs], winS[:, c, FF:FF + 8], c == 0, c == KC - 1, [xsT_b, winS_b], [pb])
            pg.tt("dve", ltmp[:, 0:8], pt[0:TD, 0:8], bffS[0:TD, :], ALU.add, [pb, sc], [ltmp_b])
            lfn_b = pg.buf("lfn")
            lfn_bs.append(lfn_b)
            log_sigmoid(lfnS[:, b, :], ltmp[:, 0:8], ltmp[:, 8:16], ltmp[:, 16:24], ltmp_b, [ltmp_b], [lfn_b])
            pg.dma("sp", lfs_o[b * TD:(b + 1) * TD, :], lfnS[:, b, :], [lfn_b], [], sb=semSo)
            pt, pb = psS_mm.next()
            for c in range(KC):
                pg.mm(pt[0:TD, :], xsT[:, c, cs], winS[:, c, MV:MV + 512], c == 0, c == KC - 1, [xsT_b, winS_b], [pb])
            for hm in range(4):
                pg.ts("dve", vaS[:, b, hm, 0:128], pt[0:TD, hm * 128:(hm + 1) * 128], tscS[:, b, hm:hm + 1], None, ALU.mult, None,
                      [pb, tscS_b], [tb])
            pg.cp("dve", vaS[:, b, :, 128:129], tscS[:, b, 0:4].unsqueeze(2), [tscS_b], [tb])
            pt, pb = psS_mm.next()
            for c in range(KC):
                pg.mm(pt[0:TD, :], xsT[:, c, cs], winS[:, c, MO:MO + 512], c == 0, c == KC - 1, [xsT_b, winS_b], [pb])
            pg.tt("dve", ogtS[:, :], pt[0:TD, :], bogS[0:TD, :], ALU.add, [pb, sc], [ltmp_b])
            pg.act(ogS[:, b, :], ogtS[:, :], AF.Sigmoid, [ltmp_b], [tb])
            pbt, pbb = psS_b.next()
            for hm in range(4):
                pg.tr(pbt[0:TD, hm * 128:(hm + 1) * 128], qkS[4 + hm][:, cs], ident_b[:, :], [qkS_b[4 + hm], cB], [pbb])
            pg.cp("act", ktS[:, b, :, :], pbt[0:TD, 0:512].rearrange("p (h d) -> p h d", h=4), [pbb], [tb])

        CfS = Rot(pg, [sS(f"CfS{i}", [128, 130]) for i in range(2)], "CfS")
        CbS = Rot(pg, [sS(f"CbS{i}", [128, 130], BF16) for i in range(2)], "CbS")
        CnS = Rot(pg, [sS(f"CnS{i}", [128, 130]) for i in range(2)], "CnS")
        ATS = Rot(pg, [sS(f"ATS{i}", [8, 8], BF16) for i in range(2)], "ATS")
        qgS = Rot(pg, [sS(f"qgS{i}", [128, 8], BF16) for i in range(2)], "qgS")
        nsS = Rot(pg, [sS(f"nsS{i}", [8, 130]) for i in range(2)], "nsS")
        hbS = Rot(pg, [sS(f"hbS{i}", [8, 512]) for i in range(2)], "hbS")
        hsmS = sS("hsmS", [8, 8]); statS = sS("statS", [8, 4, 6]); mvS = sS("mvS", [8, 4, 4]); mnfS = sS("mnfS", [8, 512])
        hsmS_b = pg.buf("hsmS"); mnfS_b = pg.buf("mnfS")
        mnbS = Rot(pg, [sS(f"mnbS{i}", [8, 512], BF16) for i in range(2)], "mnbS")
        mixS = sS("mixS", [128, 8, NT], BF16)
        mixS_b = pg.buf("mixS")
        for b in range(NSQ):
            cs = slice(b * TD, (b + 1) * TD)
            tb = tokS_b[b]
            hb, hbb = hbS.next()
            for hm in range(4):
                cf, cfb = CfS.next()
                pg.group_begin()
                pg.dma("sp", cf[:, 0:128], sC_d[b, hm, :, :], [], [cfb], sb=cfb)
                pg.dma("sp", cf[:, 128:129], sn_d[b, hm:hm + 1, :].rearrange("o d -> d o"), [], [cfb], sb=cfb,
                       allow_slow_non_contiguous=True)
                pg.group_end()
                cb_, cbb = CbS.next()
                pg.cp("act", cb_[:, 0:129], cf[:, 0:129], [cfb], [cbb])
                gcol = gbS[:, hm * 4 + b:hm * 4 + b + 1]
                pss, psb = psS_s.next()
                pg.mm(pss[0:TD, 0:TD], qkS[4 + hm][:, cs], qkS[hm][:, cs], True, True, [qkS_b[4 + hm], qkS_b[hm]], [psb])
                at, atb = ATS.next()
                pg.tt("dve", at[:, :], pss[0:TD, 0:TD], mask01[0:TD, 0:TD], ALU.mult, [psb, cB], [atb])
                qgt, qgb = qgS.next()
                pg.ts("dve", qgt[:, :], qkS[hm][:, cs], gcol, None, ALU.mult, None, [qkS_b[hm], tscS_b], [qgb])
                pg.mm(pss[0:TD, 256:385], at[:, :], vaS[:, b, hm, 0:129], True, False, [atb, tb], [psb])
                pg.mm(pss[0:TD, 256:385], qgt[:, :], cb_[:, 0:129], False, True, [qgb, cbb], [psb])
                ns_, nsb = nsS.next()
                pg.cp("act", ns_[:, 0:129], pss[0:TD, 256:385], [psb], [nsb])
                psu, pub = psS_s.next()
                pg.mm(psu[:, 0:129], ktS[:, b, hm, :], vaS[:, b, hm, 0:129], True, True, [tb], [pub])
                cn, cnb = CnS.next()
                pg.stt("dve", cn[:, 0:129], cf[:, 0:129], gcol, psu[:, 0:129], ALU.mult, ALU.add, [pub, tscS_b, cfb], [cnb])
                pg.group_begin()
                pg.dma("sp", Cs_o[b, hm, :, :], cn[:, 0:128], [cnb], [], sb=cnb)
                pg.dma("sp", ns_o[b, hm:hm + 1, :].rearrange("o d -> d o"), cn[:, 128:129], [cnb], [], sb=cnb,
                       allow_slow_non_contiguous=True)
                pg.group_end()
                pg.stt("dve", hsmS[:, 0:1], ns_[:, 128:129], -1.0, ns_[:, 128:129], ALU.mult, ALU.max, [nsb], [hsmS_b])
                pg.tt("dve", hsmS[:, 1:2], hsmS[:, 0:1], tscS[:, b, 4 + hm:5 + hm], ALU.max, [hsmS_b, tscS_b], [hsmS_b])
                pg.op("dve", lambda e: e.reciprocal(hsmS[:, 2:3], hsmS[:, 1:2]), [hsmS_b], [hsmS_b])
                pg.ts("dve", hb[:, hm * 128:(hm + 1) * 128], ns_[:, 0:128], hsmS[:, 2:3], None, ALU.mult, None, [nsb, hsmS_b], [hbb])
            for hm in range(4):
                pg.op("dve", lambda e, hm=hm, hb=hb: e.bn_stats(statS[:, hm, :], hb[:, hm * 128:(hm + 1) * 128]), [hbb], [hsmS_b])
                pg.op("dve", lambda e, hm=hm: e.bn_aggr(mvS[:, hm, 0:2], statS[:, hm, :]), [hsmS_b], [hsmS_b])
            pg.act(mvS[:, :, 2:3], mvS[:, :, 1:2], AF.Ln, [hsmS_b], [hsmS_b], bias=eps_c[0:TD, 0:1])
            pg.act(mvS[:, :, 2:3], mvS[:, :, 2:3], AF.Exp, [hsmS_b], [hsmS_b], scale=-0.5)
            for hm in range(4):
                pg.ts("dve", mnfS[:, hm * 128:(hm + 1) * 128], hb[:, hm * 128:(hm + 1) * 128], mvS[:, hm, 0:1], mvS[:, hm, 2:3],
                      ALU.subtract, ALU.mult, [hbb, hsmS_b], [mnfS_b])
            pg.tt("pool", mnfS[:, :], mnfS[:, :], mnwS[0:TD, :], ALU.mult, [mnfS_b, sc], [mnfS_b])
            mb, mbb = mnbS.next()
            pg.tt("pool", mb[:, :], mnfS[:, :], ogS[:, b, :], ALU.mult, [mnfS_b, tb], [mbb])
            pbt, pbb = psS_b.next()
            for hm in range(4):
                pg.tr(pbt[:, hm * 8:hm * 8 + TD], mb[:, hm * 128:(hm + 1) * 128], ident_b[0:TD, 0:TD], [mbb, cB], [pbb])
            pg.cp("act", mixS[:, 4:8, cs], pbt[:, 0:32].rearrange("p (h t) -> p h t", h=4), [pbb], [mixS_b])

        if with_cache:
            ptI = sS("ptI", [128, NPG], I32); ptF = sS("ptF", [128, NPG]); pcI = sS("pcI", [128, 1], I32); pcF = sS("pcF", [128, 1])
            idxF = sS("idxF", [128, NPG]); idxI = sS("idxI", [128, NPG], I32); pgI = sS("pgI", [128, 1], I32)
            idx_b = pg.buf("idx")
            lfp = sS("lfp", [128, 1024]); lfc = sS("lfc", [128, 8, 128]); ltot = sS("ltot", [128, 8]); llat = sS("llat", [128, 8])
            Rpg = sS("Rpg", [128, 8, 128]); RkT = sS("RkT", [128, 8, 128]); LnL = sS("LnL", [128, 8]); LnS = sS("LnS", [8, 8])
            bnew = sS("bnew", [8, 8])
            R_b = pg.buf("Rb")
            KVp = Rot(pg, [sS(f"KVp{i}", [128, 1024], BF16) for i in range(5)], "KVp")
            KTs = Rot(pg, [sS(f"KTs{i}", [128, 4, 128], BF16) for i in range(3)], "KTs")
            sTs = Rot(pg, [sS(f"sTs{i}", [128, 64]) for i in range(3)], "sTs")
            PTs = Rot(pg, [sS(f"PTs{i}", [128, 64], BF16) for i in range(3)], "PTs")
            oS = sS("oS", [8, 512]); lrow = sS("lrow", [1, 64]); lq = sS("lq", [8, 8]); rlq = sS("rlq", [8, 8])
            foS = sS("foS", [8, 512], BF16)
            fin_b = pg.buf("fin")
            pg.op("pool", lambda e: e.iota(pcI[:, :], pattern=[[0, 1]], base=0, channel_multiplier=1), [], [idx_b])
            pg.cp("dve", pcF[:, :], pcI[:, :], [idx_b], [idx_b])
            clfv = clf_d.rearrange("(pg t) h -> pg (t h)", t=128)
            for b in range(NSQ):
                cs = slice(b * TD, (b + 1) * TD)
                tb = tokS_b[b]
                pg.group_begin()
                pg.dma("sp", ptI[:, :], pt_d[b:b + 1, :].partition_broadcast(128), [], [idx_b], sb=idx_b)
                pg.dma("sp", pgI[:, :], pt_d[b:b + 1, :].rearrange("o p -> p o"), [], [idx_b], sb=idx_b, allow_slow_non_contiguous=True)
                pg.group_end()
                pg.cp("dve", ptF[:, :], ptI[:, :], [idx_b], [idx_b])
                pg.ts("dve", idxF[:, :], ptF[:, :], 128.0, pcF[:, 0:1], ALU.mult, ALU.add, [idx_b], [idx_b])
                pg.cp("dve", idxI[:, :], idxF[:, :], [idx_b], [idx_b])
                pg.dmaf("pool", lambda e: e.indirect_dma_start(out=lfp[:, :], out_offset=None, in_=clfv,
                                                              in_offset=bass.IndirectOffsetOnAxis(ap=pgI[:, 0:1], axis=0)),
                        [idx_b], [R_b], R_b)
                lf3 = lfp[:, :].rearrange("p (t h) -> p h t", h=8)
                for h in range(8):
                    pg.op("dve", lambda e, h=h: e.tensor_tensor_scan(out=lfc[:, h, :], data0=one_c[:, 0:1].broadcast_to([128, 128]),
                                                                      data1=lf3[:, h, :], initial=0.0, op0=ALU.mult, op1=ALU.add),
                          [R_b, cB], [R_b])
                pg.cp("dve", ltot[:, :], lfc[:, :, 127], [R_b], [R_b])
                pgt, pgb = psS_g.next()
                pg.mm(pgt[:, 0:8], maskU[:, :], ltot[:, :], True, True, [R_b, sc], [pgb])
                pg.mm(pgt[:, 8:16], ones_f[0:TD, 0:128], lfnS[:, b, :], True, True, [lfn_bs[b], cB], [pgb])
                pg.mm(pgt[0:TD, 16:24], mask01[0:TD, 0:TD], lfnS[:, b, :], True, True, [lfn_bs[b], cB], [pgb])
                pg.tt("dve", llat[:, :], pgt[:, 0:8], ltot[:, :], ALU.add, [pgb, R_b], [R_b])
                pg.cp("dve", LnL[:, :], pgt[:, 8:16], [pgb], [R_b])
                pg.cp("dve", LnS[:, :], pgt[0:TD, 16:24], [pgb], [R_b])
                pg.tt("dve", Rpg[:, :, :], llat[:, :].unsqueeze(2).broadcast_to([128, 8, 128]), lfc[:, :, :], ALU.subtract, [R_b], [R_b])
                for h in range(8):
                    ptt, ptb = psS_T.next()
                    pg.tr(ptt[:, 0:128], Rpg[:, h, :], ident_f[:, :], [R_b, cB], [ptb])
                    pg.ts("dve", RkT[:, h, :], ptt[:, 0:128], LnL[:, h:h + 1], None, ALU.add, None, [ptb, R_b], [R_b])
                pg.tt("dve", bnew[:, :], LnL[0:TD, :], LnS[:, :], ALU.subtract, [R_b], [R_b])
                po, pob = psS_o.next()
                pl, plb = psS_l.next()
                npg_ = NPG if 'p4' not in DBG else 4
                st1 = {}
                st2 = {}

                def stage1(p_, b=b):
                    kvp, kpb = KVp.next()
                    pg.dmaf("pool", lambda e, kvp=kvp, p_=p_: e.indirect_dma_start(
                        out=kvp[:, :], out_offset=None, in_=ckv_d, in_offset=bass.IndirectOffsetOnAxis(ap=idxI[:, p_:p_ + 1], axis=0)),
                        [idx_b], [kpb], kpb)
                    pbt, pbb = psS_b.next()
                    for pp in range(4):
                        pg.tr(pbt[:, pp * 128:(pp + 1) * 128], kvp[:, pp * 128:(pp + 1) * 128], ident_b[:, :], [kpb, cB], [pbb])
                    kt_, ktb_ = KTs.next()
                    pg.cp("dve", kt_[:, :, :], pbt[:, 0:512].rearrange("p (a t) -> p a t", a=4), [pbb], [ktb_])
                    st1[p_] = (kvp, kpb, kt_, ktb_)

                def stage2(p_, b=b):
                    kvp, kpb, kt_, ktb_ = st1.pop(p_)
                    pss, psb = psS_s.next()
                    for pp in range(4):
                        pg.mm(pss[:, pp * 16:(pp + 1) * 16], kt_[:, pp, :], bdq[:, b, pp, :], True, True, [ktb_, bdq_b], [psb])
                    st_, stb_ = sTs.next()
                    pg.tt("dve", st_[:, :].rearrange("p (h q) -> p h q", h=8), pss[:, 0:64].rearrange("p (h q) -> p h q", h=8),
                          RkT[:, :, p_].unsqueeze(2).broadcast_to([128, 8, 8]), ALU.add, [psb, R_b], [stb_])
                    pt_, ptb_ = PTs.next()
                    pg.act(pt_[:, :], st_[:, :], AF.Exp, [stb_], [ptb_])
                    st2[p_] = (kvp, kpb, pt_, ptb_)

                def stage3(p_, po=po, pob=pob, pl=pl, plb=plb):
                    kvp, kpb, pt_, ptb_ = st2.pop(p_)
                    for h in range(8):
                        pg.mm(po[0:TD, h * 64:(h + 1) * 64], pt_[:, h * 8:(h + 1) * 8], kvp[:, 512 + h * 64:512 + (h + 1) * 64],
                              p_ == 0, False, [ptb_, kpb], [pob])
                    pg.mm(pl[0:1, 0:64], ones_b[:, 0:1], pt_[:, :], p_ == 0, False, [ptb_, sc], [plb])

                for t_ in range(npg_ + 2):
                    if t_ < npg_:
                        stage1(t_)
                    if 0 <= t_ - 1 < npg_:
                        stage2(t_ - 1)
                    if 0 <= t_ - 2 < npg_:
                        stage3(t_ - 2)
                pss, psb = psS_s.next()
                for pp in range(4):
                    pg.mm(pss[0:TD, pp * 16:(pp + 1) * 16], kTn[:, pp, cs], bdq[:, b, pp, :], True, True, [kTn_b, bdq_b], [psb])
                st_, stb_ = sTs.next()
                pg.tt("dve", st_[0:TD, :].rearrange("p (h q) -> p h q", h=8), pss[0:TD, 0:64].rearrange("p (h q) -> p h q", h=8),
                      bnew[:, :].unsqueeze(2).broadcast_to([TD, 8, 8]), ALU.add, [psb, R_b], [stb_])
                pg.act(st_[0:TD, :], st_[0:TD, :], AF.Exp, [stb_], [stb_])
                pt_, ptb_ = PTs.next()
                pg.tt("dve", pt_[0:TD, :].rearrange("p (h q) -> p h q", h=8), st_[0:TD, :].rearrange("p (h q) -> p h q", h=8),
                      mask01[0:TD, 0:TD].unsqueeze(1).broadcast_to([TD, 8, TD]), ALU.mult, [stb_, cB], [ptb_])
                for h in range(8):
                    pg.mm(po[0:TD, h * 64:(h + 1) * 64], pt_[0:TD, h * 8:(h + 1) * 8], vnB[:, b, h * 64:(h + 1) * 64], False, True,
                          [ptb_, tb], [pob])
                pg.mm(pl[0:1, 0:64], ones_b[0:TD, 0:1], pt_[0:TD, :], False, True, [ptb_, sc], [plb])
                pg.cp("act", oS[:, :], po[0:TD, :], [pob], [fin_b])
                pg.cp("dve", lrow[:, :], pl[0:1, 0:64], [plb], [fin_b])
                lr_h = lrow.tensor if hasattr(lrow, "tensor") else lrow
                pg.group_begin()
                for q_ in range(TD):
                    pg.dma("sp", lq[q_:q_ + 1, :], bass.AP(lr_h, lrow[:, :].offset + q_, [[lrow[:, :].ap[0][0], 1], [8, 8]]),
                           [fin_b], [fin_b], sb=fin_b, allow_slow_non_contiguous=True)
                pg.group_end()
                pg.op("dve", lambda e: e.reciprocal(rlq[:, :], lq[:, :]), [fin_b], [fin_b])
                pg.tt("dve", foS[:, :].rearrange("p (h d) -> p h d", h=8), oS[:, :].rearrange("p (h d) -> p h d", h=8),
                      rlq[:, :].unsqueeze(2).broadcast_to([TD, 8, 64]), ALU.mult, [fin_b], [fin_b])
                pbt, pbb = psS_b.next()
                for c4 in range(4):
                    pg.tr(pbt[:, c4 * 8:c4 * 8 + TD], foS[:, c4 * 128:(c4 + 1) * 128], ident_b[0:TD, 0:TD], [fin_b, cB], [pbb])
                pg.cp("act", mixS[:, 0:4, cs], pbt[:, 0:32].rearrange("p (h t) -> p h t", h=4), [pbb], [mixS_b])
        pg.dma("sp", mix_d[:, :, S:S + NT], mixS[:, :, :], [mixS_b], [mix_db], sb=mixS_b)
        nS = pg.emit()
        esS.close()

    esC = ExitStack()

    def sC(name, shape, dt=F32):
        return esC.enter_context(nc.sbuf_tensor(name, list(shape), dt))

    def pC(name, dt=F32, n=512):
        return esC.enter_context(nc.psum_tensor(name, [128, n], dt))

    TB = 256
    w1b = sC("w1b", [128, KC, DFF // 2], BF16)
    w1b_b = pg.buf("w1b")
    for c in range(KC):
        pg.dma("pool", w1b[:, c, :], w1v[:, c, 2048:4096], [], [w1b_b], sb=w1b_b)
    w2S = sC("w2S", [128, 32, D], BF16)
    for c in range(32):
        pg.dma("pool", w2S[:, c, :], w2v[:, c, :], [], [w2_b], sb=w2_b)
    lnp = sC("lnp", [128, 4, D])
    for ii, dd in enumerate((l1g_d, l1b_d, l2g_d, l2b_d)):
        pg.dma("sp", lnp[:, ii, :], dd[0:1, :].partition_broadcast(128), [], [cB], sb=cB)
    mixC = sC("mixC", [128, KC, TB], BF16)
    mixC_b = pg.buf("mixC")
    xr = Rot(pg, [sC("xr0", [128, D])], "xr")
    x1s = [sC(f"x1s{i}", [128, D]) for i in range(4)]
    x1s_b = [pg.buf("x1s") for i in range(4)]
    x1T = sC("x1T", [128, KC, TB], BF16)
    x1T_b = pg.buf("x1T")
    hidT = sC("hidT", [128, 32, TB], BF16)
    hidT_b = [pg.buf("hidT") for f in range(32)]
    rtmp = Rot(pg, [sC(f"rtmp{i}", [128, TB]) for i in range(1)], "rtmp")
    lstat = sC("lstat", [128, 2, 6])
    lmv = sC("lmv", [128, 4])
    lst_b = pg.buf("lstat")
    psC_mm = PsumPool(pg, [pC(f"psC_mm{i}") for i in range(4)])
    psC_T = PsumPool(pg, [pC(f"psC_T{i}") for i in range(2)])

    def ln_inplace(xa, T, gi, xb):
        for h2 in range(2):
            pg.op("dve", lambda e, h2=h2: e.bn_stats(lstat[0:T, h2, :], xa[:, h2 * 512:(h2 + 1) * 512]), [xb], [lst_b])
        pg.op("dve", lambda e: e.bn_aggr(lmv[0:T, 0:2], lstat[0:T, :, :]), [lst_b], [lst_b])
        pg.act(lmv[0:T, 2:3], lmv[0:T, 1:2], AF.Ln, [lst_b], [lst_b], bias=eps_c[0:T, 0:1])
        pg.act(lmv[0:T, 2:3], lmv[0:T, 2:3], AF.Exp, [lst_b], [lst_b], scale=-0.5)
        pg.ts("dve", xa, xa, lmv[0:T, 0:1], lmv[0:T, 2:3], ALU.subtract, ALU.mult, [xb, lst_b], [xb])
        pg.tt("pool", xa, xa, lnp[0:T, gi, :], ALU.mult, [xb, cB], [xb])
        pg.tt("pool", xa, xa, lnp[0:T, gi + 1, :], ALU.add, [xb, cB], [xb])

    blocks = []
    nblk = (S // TB) if 'j1' not in DBG else 2
    if 'C' in SKIP:
        nblk = 0
    for bi in range(nblk):
        blocks.append((x_d, y_o, bi * TB, bi * TB, 2, 128))
    if 'nosample' not in DBG:
        blocks.append((xs_d, ys_o, 0, S, 1, NSTOK))
    def phase1(bi, blk):
        (src_d, out_d, row0, col0, ntl, T) = blk
        W = ntl * T if T == 128 else T
        pg.dma("sp", mixC[:, :, 0:W], mix_d[:, :, col0:col0 + W], [mix_db], [mixC_b], sb=mixC_b)
        for tl in range(ntl):
            sl = slice(tl * T, (tl + 1) * T)
            xi = (bi % 2) * 2 + tl
            xt, xb = xr.next()
            pg.dma("sp", xt[0:T, :], src_d[row0 + tl * T:row0 + (tl + 1) * T, :], [], [xb], sb=xb)
            xa = x1s[xi][0:T, :]
            for n2 in range(2):
                pt, pb = psC_mm.next()
                for c in range(KC):
                    pg.mm(pt[0:T, :], mixC[:, c, sl], woS[:, c, n2 * 512:(n2 + 1) * 512], c == 0, c == KC - 1,
                          [mixC_b, wo_b], [pb])
                pg.stt("dve", xa[:, n2 * 512:(n2 + 1) * 512], xt[0:T, n2 * 512:(n2 + 1) * 512], ALPHA, pt[0:T, :],
                       ALU.mult, ALU.add, [xb, pb], [x1s_b[xi]])
            ln_inplace(xa, T, 0, x1s_b[xi])

    def phase2(bi, blk):
        (src_d, out_d, row0, col0, ntl, T) = blk
        W = ntl * T if T == 128 else T
        for tl in range(ntl):
            xi = (bi % 2) * 2 + tl
            xa = x1s[xi][0:T, :]
            for g in range(2):
                ptt, ptb = psC_T.next()
                for cc in range(4):
                    c = g * 4 + cc
                    pg.tr(ptt[:, cc * 128:cc * 128 + T], xa[:, c * 128:(c + 1) * 128], ident_f[0:T, 0:T], [x1s_b[xi], cB], [ptb])
                pg.cp(pg.ev(), x1T[:, g * 4:g * 4 + 4, tl * T:(tl + 1) * T],
                      ptt[:, :].rearrange("p (c t) -> p c t", c=4)[:, :, 0:T], [ptb], [x1T_b])
        for f in range(32):
            pt, pb = psC_mm.next()
            for c in range(KC):
                w1t = w1a if f < 16 else w1b
                fo = (f % 16) * 128
                pg.mm(pt[:, 0:W], w1t[:, c, fo:fo + 128], x1T[:, c, 0:W], c == 0, c == KC - 1,
                      [x1T_b, w1_b if f < 16 else w1b_b], [pb])
            rt, rtb = rtmp.next()
            pg.act(rt[:, 0:W], pt[:, 0:W], AF.Relu, [pb], [rtb])
            pg.tt("dve" if f % 2 == 0 else "pool", hidT[:, f, 0:W], rt[:, 0:W], rt[:, 0:W], ALU.mult, [rtb], [hidT_b[f]])

    def phase3(bi, blk):
        (src_d, out_d, row0, col0, ntl, T) = blk
        for tl in range(ntl):
            sl = slice(tl * T, (tl + 1) * T)
            xi = (bi % 2) * 2 + tl
            xa = x1s[xi][0:T, :]
            for n2 in range(2):
                pt, pb = psC_mm.next()
                for f in range(32):
                    pg.mm(pt[0:T, :], hidT[:, f, sl], w2S[:, f, n2 * 512:(n2 + 1) * 512], f == 0, f == 31,
                          [hidT_b[f], w2_b], [pb])
                pg.stt("dve", xa[:, n2 * 512:(n2 + 1) * 512], xa[:, n2 * 512:(n2 + 1) * 512], ALPHA, pt[0:T, :],
                       ALU.mult, ALU.add, [x1s_b[xi], pb], [x1s_b[xi]])
            ln_inplace(xa, T, 2, x1s_b[xi])
            pg.dma("sp", out_d[row0 + tl * T:row0 + (tl + 1) * T, :], xa, [x1s_b[xi]], [], sb=x1s_b[xi])

    for bi, blk in enumerate(blocks):
        phase1(bi, blk)
        if bi > 0:
            phase3(bi - 1, blocks[bi - 1])
        phase2(bi, blk)
    if blocks:
        phase3(len(blocks) - 1, blocks[-1])
    nC = pg.emit()
    esC.close()
    return nc, es, pg, dict(nA=nA, nB=nB, nC=nC)


_CACHE = {}


def kernel(**inputs):
    n = 8
    nc, es, pg, info = build_program(debug=False, with_cache=True)
    I = {k: np.asarray(v) for k, v in inputs.items()}
    in_maps = []
    ckv = np.concatenate([I["cache_k"].reshape(5120 * 128, 512), I["cache_v"].reshape(5120 * 128, 512)], axis=1)
    clf = np.ascontiguousarray(I["cache_logf"]).reshape(5120 * 128, 8)
    for c in range(n):
        sl = slice(c * NSQ, (c + 1) * NSQ)
        in_maps.append({
            "x": np.ascontiguousarray(I["x_prompt"][c]),
            "xs": np.ascontiguousarray(I["x_sample"][sl].reshape(NSTOK, D)),
            "state_C": np.ascontiguousarray(I["state_C"][0, sl]),
            "state_n": np.ascontiguousarray(I["state_n"][0, sl]),
            "state_m": np.ascontiguousarray(I["state_m"][0, sl]),
            "state_conv": np.ascontiguousarray(I["state_conv"][0, sl]),
            "page_table": np.ascontiguousarray(I["page_table"][sl]),
            "cache_kv": ckv, "cache_logf": clf,
            "w_in": I["w_in"][0], "b_fox_f": I["b_fox_f"], "b_ig": I["b_ig"], "b_fg": I["b_fg"],
            "b_og": I["b_og"], "conv_w": I["conv_w"][0], "conv_b": I["conv_b"],
            "mlstm_norm_w": I["mlstm_norm_w"], "w_o": I["w_o"][0], "ln1_g": I["ln1_g"], "ln1_b": I["ln1_b"],
            "w1": I["w1"][0], "w2": I["w2"][0], "ln2_g": I["ln2_g"], "ln2_b": I["ln2_b"],
        })
    res = run_bass_kernel_spmd(nc, in_maps, core_ids=list(range(n)))
    R = res.results

    def cat(name, shape):
        return np.stack([R[c][name] for c in range(n)], 0).reshape(shape).astype(np.float32)

    y = cat("o_y", (8, S, D))
    ys = cat("o_ys", (32, TD, D))
    k = cat("o_k", (1, 8, S, 8, 64))
    v = cat("o_v", (1, 8, S, 8, 64))
    lf = cat("o_logf", (1, 8, S, 8))
    C = cat("o_C", (1, 8, 4, 128, 128))
    nn = cat("o_n", (1, 8, 4, 128))
    m = cat("o_m", (1, 8, 4))
    conv = cat("o_conv", (1, 8, 3, 1024))
    ks = cat("o_ks", (1, 32, TD, 8, 64))
    vs = cat("o_vs", (1, 32, TD, 8, 64))
    lfs = cat("o_logfs", (1, 32, TD, 8))
    Cs = cat("o_Cs", (1, 32, 4, 128, 128))
    ns = cat("o_ns", (1, 32, 4, 128))
    ms = cat("o_ms", (1, 32, 4))
    convs = cat("o_convs", (1, 32, 3, 1024))
    return (y, ys, k, v, lf, C, nn, m, conv, ks, vs, lfs, Cs, ns, ms, convs)
```
